# Optimizing a Trainium2 kernel written in Bass

```python
import jax, jax.numpy as jnp
from jax import lax
import numpy as np

D_MODEL = 2048
BATCH = 4
SEQ = 2048
DEPTH = 2
DEC_BATCH = 128
DEC_SEQ = 4
PAST_LEN = 16384
PAGE_SIZE = 128

N_MIXERS = 2
N_POOL_LAYERS = (DEPTH + 1) // 2
N_RET_LAYERS = DEPTH // 2
POOL_WINDOWS = (2, 4, 8, 16)
POOL_GROUPS = len(POOL_WINDOWS)
POOL_GW = D_MODEL // POOL_GROUPS
POOL_BUF = max(POOL_WINDOWS) - 1
RET_HEADS = 8
RET_DK = D_MODEL // RET_HEADS
RET_DV = 2 * RET_DK
RET_QK = RET_HEADS * RET_DK
RET_V = RET_HEADS * RET_DV
RET_CHUNK = 128
ROPE_BASE = 10000.0
FFN_DIM = 5632
CONV_W = 3
PLE_DIM = 256
EPS = 1e-6

kernel_name = "hybrid_pool_retention_decoder_step"

F32 = jnp.float32


def _rmsnorm(x, g):
    xf = x.astype(F32)
    y = xf * lax.rsqrt(jnp.mean(xf * xf, axis=-1, keepdims=True) + EPS)
    return (y * g.astype(F32)).astype(x.dtype)


def _pool_mixer(h, buf, pos0, w, scale):
    B, T, D = h.shape
    h_ext = jnp.concatenate([buf.astype(h.dtype), h], axis=1)
    cs = jnp.cumsum(h_ext.astype(F32), axis=1)
    cs = jnp.concatenate([jnp.zeros((B, 1, D), F32), cs], axis=1)
    pos = pos0 + jnp.arange(T)
    P = POOL_BUF
    outs = []
    for gi, win in enumerate(POOL_WINDOWS):
        sl = slice(gi * POOL_GW, (gi + 1) * POOL_GW)
        hi = cs[:, P + 1:P + 1 + T, sl]
        lo = cs[:, P + 1 - win:P + 1 - win + T, sl]
        cnt = jnp.minimum(pos + 1, win).astype(F32)[None, :, None]
        d = (hi - lo) / cnt - h[..., sl].astype(F32)
        outs.append(jnp.einsum('btc,cd->btd', d.astype(h.dtype), w[gi]))
    y = jnp.concatenate(outs, axis=-1) * scale
    return y, h_ext[:, -P:]


def _rotary(x, pos):
    half = RET_DK // 2
    inv = ROPE_BASE ** (-jnp.arange(half, dtype=F32) / half)
    ang = pos.astype(F32)[:, None] * inv[None, :]
    cos, sin = jnp.cos(ang), jnp.sin(ang)
    x1, x2 = x[..., :half], x[..., half:]
    return jnp.concatenate([x1 * cos - x2 * sin, x1 * sin + x2 * cos], axis=-1)


def _retention_decays(C):
    log_g = jnp.log1p(-(2.0 ** (-5.0 - jnp.arange(RET_HEADS, dtype=F32))))
    n = jnp.arange(C, dtype=F32)
    diff = n[:, None] - n[None, :]
    intra = jnp.where(diff >= 0, jnp.exp(log_g[:, None, None] * jnp.maximum(diff, 0.0)), 0.0)
    cross = jnp.exp(log_g[:, None] * (n + 1.0))
    kdec = jnp.exp(log_g[:, None] * (C - 1.0 - n))
    sdec = jnp.exp(log_g * C)
    return intra, cross, kdec, sdec


def _retention(h, s0, pos0, w_in, w_out):
    B, T, _ = h.shape
    proj = jnp.einsum('btd,de->bte', h, w_in)
    q = proj[..., :RET_QK]
    k = proj[..., RET_QK:2 * RET_QK]
    v = proj[..., 2 * RET_QK:2 * RET_QK + RET_V]
    g = proj[..., 2 * RET_QK + RET_V:]

    def heads(t, dh):
        return t.reshape(B, T, RET_HEADS, dh).transpose(0, 2, 1, 3).astype(F32)

    pos = pos0 + jnp.arange(T)
    q = _rotary(heads(q, RET_DK), pos)
    k = _rotary(heads(k, RET_DK), pos) * (RET_DK ** -0.5)
    v = heads(v, RET_DV)
    C = RET_CHUNK if T % RET_CHUNK == 0 else T
    nc = T // C
    intra, cross, kdec, sdec = _retention_decays(C)

    def to_chunks(t):
        return jnp.moveaxis(t.reshape(B, RET_HEADS, nc, C, t.shape[-1]), 2, 0)

    def step(S, qkv):
        qc, kc, vc = qkv
        sc = jnp.einsum('bhnk,bhmk->bhnm', qc, kc) * intra
        o = (jnp.einsum('bhnm,bhmv->bhnv', sc, vc)
             + jnp.einsum('bhnk,bhkv->bhnv', qc, S) * cross[None, :, :, None])
        S = sdec[None, :, None, None] * S + jnp.einsum('bhmk,bhmv->bhkv', kc * kdec[None, :, :, None], vc)
        return S, o

    s_new, o = lax.scan(step, s0.astype(F32), (to_chunks(q), to_chunks(k), to_chunks(v)))
    o = jnp.moveaxis(o, 0, 2).reshape(B, RET_HEADS, T, RET_DV)
    mu = jnp.mean(o, axis=-1, keepdims=True)
    var = jnp.mean(jnp.square(o - mu), axis=-1, keepdims=True)
    o = (o - mu) * lax.rsqrt(var + EPS)
    o = o.transpose(0, 2, 1, 3).reshape(B, T, RET_V)
    y = jnp.einsum('btv,vd->btd', (jax.nn.silu(g.astype(F32)) * o).astype(h.dtype), w_out)
    return y, s_new


def _conv_ffn(h, buf, w_up, cw, cb, w_down):
    T = h.shape[1]
    u = jnp.einsum('btd,df->btf', h, w_up)
    u_ext = jnp.concatenate([buf.astype(u.dtype), u], axis=1)
    c = cw[0] * u_ext[:, :T] + cw[1] * u_ext[:, 1:T + 1] + cw[2] * u_ext[:, 2:T + 2] + cb
    gate, up = c[..., :FFN_DIM], c[..., FFN_DIM:]
    y = jnp.einsum('btf,fd->btd', jax.nn.silu(gate) * up, w_down)
    return y, u_ext[:, -(CONV_W - 1):]


def _trunk(x, p, pool_bufs, ret_states, conv_bufs, pos0, norm_mix, norm_ffn, norm_ple, norm_final,
           pool_w, pool_scale, ret_w_in, ret_w_out, ffn_w_up, ffn_conv_w, ffn_conv_b, ffn_w_down,
           ple_w_proj, ple_w_gate):
    new_pool, new_ret, new_conv = [], [], []
    for i in range(DEPTH):
        h = _rmsnorm(x, norm_mix[i])
        j = i // N_MIXERS
        if i % N_MIXERS == 0:
            y, nb = _pool_mixer(h, pool_bufs[j], pos0, pool_w[j], pool_scale[j])
            new_pool.append(nb)
        else:
            y, ns = _retention(h, ret_states[j], pos0, ret_w_in[j], ret_w_out[j])
            new_ret.append(ns)
        x = x + y
        h = _rmsnorm(x, norm_ffn[i])
        y, nc = _conv_ffn(h, conv_bufs[i], ffn_w_up[i], ffn_conv_w[i], ffn_conv_b[i], ffn_w_down[i])
        new_conv.append(nc)
        x = x + y
        gate = jax.nn.sigmoid(jnp.einsum('btd,de->bte', _rmsnorm(x, norm_ple[i]), ple_w_gate[i]).astype(F32))
        emb = jnp.einsum('btp,pd->btd', p[i].astype(x.dtype), ple_w_proj[i])
        x = x + (gate * emb.astype(F32)).astype(x.dtype)
    return _rmsnorm(x, norm_final), jnp.stack(new_pool), jnp.stack(new_ret), jnp.stack(new_conv)


def setup_inputs(seed: int = 0) -> dict:
    key = jax.random.key(seed)
    ks = jax.random.split(key, 24)
    nrm = jax.random.normal
    F2 = 2 * FFN_DIM
    return {
        "x_prompt": nrm(ks[0], (BATCH, SEQ, D_MODEL), F32),
        "x_sample": nrm(ks[1], (DEC_BATCH, DEC_SEQ, D_MODEL), F32),
        "p_prompt": nrm(ks[2], (DEPTH, BATCH, SEQ, PLE_DIM), F32),
        "p_sample": nrm(ks[3], (DEPTH, DEC_BATCH, DEC_SEQ, PLE_DIM), F32),
        "state_pool": nrm(ks[4], (N_POOL_LAYERS, DEC_BATCH, POOL_BUF, D_MODEL), F32),
        "state_ret": 0.1 * nrm(ks[5], (N_RET_LAYERS, DEC_BATCH, RET_HEADS, RET_DK, RET_DV), F32),
        "state_conv": nrm(ks[6], (DEPTH, DEC_BATCH, CONV_W - 1, F2), F32),
        "norm_mix": 1.0 + 0.02 * nrm(ks[7], (DEPTH, D_MODEL), F32),
        "norm_ffn": 1.0 + 0.02 * nrm(ks[8], (DEPTH, D_MODEL), F32),
        "norm_ple": 1.0 + 0.02 * nrm(ks[9], (DEPTH, D_MODEL), F32),
        "norm_final": 1.0 + 0.02 * nrm(ks[10], (D_MODEL,), F32),
        "pool_w": nrm(ks[11], (N_POOL_LAYERS, POOL_GROUPS, POOL_GW, POOL_GW), F32) * POOL_GW ** -0.5,
        "pool_scale": 0.5 + 0.05 * nrm(ks[12], (N_POOL_LAYERS, D_MODEL), F32),
        "ret_w_in": nrm(ks[13], (N_RET_LAYERS, D_MODEL, 2 * RET_QK + 2 * RET_V), F32) * D_MODEL ** -0.5,
        "ret_w_out": nrm(ks[14], (N_RET_LAYERS, RET_V, D_MODEL), F32) * RET_V ** -0.5,
        "ffn_w_up": nrm(ks[15], (DEPTH, D_MODEL, F2), F32) * D_MODEL ** -0.5,
        "ffn_conv_w": 0.5 * nrm(ks[16], (DEPTH, CONV_W, F2), F32),
        "ffn_conv_b": 0.01 * nrm(ks[17], (DEPTH, F2), F32),
        "ffn_w_down": nrm(ks[18], (DEPTH, FFN_DIM, D_MODEL), F32) * FFN_DIM ** -0.5,
        "ple_w_proj": nrm(ks[19], (DEPTH, PLE_DIM, D_MODEL), F32) * PLE_DIM ** -0.5,
        "ple_w_gate": nrm(ks[20], (DEPTH, D_MODEL, D_MODEL), F32) * D_MODEL ** -0.5,
    }


def reference(x_prompt, x_sample, p_prompt, p_sample, state_pool, state_ret, state_conv,
              norm_mix, norm_ffn, norm_ple, norm_final, pool_w, pool_scale, ret_w_in, ret_w_out,
              ffn_w_up, ffn_conv_w, ffn_conv_b, ffn_w_down, ple_w_proj, ple_w_gate):
    dt = x_prompt.dtype
    zero_pool = jnp.zeros((N_POOL_LAYERS, BATCH, POOL_BUF, D_MODEL), dt)
    zero_ret = jnp.zeros((N_RET_LAYERS, BATCH, RET_HEADS, RET_DK, RET_DV), F32)
    zero_conv = jnp.zeros((DEPTH, BATCH, CONV_W - 1, 2 * FFN_DIM), dt)
    y_prompt, pool_p, ret_p, conv_p = _trunk(
        x_prompt, p_prompt, zero_pool, zero_ret, zero_conv, 0,
        norm_mix, norm_ffn, norm_ple, norm_final, pool_w, pool_scale, ret_w_in, ret_w_out,
        ffn_w_up, ffn_conv_w, ffn_conv_b, ffn_w_down, ple_w_proj, ple_w_gate)
    y_sample, pool_s, ret_s, conv_s = _trunk(
        x_sample, p_sample, state_pool, state_ret, state_conv, PAST_LEN,
        norm_mix, norm_ffn, norm_ple, norm_final, pool_w, pool_scale, ret_w_in, ret_w_out,
        ffn_w_up, ffn_conv_w, ffn_conv_b, ffn_w_down, ple_w_proj, ple_w_gate)
    return (y_prompt, y_sample, pool_p, pool_s, ret_p, ret_s, conv_p, conv_s)
```

```python
import contextlib
import numpy as np
import concourse.bass as bass
import concourse.mybir as mybir
from concourse.bass_utils import run_bass_kernel_spmd

F32 = mybir.dt.float32
BF16 = mybir.dt.bfloat16
AF = mybir.ActivationFunctionType
ALU = mybir.AluOpType

D = 2048
DC = 16
FF = 5632
F2 = 11264
NH = 8
DK = 256
DV = 512
NPR = 1024
NSQ = 16
NS = 64
HALO = 17
P0 = HALO
S0 = P0 + NPR
NX = S0 + NS
NTOK = NPR + NS
EPS = 1e-6
DEBUG = False
ENGS = ("pe", "act", "dve", "pool", "sp")
PAIRS = [[0, 1], [2, 3], [4, 5], [6, 7]]


class Res:
    __slots__ = ("name", "w", "r")

    def __init__(self, name):
        self.name = name
        self.w = None
        self.r = []


class KB:
    def __init__(self, nc, stack, n_sp=24, n_pool=8):
        self.nc = nc
        self.ops = {e: [] for e in ENGS}
        self.sem = {}
        self.cnt = {}
        for e in ("pe", "act", "dve", "pool"):
            self.sem[e] = stack.enter_context(nc.semaphore("prog_" + e))
            self.cnt[e] = 0
        self.dsems = {}
        for q, n in (("sp", n_sp), ("pool", n_pool)):
            self.dsems[q] = [[stack.enter_context(nc.semaphore(f"d_{q}_{i}")), 0] for i in range(n)]
        self.dnext = {"sp": 0, "pool": 0}
        self.waited = {e: {} for e in ENGS}
        self.semobj = {}
        self.csems = []

    def _collect(self, eng, reads, writes):
        need = {}

        def add(tok, same_ok):
            if tok is None:
                return
            key, val = tok
            if same_ok and key == eng:
                return
            if need.get(key, 0) < val:
                need[key] = val
        for r in reads:
            add(r.w, False)
        for w in writes:
            add(w.w, True)
            for t in w.r:
                add(t, True)
        out = []
        for key, val in need.items():
            if self.waited[eng].get(key, 0) >= val:
                continue
            self.waited[eng][key] = val
            out.append((key, val))
        return out

    def _emit_waits(self, eng, waits):
        for key, val in waits:
            semh = self.sem[key] if isinstance(key, str) else self.semobj[key]
            self.ops[eng].append(("wait", semh, val))

    def _commit(self, tok, reads, writes):
        for r in reads:
            r.r.append(tok)
        for w in writes:
            w.w = tok
            w.r = []

    def op(self, eng, fn, reads=(), writes=()):
        self._emit_waits(eng, self._collect(eng, reads, writes))
        self.cnt[eng] += 1
        tok = (eng, self.cnt[eng])
        self.ops[eng].append(("ins", fn, self.sem[eng], 1))
        self._commit(tok, reads, writes)
        return tok

    def group(self, eng, fns, reads=(), writes=()):
        self._emit_waits(eng, self._collect(eng, reads, writes))
        for fn in fns[:-1]:
            self.ops[eng].append(("ins", fn, None, 0))
        self.cnt[eng] += 1
        tok = (eng, self.cnt[eng])
        self.ops[eng].append(("ins", fns[-1], self.sem[eng], 1))
        self._commit(tok, reads, writes)
        return tok

    def dma(self, q, out, in_, reads=(), writes=()):
        ring = self.dsems[q]
        i = self.dnext[q]
        self.dnext[q] = (i + 1) % len(ring)
        ent = ring[i]
        semh = ent[0]
        key = ("d", q, i)
        self.semobj[key] = semh
        if ent[1] > 0 and self.waited[q].get(key, 0) < ent[1]:
            self.waited[q][key] = ent[1]
            self.ops[q].append(("wait", semh, ent[1]))
        self._emit_waits(q, self._collect(q, reads, writes))
        ent[1] += 16
        tok = (key, ent[1])
        self.ops[q].append(("dma", out, in_, semh))
        self._commit(tok, reads, writes)
        return tok

    def custom(self, q, fn, cs, reads=(), writes=()):
        self._emit_waits(q, self._collect(q, reads, writes))
        key = ("c", id(cs))
        self.semobj[key] = cs[0]
        cs[1] += 1
        tok = (key, cs[1])
        self.ops[q].append(("ins", fn, cs[0], 1))
        self._commit(tok, reads, writes)
        return tok

    def barrier(self):
        toks = [(e, self.cnt[e]) for e in ("pe", "act", "dve", "pool") if self.cnt[e] > 0]
        for q, ring in self.dsems.items():
            for i, ent in enumerate(ring):
                if ent[1] > 0:
                    key = ("d", q, i)
                    self.semobj[key] = ent[0]
                    toks.append((key, ent[1]))
        for e in ENGS:
            for key, val in toks:
                if key == e:
                    continue
                if self.waited[e].get(key, 0) >= val:
                    continue
                self.waited[e][key] = val
                semh = self.sem[key] if isinstance(key, str) else self.semobj[key]
                self.ops[e].append(("wait", semh, val))

    def wait_all(self, q, resources):
        self._emit_waits(q, self._collect(q, resources, resources))

    def replay(self, block):
        names = {"pe": "tensor", "act": "scalar", "dve": "vector", "pool": "gpsimd", "sp": "sync"}

        def make(e):
            def run(eng):
                for item in self.ops[e]:
                    k = item[0]
                    if k == "wait":
                        eng.wait_ge(item[1], item[2])
                    elif k == "ins":
                        ins = item[1](eng)
                        if item[2] is not None:
                            ins.then_inc(item[2], item[3])
                    else:
                        eng.dma_start(out=item[1], in_=item[2]).then_inc(item[3], 16)
            return run
        for e in ENGS:
            getattr(block, names[e])(make(e))


def ctiles(c0, c1, step=512):
    out = []
    while c0 < c1:
        n = min(step, c1 - c0)
        out.append((c0, n))
        c0 += n
    return out


def build_program():
    nc = bass.Bass("TRN2", target_bir_lowering=False)

    def din(name, shape):
        return nc.dram_tensor(name, list(shape), F32, kind="ExternalInput").ap()

    def dout(name, shape):
        return nc.dram_tensor(name, list(shape), F32, kind="ExternalOutput").ap()

    xin = din("xin", (NX, D))
    pin = din("pin", (2, NTOK, 256))
    spool = din("spool", (240, D))
    sret = din("sret", (NSQ, NH, DK, DV))
    sconv = din("sconv", (2, 32, F2))
    small = din("small", (128, 128))
    convp = din("convp", (768, 128))
    identd = din("identd", (128, 128))
    pool_w = din("pool_w", (4, 512, 512))
    w_in = din("ret_w_in", (D, 12288))
    w_out = din("ret_w_out", (4096, D))
    w_up = din("ffn_w_up", (2, D, F2))
    w_down = din("ffn_w_down", (2, FF, D))
    w_proj = din("ple_w_proj", (2, 256, D))
    w_gate = din("ple_w_gate", (2, D, D))
    cs_d = din("cs", (128, 2, NTOK))
    crossq_d = din("crossq", (NH, NTOK))
    dmask_d = din("dmask", (128, NH, 128))
    dmasks_d = din("dmasks", (64, NH, 64))
    kdt_d = din("kdt", (128, NH))
    kds_d = din("kds", (64, NH, NSQ))
    invc_d = din("invc", (1, 60))
    flag_d = din("flagv", (1, 1 + NH))

    y_o = dout("y", (NTOK, D))
    poolp_o = dout("pool_p", (15, D))
    pools_o = dout("pool_s", (240, D))
    retp_o = dout("ret_p", (NH, DK, DV))
    rets_o = dout("ret_s", (NSQ, NH, DK, DV))
    convp_o = dout("conv_p", (2, 2, F2))
    convs_o = dout("conv_s", (2, 32, F2))

    xspill = nc.dram_tensor("xspill", [128, DC * NX], F32).ap()
    cc_in = [nc.dram_tensor(f"cc_in{h}", [DK, DV], F32) for h in range(NH)]
    cc_out = [nc.dram_tensor(f"cc_out{h}", [2 * DK, DV], F32) for h in range(NH)]
    cx_in = nc.dram_tensor("cx_in", [2, D], F32)
    cx_out = nc.dram_tensor("cx_out", [4, D], F32)

    with contextlib.ExitStack() as st:
        E = st.enter_context
        kb = KB(nc, st)
        ccsem = [E(nc.semaphore("ccsem")), 0]

        def sb(name, shape, dt):
            return E(nc.sbuf_tensor("sb_" + name, list(shape), dt))

        xT = sb("xT", (128, DC, NX), F32)
        hT = sb("hT", (128, DC, NX), BF16)
        NWS = 4
        wsl = [sb(f"wsl{i}", (128, 4096), BF16) for i in range(NWS)]
        gamT = sb("gamT", (128, 128), F32)
        convT = sb("convT", (128, 768), F32)
        identf = sb("identf", (128, 128), F32)
        identb = sb("identb", (128, 128), BF16)
        onesb = sb("onesb", (128, 128), BF16)
        epst = sb("epst", (128, 1), F32)
        flagt = sb("flagt", (128, 1 + NH), F32)
        cst = sb("cst", (128, 2, NTOK), F32)
        dmask = sb("dmask", (128, NH, 128), F32)
        dmasks = sb("dmasks", (64, NH, 64), F32)
        kdt = sb("kdt", (128, NH), F32)
        kds = sb("kds", (64, NH, NSQ), F32)
        invc = sb("invc", (128, 60), F32)
        rstd = sb("rstd", (128, 512), F32)
        SCRW = 12544
        scr = sb("scr", (128, SCRW), F32)
        pbank = [E(nc.psum_tensor(f"pb{i}", [128, 512], F32)) for i in range(8)]

        R = {}

        def res(name):
            if name not in R:
                R[name] = Res(name)
            return R[name]

        Rx, Rh = res("xT"), res("hT")
        Rw = [res(f"w{i}") for i in range(NWS)]
        Rp = [res(f"pb{i}") for i in range(8)]
        Rc = res("consts")
        Rrstd = res("rstd")
        Rscr = {}

        def carve(off_b, shape, dt, base=None):
            t = scr if base is None else base
            n = int(np.prod(shape[1:]))
            nb = n * (4 if dt == F32 else 2)
            assert off_b % 4 == 0 and nb % 4 == 0
            ap = t[:, off_b // 4: off_b // 4 + nb // 4]
            if dt != F32:
                ap = ap.bitcast(dt)
            if len(shape) == 3:
                ap = ap.rearrange("p (a b) -> p a b", a=shape[1])
            elif len(shape) == 4:
                ap = ap.rearrange("p (a b c) -> p a b c", a=shape[1], b=shape[2])
            return ap[0:shape[0]]

        state = {"bank": 0, "ws": 0}

        def nbank(lo=0, hi=6):
            b = lo + state["bank"] % (hi - lo)
            state["bank"] += 1
            return b

        def wslot():
            s = state["ws"] % NWS
            state["ws"] += 1
            return s

        kb.dma("sp", identf[:], identd, writes=[Rc])
        kb.dma("pool", identb[:], identd, writes=[Rc])
        kb.dma("sp", cst[:], cs_d, writes=[Rc])
        kb.dma("sp", dmask[:], dmask_d, writes=[Rc])
        kb.dma("sp", dmasks[:], dmasks_d, writes=[Rc])
        kb.dma("sp", kdt[:], kdt_d, writes=[Rc])
        kb.dma("sp", kds[:], kds_d, writes=[Rc])
        kb.dma("sp", invc[:], invc_d.partition_broadcast(128), writes=[Rc])
        kb.dma("sp", flagt[:], flag_d.partition_broadcast(128), writes=[Rc])
        kb.op("dve", lambda e: e.memset(onesb[:], 1.0), writes=[Rc])
        kb.op("dve", lambda e: e.memset(epst[:], EPS), writes=[Rc])

        stg = [carve(0, (128, D), F32), carve(8192, (128, D), F32)]
        Rstg = [res("stg0"), res("stg1")]
        stgi = {"i": 0}

        def tr_in(src_rows, nrows, dst_fn, rdst, width=D, evac="act"):
            i = stgi["i"] % 2
            stgi["i"] += 1
            kb.dma("sp", stg[i][0:nrows, 0:width], src_rows, writes=[Rstg[i]])
            for c4 in range(0, width // 128, 4):
                nch = min(4, width // 128 - c4)
                b = nbank()
                pv = pbank[b][:, :].rearrange("p (a b) -> p a b", a=4)
                kb.group("pe", [
                    (lambda e, k=k, c4=c4, i=i, pv=pv: e.transpose(pv[:, k, 0:nrows], stg[i][0:nrows, (c4 + k) * 128:(c4 + k + 1) * 128], identf[0:nrows, 0:nrows]))
                    for k in range(nch)], reads=[Rstg[i], Rc], writes=[Rp[b]])
                dst = dst_fn(c4, nch)
                if evac == "act":
                    kb.op("act", lambda e, dst=dst, pv=pv, nch=nch: e.activation(out=dst, in_=pv[:, 0:nch, 0:nrows], func=AF.Copy), reads=[Rp[b]], writes=[rdst])
                else:
                    kb.op("dve", lambda e, dst=dst, pv=pv, nch=nch: e.tensor_copy(out=dst, in_=pv[:, 0:nch, 0:nrows]), reads=[Rp[b]], writes=[rdst])

        def tr_out(src_fn, rsrc, ncols, nch_total, ostg, rostg, dst_rows):
            for c4 in range(0, nch_total, 4):
                nch = min(4, nch_total - c4)
                b = nbank()
                kb.group("pe", [
                    (lambda e, k=k, c4=c4, b=b: e.transpose(pbank[b][0:ncols, k * 128:(k + 1) * 128], src_fn(c4 + k), identf[:, :]))
                    for k in range(nch)], reads=[rsrc, Rc], writes=[Rp[b]])
                kb.op("act", lambda e, c4=c4, nch=nch, b=b: e.activation(out=ostg[0:ncols, c4 * 128:(c4 + nch) * 128], in_=pbank[b][0:ncols, 0:nch * 128], func=AF.Copy),
                      reads=[Rp[b]], writes=[rostg])
            kb.dma("sp", dst_rows, ostg[0:ncols, 0:nch_total * 128], reads=[rostg], writes=[res("outdram")])

        def wload(pieces):
            s = wslot()
            for dst_fn, src in pieces:
                kb.dma("pool", dst_fn(wsl[s]), src, writes=[Rw[s]])
            return s

        def wview_d(s, ncols):
            return wsl[s][:, 0:16 * ncols].rearrange("p (c f) -> p c f", c=16)

        def rmsnorm(gcol, c0, c1, sqv, rsq):
            for (t0, n) in ctiles(c0, c1):
                kb.op("act", lambda e, t0=t0, n=n: e.activation(out=sqv[:, :, 0:n], in_=xT[:, :, t0:t0 + n], func=AF.Square, scale=float(D) ** -0.5),
                      reads=[Rx], writes=[rsq])
                b = nbank()
                kb.group("pe", [(lambda e, c=c, n=n, b=b: e.matmul(pbank[b][:, 0:n], onesb[:, :], sqv[:, c, 0:n], start=(c == 0), stop=(c == DC - 1))) for c in range(DC)],
                         reads=[rsq, Rc], writes=[Rp[b]])
                kb.op("act", lambda e, n=n, b=b: e.activation(out=rstd[:, 0:n], in_=pbank[b][:, 0:n], func=AF.Sqrt, bias=epst[:, 0:1]),
                      reads=[Rp[b], Rc], writes=[Rrstd])
                kb.op("dve", lambda e, n=n: e.reciprocal(out=rstd[:, 0:n], in_=rstd[:, 0:n]), reads=[Rrstd], writes=[Rrstd])
                for c in range(DC):
                    kb.op("dve", lambda e, c=c, t0=t0, n=n: e.scalar_tensor_tensor(out=hT[:, c, t0:t0 + n], in0=xT[:, c, t0:t0 + n], scalar=gamT[:, gcol + c:gcol + c + 1],
                                                                                  in1=rstd[:, 0:n], op0=ALU.mult, op1=ALU.mult),
                          reads=[Rx, Rrstd, Rc], writes=[Rh])

        for i in range(1):
            kb.dma("sp", stg[0][:, 0:128], small, writes=[Rstg[0]])
            b = nbank()
            kb.op("pe", lambda e, b=b: e.transpose(pbank[b][:, 0:128], stg[0][:, 0:128], identf[:, :]), reads=[Rstg[0], Rc], writes=[Rp[b]])
            kb.op("act", lambda e, b=b: e.activation(out=gamT[:, :], in_=pbank[b][:, 0:128], func=AF.Copy), reads=[Rp[b]], writes=[Rc])
            for j in range(6):
                kb.dma("sp", stg[1][:, 0:128], convp[j * 128:(j + 1) * 128, :], writes=[Rstg[1]])
                b = nbank()
                kb.op("pe", lambda e, b=b: e.transpose(pbank[b][:, 0:128], stg[1][:, 0:128], identf[:, :]), reads=[Rstg[1], Rc], writes=[Rp[b]])
                kb.op("act", lambda e, b=b, j=j: e.activation(out=convT[:, j * 128:(j + 1) * 128], in_=pbank[b][:, 0:128], func=AF.Copy), reads=[Rp[b]], writes=[Rc])
        for (r0, n) in ctiles(0, NX, 128):
            tr_in(xin[r0:r0 + n, :], n, lambda c4, nch, r0=r0, n=n: xT[:, c4:c4 + nch, r0:r0 + n], Rx)
        kb.barrier()

        def tr_in2(stg_ap, rstg, src_rows, nrows, width, copies):
            kb.dma("sp", stg_ap[0:nrows, 0:width], src_rows, writes=[rstg])
            nch = width // 128
            b = nbank()
            pv = pbank[b][:, :].rearrange("p (a b) -> p a b", a=4)
            kb.group("pe", [(lambda e, k=k, pv=pv: e.transpose(pv[:, k, 0:nrows], stg_ap[0:nrows, k * 128:(k + 1) * 128], identf[0:nrows, 0:nrows])) for k in range(nch)],
                     reads=[rstg, Rc], writes=[Rp[b]])
            copies(pv, b)

        def sumsq_rstd(dst, rdst, c0, c1, sqv, rsq, step):
            for (t0, n) in ctiles(c0, c1, step):
                kb.op("act", lambda e, t0=t0, n=n: e.activation(out=sqv[:, :, 0:n], in_=xT[:, :, t0:t0 + n], func=AF.Square, scale=float(D) ** -0.5), reads=[Rx], writes=[rsq])
                b = nbank()
                kb.group("pe", [(lambda e, c=c, n=n, b=b: e.matmul(pbank[b][:, 0:n], onesb[:, :], sqv[:, c, 0:n], start=(c == 0), stop=(c == DC - 1))) for c in range(DC)],
                         reads=[rsq, Rc], writes=[Rp[b]])
                kb.op("act", lambda e, t0=t0, n=n, b=b: e.activation(out=dst[:, t0:t0 + n], in_=pbank[b][:, 0:n], func=AF.Sqrt, bias=epst[:, 0:1]), reads=[Rp[b], Rc], writes=[rdst])
            kb.op("dve", lambda e: e.reciprocal(out=dst[:, c0:c1], in_=dst[:, c0:c1]), reads=[rdst], writes=[rdst])

        POOLW = (2, 4, 8, 16)
        L = S0
        ta = carve(0, (128, NX), F32)
        hf = carve(4480, (128, NX), F32)
        tb = carve(8960, (128, NX), F32)
        pp = carve(13440, (128, NX), F32)
        hs_g = carve(17920, (128, 4, NSQ, 19), F32)
        hst = [carve(22784, (128, NSQ, 19), F32), carve(24000, (128, NSQ, 19), F32)]
        fixt = carve(25216, (128, 16), F32)
        pend_g = carve(25280, (128, 4, 15), F32)
        hcmp_g = carve(25536, (128, 4, 120), F32)
        stg2 = [carve(27456, (128, 512), F32), carve(29504, (128, 512), F32)]
        ostg_g = carve(31552, (128, 512), F32)
        sqv = carve(33600, (128, DC, 64), BF16)
        Rta, Rhf, Rtb, Rpp, Rhsg, Rhst, Rfix, Rpend, Rhcmp, Rostg, Rsq = [res(n) for n in "ta hf tb pp hsg hst fix pend hcmp ostg sq".split()]
        Rstg2 = [res("stg2a"), res("stg2b")]
        sumsq_rstd(ta, Rta, 0, NX, sqv, Rsq, 64)
        for g in range(4):
            win = POOLW[g]
            steps = {2: 0, 4: 1, 8: 2, 16: 3}[win]
            for half in range(2):
                def cp(pv, b, half=half):
                    for k in range(4):
                        kb.op("act", lambda e, k=k, pv=pv: e.activation(out=hs_g[:, k, half * 8:(half + 1) * 8, 0:15], in_=pv[:, k, 0:120].rearrange("p (b c) -> p b c", b=8), func=AF.Copy),
                              reads=[Rp[b]], writes=[Rhsg])
                tr_in2(stg2[half], Rstg2[half], spool[half * 120:(half + 1) * 120, g * 512:(g + 1) * 512], 120, 512, cp)
            for k in range(4):
                c = 4 * g + k
                kb.op("dve", lambda e, c=c: e.scalar_tensor_tensor(out=hf[:, :], in0=xT[:, c, :], scalar=gamT[:, c:c + 1], in1=ta[:, :], op0=ALU.mult, op1=ALU.mult),
                      reads=[Rx, Rta, Rc], writes=[Rhf])
                kb.op("act", lambda e, k=k: e.activation(out=hs_g[:, k, :, 15:19], in_=hf[:, S0:NX].rearrange("p (b t) -> p b t", b=NSQ), func=AF.Copy), reads=[Rhf], writes=[Rhsg])
                kb.op("act", lambda e, k=k: e.activation(out=pend_g[:, k, :], in_=hf[:, S0 - 15:S0], func=AF.Copy), reads=[Rhf], writes=[Rpend])
                kb.op("dve", lambda e: e.tensor_tensor(out=tb[:, 1:L], in0=hf[:, 1:L], in1=hf[:, 0:L - 1], op=ALU.add), reads=[Rhf], writes=[Rtb])
                src, rsrc = tb, Rtb
                sh = 2
                for _ in range(steps):
                    dst, rdst = (pp, Rpp) if src is tb else (tb, Rtb)
                    kb.op("dve", lambda e, src=src, dst=dst, sh=sh: e.tensor_tensor(out=dst[:, 2 * sh - 1:L], in0=src[:, 2 * sh - 1:L], in1=src[:, sh - 1:L - sh], op=ALU.add),
                          reads=[rsrc], writes=[rdst])
                    src, rsrc = dst, rdst
                    sh *= 2
                kb.op("dve", lambda e, c=c, src=src, win=win: e.scalar_tensor_tensor(out=hT[:, c, 15:L], in0=src[:, 15:L], scalar=1.0 / win, in1=hf[:, 15:L], op0=ALU.mult, op1=ALU.subtract),
                      reads=[rsrc, Rhf], writes=[Rh])
                kb.op("dve", lambda e, src=src, g=g: e.tensor_tensor(out=fixt[:, 0:15], in0=src[:, P0:P0 + 15], in1=invc[:, g * 15:(g + 1) * 15], op=ALU.mult), reads=[rsrc, Rc], writes=[Rfix])
                kb.op("dve", lambda e, c=c: e.tensor_tensor(out=hT[:, c, P0:P0 + 15], in0=fixt[:, 0:15], in1=hf[:, P0:P0 + 15], op=ALU.subtract), reads=[Rfix, Rhf], writes=[Rh])
                hsrc = hs_g[:, k, :, :]
                kb.op("dve", lambda e, hsrc=hsrc: e.tensor_tensor(out=hst[0][:, :, 1:19], in0=hsrc[:, :, 1:19], in1=hsrc[:, :, 0:18], op=ALU.add), reads=[Rhsg], writes=[Rhst])
                a_i = 0
                sh = 2
                for _ in range(steps):
                    kb.op("dve", lambda e, a=a_i, sh=sh: e.tensor_tensor(out=hst[1 - a][:, :, 2 * sh - 1:19], in0=hst[a][:, :, 2 * sh - 1:19], in1=hst[a][:, :, sh - 1:19 - sh], op=ALU.add),
                          reads=[Rhst], writes=[Rhst])
                    a_i = 1 - a_i
                    sh *= 2
                kb.op("dve", lambda e, c=c, a=a_i, win=win, hsrc=hsrc: e.scalar_tensor_tensor(out=hT[:, c, S0:NX].rearrange("p (b t) -> p b t", b=NSQ), in0=hst[a][:, :, 15:19], scalar=1.0 / win,
                                                                                          in1=hsrc[:, :, 15:19], op0=ALU.mult, op1=ALU.subtract), reads=[Rhst, Rhsg], writes=[Rh])
            tr_out(lambda k: pend_g[:, k, :], Rpend, 15, 4, ostg_g, Rostg, poolp_o[:, g * 512:(g + 1) * 512])
            for half in range(2):
                for k in range(4):
                    kb.op("act", lambda e, k=k, half=half: e.activation(out=hcmp_g[:, k, :].rearrange("p (b t) -> p b t", b=8), in_=hs_g[:, k, half * 8:(half + 1) * 8, 4:19], func=AF.Copy),
                          reads=[Rhsg], writes=[Rhcmp])
                tr_out(lambda k: hcmp_g[:, k, :], Rhcmp, 120, 4, ostg_g, Rostg, pools_o[half * 120:(half + 1) * 120, g * 512:(g + 1) * 512])
            s_ = wload([(lambda sl: sl[:, 0:2048].rearrange("p (c f) -> p c f", c=4), pool_w[g].rearrange("(c p) f -> p c f", p=128))])
            wv = wsl[s_][:, 0:2048].rearrange("p (c f) -> p c f", c=4)
            for o in range(4):
                for (t0, n) in ctiles(15, NX):
                    b = nbank()
                    kb.group("pe", [(lambda e, ci=ci, o=o, t0=t0, n=n, b=b, wv=wv, g=g: e.matmul(pbank[b][:, 0:n], wv[:, ci, o * 128:(o + 1) * 128], hT[:, 4 * g + ci, t0:t0 + n], start=(ci == 0), stop=(ci == 3)))
                                    for ci in range(4)], reads=[Rw[s_], Rh], writes=[Rp[b]])
                    kb.op("dve", lambda e, o=o, t0=t0, n=n, b=b, g=g: e.scalar_tensor_tensor(out=xT[:, 4 * g + o, t0:t0 + n], in0=pbank[b][:, 0:n], scalar=gamT[:, 112 + 4 * g + o:113 + 4 * g + o],
                                                                                         in1=xT[:, 4 * g + o, t0:t0 + n], op0=ALU.mult, op1=ALU.add), reads=[Rp[b], Rx, Rc], writes=[Rx])
        kb.barrier()

        sq512 = carve(0, (128, DC, 512), BF16)
        Rsq5 = res("sq512")

        def conv_ffn(l):
            rmsnorm(32 + l * 16, 15, NX, sq512, Rsq5)
            kb.barrier()
            NU = NX - 15
            ub = [carve(0, (128, NU), F32), carve(4416, (128, NU), F32)]
            cbf = [carve(8832, (128, NTOK), F32), carve(8832 + 4352, (128, NTOK), F32)]
            sgb = carve(17536, (128, NTOK), F32)
            uext = carve(21888, (128, 2, NSQ, 6), F32)
            aT = [carve(22656, (128, 4, NTOK), BF16), carve(22656 + 8704, (128, 4, NTOK), BF16)]
            stg_s = [carve(40064, (128, 512), F32), carve(42112, (128, 512), F32)]
            uh_g = carve(44160, (128, 4, 32), F32)
            un_g = carve(44672, (128, 4, 34), F32)
            ostg_s = carve(45248, (128, 512), F32)
            Rub, Rcb = [res("ub0"), res("ub1")], [res("cb0"), res("cb1")]
            Rsg, Rue, RaT = res("sgb"), res("uext"), [res("aT0"), res("aT1")]
            Rss, Ruh, Run, Ros = [res("stgs0"), res("stgs1")], res("uhg"), res("ung"), res("ostgs")
            wupv = w_up[l].rearrange("(c p) f -> p c f", p=128)
            wdnv = w_down[l].rearrange("(k p) d -> p k d", p=128)
            for jj in range(22):
                sg_ = wload([(lambda sl: sl[:, :].rearrange("p (c f) -> p c f", c=16), wupv[:, :, jj * 256:(jj + 1) * 256])])
                su_ = wload([(lambda sl: sl[:, :].rearrange("p (c f) -> p c f", c=16), wupv[:, :, FF + jj * 256:FF + (jj + 1) * 256])])
                wg_v = wsl[sg_][:, :].rearrange("p (c f) -> p c f", c=16)
                wu_v = wsl[su_][:, :].rearrange("p (c f) -> p c f", c=16)
                for gu in range(2):
                    def cp(pv, b, gu=gu):
                        kb.op("act", lambda e, pv=pv: e.activation(out=uh_g[:, 2 * gu:2 * gu + 2, :], in_=pv[:, 0:2, 0:32], func=AF.Copy), reads=[Rp[b]], writes=[Ruh])
                    tr_in2(stg_s[gu], Rss[gu], sconv[l, :, gu * FF + jj * 256:gu * FF + (jj + 1) * 256], 32, 256, cp)
                a_i = (jj // 2) % 2
                kk0 = 2 * (jj % 2)
                for k in range(2):
                    j = 2 * jj + k
                    for gu, wv_ in ((0, wg_v), (1, wu_v)):
                        fch = gu * 44 + j
                        for (t0, n) in ctiles(15, NX):
                            b = nbank()
                            kb.group("pe", [(lambda e, c=c, k=k, t0=t0, n=n, b=b, wv_=wv_: e.matmul(pbank[b][:, 0:n], wv_[:, c, k * 128:(k + 1) * 128], hT[:, c, t0:t0 + n], start=(c == 0), stop=(c == DC - 1)))
                                            for c in range(DC)], reads=[Rw[sg_ if gu == 0 else su_], Rh], writes=[Rp[b]])
                            kb.op("act", lambda e, gu=gu, t0=t0, n=n, b=b: e.activation(out=ub[gu][:, t0 - 15:t0 - 15 + n], in_=pbank[b][:, 0:n], func=AF.Copy), reads=[Rp[b]], writes=[Rub[gu]])
                        cw = [convT[:, (l * 3 + q) * 88 + fch:(l * 3 + q) * 88 + fch + 1] for q in range(3)]
                        cbias = convT[:, 528 + l * 88 + fch:528 + l * 88 + fch + 1]
                        u = ub[gu]
                        kb.op("act", lambda e, gu=gu, u=u, cw=cw, cbias=cbias: e.activation(out=cbf[gu][:, 0:NPR], in_=u[:, 2:2 + NPR], func=AF.Identity, scale=cw[2], bias=cbias), reads=[Rub[gu], Rc], writes=[Rcb[gu]])
                        kb.op("dve", lambda e, gu=gu, u=u, cw=cw: e.scalar_tensor_tensor(out=cbf[gu][:, 0:NPR], in0=u[:, 1:1 + NPR], scalar=cw[1], in1=cbf[gu][:, 0:NPR], op0=ALU.mult, op1=ALU.add),
                              reads=[Rub[gu], Rcb[gu], Rc], writes=[Rcb[gu]])
                        kb.op("dve", lambda e, gu=gu, u=u, cw=cw: e.scalar_tensor_tensor(out=cbf[gu][:, 0:NPR], in0=u[:, 0:NPR], scalar=cw[0], in1=cbf[gu][:, 0:NPR], op0=ALU.mult, op1=ALU.add),
                              reads=[Rub[gu], Rcb[gu], Rc], writes=[Rcb[gu]])
                        ue = uext[:, gu, :, :]
                        kb.op("act", lambda e, ue=ue, gu=gu, k=k: e.activation(out=ue[:, :, 0:2], in_=uh_g[:, 2 * gu + k, :].rearrange("p (b t) -> p b t", b=NSQ), func=AF.Copy), reads=[Ruh], writes=[Rue])
                        kb.op("act", lambda e, ue=ue, u=u: e.activation(out=ue[:, :, 2:6], in_=u[:, 2 + NPR:2 + NPR + NS].rearrange("p (b t) -> p b t", b=NSQ), func=AF.Copy), reads=[Rub[gu]], writes=[Rue])
                        cs_ = cbf[gu][:, NPR:NTOK].rearrange("p (b t) -> p b t", b=NSQ)
                        kb.op("act", lambda e, ue=ue, cs_=cs_, cw=cw, cbias=cbias: e.activation(out=cs_, in_=ue[:, :, 2:6], func=AF.Identity, scale=cw[2], bias=cbias), reads=[Rue, Rc], writes=[Rcb[gu]])
                        kb.op("dve", lambda e, ue=ue, cs_=cs_, cw=cw: e.scalar_tensor_tensor(out=cs_, in0=ue[:, :, 1:5], scalar=cw[1], in1=cs_, op0=ALU.mult, op1=ALU.add), reads=[Rue, Rcb[gu], Rc], writes=[Rcb[gu]])
                        kb.op("dve", lambda e, ue=ue, cs_=cs_, cw=cw: e.scalar_tensor_tensor(out=cs_, in0=ue[:, :, 0:4], scalar=cw[0], in1=cs_, op0=ALU.mult, op1=ALU.add), reads=[Rue, Rcb[gu], Rc], writes=[Rcb[gu]])
                        kb.op("act", lambda e, u=u, gu=gu, k=k: e.activation(out=un_g[:, 2 * gu + k, 0:2], in_=u[:, NPR:NPR + 2], func=AF.Copy), reads=[Rub[gu]], writes=[Run])
                        kb.op("act", lambda e, ue=ue, gu=gu, k=k: e.activation(out=un_g[:, 2 * gu + k, 2:34].rearrange("p (b t) -> p b t", b=NSQ), in_=ue[:, :, 4:6], func=AF.Copy), reads=[Rue], writes=[Run])
                    kb.op("act", lambda e: e.activation(out=sgb[:, :], in_=cbf[0][:, :], func=AF.Silu), reads=[Rcb[0]], writes=[Rsg])
                    kb.op("dve", lambda e, a_i=a_i, k=k, kk0=kk0: e.tensor_tensor(out=aT[a_i][:, kk0 + k, :], in0=sgb[:, :], in1=cbf[1][:, :], op=ALU.mult), reads=[Rsg, Rcb[1]], writes=[RaT[a_i]])
                for gu in range(2):
                    b = nbank()
                    kb.group("pe", [(lambda e, k=k, gu=gu, b=b: e.transpose(pbank[b][0:34, k * 128:(k + 1) * 128], un_g[:, 2 * gu + k, :], identf[:, :])) for k in range(2)], reads=[Run, Rc], writes=[Rp[b]])
                    kb.op("act", lambda e, b=b: e.activation(out=ostg_s[0:34, 0:256], in_=pbank[b][0:34, 0:256], func=AF.Copy), reads=[Rp[b]], writes=[Ros])
                    kb.dma("sp", convp_o[l, :, gu * FF + jj * 256:gu * FF + (jj + 1) * 256], ostg_s[0:2, 0:256], reads=[Ros], writes=[res("outdram")])
                    kb.dma("sp", convs_o[l, :, gu * FF + jj * 256:gu * FF + (jj + 1) * 256], ostg_s[2:34, 0:256], reads=[Ros], writes=[res("outdram")])
                if jj % 2 == 0:
                    continue
                sds = []
                for hh in range(2):
                    sd_ = wload([(lambda sl: sl[:, :].rearrange("p (k d) -> p k d", k=2), wdnv[:, 2 * (jj - 1 + hh):2 * (jj - 1 + hh) + 2, :])])
                    sds.append((sd_, wsl[sd_][:, :].rearrange("p (k d) -> p k d", k=2)))
                for o in range(DC):
                    for (t0, n) in ctiles(0, NTOK):
                        b = nbank()
                        kb.group("pe", [(lambda e, k=k, o=o, t0=t0, n=n, b=b, a_i=a_i, sds=sds: e.matmul(pbank[b][:, 0:n], sds[k // 2][1][:, k % 2, o * 128:(o + 1) * 128], aT[a_i][:, k, t0:t0 + n], start=(k == 0), stop=(k == 3)))
                                        for k in range(4)], reads=[Rw[sds[0][0]], Rw[sds[1][0]], RaT[a_i]], writes=[Rp[b]])
                        kb.op("dve", lambda e, o=o, t0=t0, n=n, b=b: e.tensor_tensor(out=xT[:, o, P0 + t0:P0 + t0 + n], in0=pbank[b][:, 0:n], in1=xT[:, o, P0 + t0:P0 + t0 + n], op=ALU.add),
                              reads=[Rp[b], Rx], writes=[Rx])
            kb.barrier()

        def ple(l):
            rmsnorm(64 + l * 16, P0, NX, sq512, Rsq5)
            kb.barrier()
            pT = carve(0, (128, 2, NTOK), BF16)
            stg_p = [carve(4352, (128, 256), F32), carve(5376, (128, 256), F32)]
            tg = [carve(6400, (128, 512), F32), carve(8448, (128, 512), F32)]
            RpT, Rsp, Rtg = res("pT"), [res("stgp0"), res("stgp1")], [res("tg0"), res("tg1")]
            for i_, (r0, n) in enumerate(ctiles(0, NTOK, 128)):
                def cp(pv, b, r0=r0, n=n):
                    kb.op("act", lambda e, pv=pv: e.activation(out=pT[:, :, r0:r0 + n], in_=pv[:, 0:2, 0:n], func=AF.Copy), reads=[Rp[b]], writes=[RpT])
                tr_in2(stg_p[i_ % 2], Rsp[i_ % 2], pin[l, r0:r0 + n, :], n, 256, cp)
            wp_v = carve(10496, (128, 2, D), BF16)
            Rwp = res("wp_v")
            kb.dma("pool", wp_v, w_proj[l].rearrange("(k p) d -> p k d", p=128), writes=[Rwp])
            wgv = w_gate[l].rearrange("(c p) f -> p c f", p=128)
            it = 0
            for gq in range(8):
                sg_ = wload([(lambda sl: sl[:, :].rearrange("p (c f) -> p c f", c=16), wgv[:, :, gq * 256:(gq + 1) * 256])])
                wg_v = wsl[sg_][:, :].rearrange("p (c f) -> p c f", c=16)
                for o2 in range(2):
                    o = 2 * gq + o2
                    for (t0, n) in ctiles(0, NTOK):
                        b1 = nbank()
                        kb.group("pe", [(lambda e, c=c, o2=o2, t0=t0, n=n, b1=b1, wg_v=wg_v: e.matmul(pbank[b1][:, 0:n], wg_v[:, c, o2 * 128:(o2 + 1) * 128], hT[:, c, P0 + t0:P0 + t0 + n], start=(c == 0), stop=(c == DC - 1)))
                                        for c in range(DC)], reads=[Rw[sg_], Rh], writes=[Rp[b1]])
                        b2 = nbank()
                        kb.group("pe", [(lambda e, k=k, o=o, t0=t0, n=n, b2=b2: e.matmul(pbank[b2][:, 0:n], wp_v[:, k, o * 128:(o + 1) * 128], pT[:, k, t0:t0 + n], start=(k == 0), stop=(k == 1)))
                                        for k in range(2)], reads=[Rwp, RpT], writes=[Rp[b2]])
                        ti = it % 2
                        it += 1
                        kb.op("act", lambda e, n=n, b1=b1, ti=ti: e.activation(out=tg[ti][:, 0:n], in_=pbank[b1][:, 0:n], func=AF.Sigmoid), reads=[Rp[b1]], writes=[Rtg[ti]])
                        kb.op("dve", lambda e, n=n, b2=b2, ti=ti: e.tensor_tensor(out=tg[ti][:, 0:n], in0=pbank[b2][:, 0:n], in1=tg[ti][:, 0:n], op=ALU.mult), reads=[Rp[b2], Rtg[ti]], writes=[Rtg[ti]])
                        kb.op("dve", lambda e, o=o, t0=t0, n=n, ti=ti: e.tensor_tensor(out=xT[:, o, P0 + t0:P0 + t0 + n], in0=tg[ti][:, 0:n], in1=xT[:, o, P0 + t0:P0 + t0 + n], op=ALU.add),
                              reads=[Rtg[ti], Rx], writes=[Rx])
            kb.barrier()

        conv_ffn(0)
        ple(0)
        if DEBUG:
            dbg0 = dout("dbg0", (NTOK, D))
            d_st = carve(0, (128, D), F32)
            for (r0, n) in ctiles(0, NTOK, 128):
                tr_out(lambda c, r0=r0, n=n: xT[:, c, P0 + r0:P0 + r0 + n], Rx, n, DC, d_st, res("dst_dbg"), dbg0[r0:r0 + n, :])
            kb.barrier()

        LOGG = [float(np.log1p(-(2.0 ** (-5.0 - h)))) for h in range(NH)]
        xflat = xT[:, :, :].rearrange("p a b -> p (a b)")
        ypart = [nc.dram_tensor(f"ypart{h}", [128, DC * NTOK], F32).ap() for h in range(NH)]
        rmsnorm(16, P0, NX, sq512, Rsq5)
        Rxsp = res("xspill")
        kb.dma("sp", xspill, xflat, reads=[Rx], writes=[Rxsp])
        kb.barrier()

        def xr(off, shape, dt):
            return carve(off, shape, dt, base=xflat)
        qT, qcT, kT = xr(0, (128, 2, NTOK), BF16), xr(4352, (128, 2, NTOK), BF16), xr(8704, (128, 2, NTOK), BF16)
        kdtok = xr(13056, (128, 9, 256), BF16)
        vtok = xr(17664, (128, 9, 512), BF16)
        ogT = xr(26880, (128, 4, NTOK), BF16)
        crq = xr(35584, (128, NTOK), F32)
        Sst = xr(39936, (128, 2, 512), F32)
        Sbf = xr(44032, (128, 2, 512), BF16)
        Sin = xr(46080, (128, 2, 512), F32)
        ktok_s = xr(50176, (128, 256), BF16)
        kdsb = [xr(50688, (128, 256), BF16), xr(51200, (128, 256), BF16)]
        Ssm = [xr(51712 + 4096 * i, (128, 2, 512), F32) for i in range(3)]
        Ssb = [xr(64000 + 2048 * i, (128, 2, 512), BF16) for i in range(2)]
        scm = xr(68096, (128, 128), BF16)
        scms = xr(68352, (128, 64), BF16)
        T = [carve(2048 * i, (128, 512), F32) for i in range(3)]
        ob = carve(6144, (128, 4, 128), BF16)
        osq = carve(7168, (128, 4, 128), BF16)
        mu, msq, var = carve(8192, (128, 128), F32), carve(8704, (128, 128), F32), carve(9216, (128, 128), F32)
        sgt = [carve(10240, (128, 512), F32), carve(12288, (128, 512), F32)]
        yst = [carve(14336 + 2048 * i, (128, 512), F32) for i in range(3)]
        sgT = carve(26624, (128, 4, NTOK), BF16)
        RsgT = res("sgT")
        RqT, RqcT, RkT, Rkd, Rvt, Rog, Rcrq, RSst, RSbf, RSin, Rkts = [res(n) for n in "qT qcT kT kdtok vtok ogT crq Sst Sbf Sin ktoks".split()]
        Rkdsb, RSsm, RSsb = [res("kdsb0"), res("kdsb1")], [res(f"Ssm{i}") for i in range(3)], [res(f"Ssb{i}") for i in range(2)]
        Rscm, Rscms, RT = res("scm"), res("scms"), [res(f"T{i}") for i in range(3)]
        Rob, Rosq, Rmu, Rmsq, Rvar = res("ob"), res("osq"), res("mu"), res("msq"), res("var")
        Rsgt, Ryst = [res("sgt0"), res("sgt1")], [res(f"yst{i}") for i in range(3)]
        w_in_v = w_in.rearrange("(c p) f -> p c f", p=128)
        w_out_v = w_out.rearrange("(k p) d -> p k d", p=128)
        cnt = {"sg": 0, "ys": 0, "ss": 0, "sb": 0, "kd": 0}

        def wd16(col0):
            s_ = wload([(lambda sl: sl[:, :].rearrange("p (c f) -> p c f", c=16), w_in_v[:, :, col0:col0 + 256])])
            return s_, wsl[s_][:, :].rearrange("p (c f) -> p c f", c=16)

        def groupnorm(psb, bo, n, tok0):
            kb.op("act", lambda e: e.activation(out=ob[:, :, 0:n], in_=psb[:, :, 0:n], func=AF.Copy), reads=[Rp[bo]], writes=[Rob])
            kb.op("act", lambda e: e.activation(out=osq[:, :, 0:n], in_=psb[:, :, 0:n], func=AF.Square), reads=[Rp[bo]], writes=[Rosq])
            b = nbank()
            kb.group("pe", [(lambda e, v=v, b=b: e.matmul(pbank[b][:, 0:n], onesb[:, :], ob[:, v, 0:n], start=(v == 0), stop=(v == 3))) for v in range(4)] +
                     [(lambda e, v=v, b=b: e.matmul(pbank[b][:, 128:128 + n], onesb[:, :], osq[:, v, 0:n], start=(v == 0), stop=(v == 3))) for v in range(4)],
                     reads=[Rob, Rosq, Rc], writes=[Rp[b]])
            kb.op("dve", lambda e, b=b: e.tensor_scalar(out=mu[:, 0:n], in0=pbank[b][:, 0:n], scalar1=1.0 / DV, scalar2=None, op0=ALU.mult), reads=[Rp[b]], writes=[Rmu])
            kb.op("dve", lambda e: e.tensor_tensor(out=msq[:, 0:n], in0=mu[:, 0:n], in1=mu[:, 0:n], op=ALU.mult), reads=[Rmu], writes=[Rmsq])
            kb.op("dve", lambda e, b=b: e.scalar_tensor_tensor(out=var[:, 0:n], in0=pbank[b][:, 128:128 + n], scalar=1.0 / DV, in1=msq[:, 0:n], op0=ALU.mult, op1=ALU.subtract),
                  reads=[Rp[b], Rmsq], writes=[Rvar])
            kb.op("act", lambda e: e.activation(out=var[:, 0:n], in_=var[:, 0:n], func=AF.Sqrt, bias=epst[:, 0:1]), reads=[Rvar, Rc], writes=[Rvar])
            kb.op("dve", lambda e: e.reciprocal(out=var[:, 0:n], in_=var[:, 0:n]), reads=[Rvar], writes=[Rvar])
            kb.op("dve", lambda e: e.tensor_tensor(out=psb[:, :, 0:n], in0=psb[:, :, 0:n], in1=mu[:, 0:n].unsqueeze(1).to_broadcast([128, 4, n]), op=ALU.subtract),
                  reads=[Rp[bo], Rmu], writes=[Rp[bo]])
            kb.op("dve", lambda e: e.tensor_tensor(out=ogT[:, :, tok0:tok0 + n], in0=psb[:, :, 0:n], in1=var[:, 0:n].unsqueeze(1).to_broadcast([128, 4, n]), op=ALU.mult),
                  reads=[Rp[bo], Rvar], writes=[Rog])

        for h in range(NH):
            g128 = float(np.exp(LOGG[h] * 128.0))
            g4 = float(np.exp(LOGG[h] * 4.0))
            kb.dma("sp", crq[:, :], crossq_d[h:h + 1, :].partition_broadcast(128), writes=[Rcrq])
            for which in range(2):
                s_, wv_ = wd16(which * 2048 + h * 256)
                for (t0, n) in ctiles(0, NTOK):
                    b1, b2 = nbank(), nbank()
                    for bb, half_ in ((b1, 0), (b2, 1)):
                        kb.group("pe", [(lambda e, c=c, bb=bb, half_=half_, t0=t0, n=n, wv_=wv_: e.matmul(pbank[bb][:, 0:n], wv_[:, c, half_ * 128:(half_ + 1) * 128], hT[:, c, P0 + t0:P0 + t0 + n], start=(c == 0), stop=(c == DC - 1)))
                                        for c in range(DC)], reads=[Rw[s_], Rh], writes=[Rp[bb]])
                    cosv, sinv = cst[:, 0, t0:t0 + n], cst[:, 1, t0:t0 + n]
                    dst = qT if which == 0 else kT
                    rdst = RqT if which == 0 else RkT
                    kb.op("dve", lambda e, b1=b1, n=n, cosv=cosv: e.tensor_tensor(out=T[0][:, 0:n], in0=pbank[b1][:, 0:n], in1=cosv, op=ALU.mult), reads=[Rp[b1], Rc], writes=[RT[0]])
                    kb.op("dve", lambda e, b2=b2, n=n, sinv=sinv: e.tensor_tensor(out=T[1][:, 0:n], in0=pbank[b2][:, 0:n], in1=sinv, op=ALU.mult), reads=[Rp[b2], Rc], writes=[RT[1]])
                    kb.op("dve", lambda e, n=n: e.tensor_tensor(out=T[0][:, 0:n], in0=T[0][:, 0:n], in1=T[1][:, 0:n], op=ALU.subtract), reads=[RT[0], RT[1]], writes=[RT[0]])
                    kb.op("act", lambda e, n=n, t0=t0, dst=dst: e.activation(out=dst[:, 0, t0:t0 + n], in_=T[0][:, 0:n], func=AF.Copy), reads=[RT[0]], writes=[rdst])
                    if which == 0:
                        kb.op("dve", lambda e, n=n, t0=t0: e.tensor_tensor(out=qcT[:, 0, t0:t0 + n], in0=T[0][:, 0:n], in1=crq[:, t0:t0 + n], op=ALU.mult), reads=[RT[0], Rcrq], writes=[RqcT])
                    kb.op("dve", lambda e, b1=b1, n=n, sinv=sinv: e.tensor_tensor(out=T[1][:, 0:n], in0=pbank[b1][:, 0:n], in1=sinv, op=ALU.mult), reads=[Rp[b1], Rc], writes=[RT[1]])
                    kb.op("dve", lambda e, b2=b2, n=n, cosv=cosv: e.tensor_tensor(out=T[2][:, 0:n], in0=pbank[b2][:, 0:n], in1=cosv, op=ALU.mult), reads=[Rp[b2], Rc], writes=[RT[2]])
                    kb.op("dve", lambda e, n=n: e.tensor_tensor(out=T[1][:, 0:n], in0=T[1][:, 0:n], in1=T[2][:, 0:n], op=ALU.add), reads=[RT[1], RT[2]], writes=[RT[1]])
                    kb.op("act", lambda e, n=n, t0=t0, dst=dst: e.activation(out=dst[:, 1, t0:t0 + n], in_=T[1][:, 0:n], func=AF.Copy), reads=[RT[1]], writes=[rdst])
                    if which == 0:
                        kb.op("dve", lambda e, n=n, t0=t0: e.tensor_tensor(out=qcT[:, 1, t0:t0 + n], in0=T[1][:, 0:n], in1=crq[:, t0:t0 + n], op=ALU.mult), reads=[RT[1], Rcrq], writes=[RqcT])
            sv = [wd16(4096 + h * 512 + hv * 256) for hv in range(2)]
            for i_, (r0, n) in enumerate(ctiles(0, NTOK, 128)):
                b = nbank()
                for hv in range(2):
                    s_, wv_ = sv[hv]
                    kb.group("pe", [(lambda e, c=c, b=b, hv=hv, r0=r0, n=n, wv_=wv_: e.matmul(pbank[b][0:n, hv * 256:(hv + 1) * 256], hT[:, c, P0 + r0:P0 + r0 + n], wv_[:, c, :], start=(c == 0), stop=(c == DC - 1)))
                                    for c in range(DC)], reads=[Rw[s_], Rh], writes=[Rp[b]])
                kb.op("act", lambda e, b=b, i_=i_, n=n: e.activation(out=vtok[0:n, i_, :], in_=pbank[b][0:n, :], func=AF.Copy), reads=[Rp[b]], writes=[Rvt])
            for i_, (r0, n) in enumerate(ctiles(0, NTOK, 128)):
                b = nbank()
                kb.group("pe", [(lambda e, ch=ch, b=b, r0=r0, n=n: e.matmul(pbank[b][0:n, ch * 128:(ch + 1) * 128], kT[:, ch, r0:r0 + n], identb[:, :], start=True, stop=True)) for ch in range(2)],
                         reads=[RkT, Rc], writes=[Rp[b]])
                if i_ < 8:
                    kb.op("act", lambda e, b=b, i_=i_, h=h: e.activation(out=kdtok[:, i_, :], in_=pbank[b][:, 0:256], func=AF.Copy, scale=kdt[:, h:h + 1]), reads=[Rp[b], Rc], writes=[Rkd])
                else:
                    kb.op("act", lambda e, b=b: e.activation(out=ktok_s[0:NS, :], in_=pbank[b][0:NS, 0:256], func=AF.Copy), reads=[Rp[b]], writes=[Rkts])

            def chain_step(j, Sdst, Rdst):
                bs = [nbank(), nbank()]
                for ch in range(2):
                    kb.op("pe", lambda e, ch=ch, j=j, bs=bs: e.matmul(pbank[bs[ch]][:, :], kdtok[:, j, ch * 128:(ch + 1) * 128], vtok[:, j, :], start=True, stop=True), reads=[Rkd, Rvt], writes=[Rp[bs[ch]]])
                    kb.op("dve", lambda e, ch=ch, bs=bs, g128=g128: e.scalar_tensor_tensor(out=Sdst[:, ch, :], in0=Sdst[:, ch, :], scalar=g128, in1=pbank[bs[ch]][:, :], op0=ALU.mult, op1=ALU.add),
                          reads=[Rdst, Rp[bs[ch]]], writes=[Rdst])

            kb.op("dve", lambda e: e.memset(Sst[:, :, :], 0.0), writes=[RSst])
            for j in range(8):
                chain_step(j, Sst, RSst)
            Rcci, Rcco = res(f"ccin{h}"), res(f"ccout{h}")
            kb.dma("sp", cc_in[h].ap().rearrange("(c p) v -> p c v", p=128), Sst[:, :, :], reads=[RSst], writes=[Rcci])
            kb.custom("pool", lambda e, h=h: e.collective_compute("AllGather", ALU.bypass, replica_groups=PAIRS, ins=[cc_in[h].ap().opt()], outs=[cc_out[h].ap().opt()]),
                      ccsem, reads=[Rcci], writes=[Rcco])
            gate_items = []
            gslots = [wd16(8192 + h * 512 + hv * 256) for hv in range(2)]
            for hv in range(2):
                for o2 in range(2):
                    for (t0, n) in ctiles(0, NTOK):
                        def item(hv=hv, o2=o2, t0=t0, n=n):
                            s_, wv_ = gslots[hv]
                            v = 2 * hv + o2
                            b = nbank()
                            kb.group("pe", [(lambda e, c=c, b=b, o2=o2, t0=t0, n=n, wv_=wv_: e.matmul(pbank[b][:, 0:n], wv_[:, c, o2 * 128:(o2 + 1) * 128], hT[:, c, P0 + t0:P0 + t0 + n], start=(c == 0), stop=(c == DC - 1)))
                                            for c in range(DC)], reads=[Rw[s_], Rh], writes=[Rp[b]])
                            kb.op("act", lambda e, b=b, n=n, v=v, t0=t0: e.activation(out=sgT[:, v, t0:t0 + n], in_=pbank[b][:, 0:n], func=AF.Silu), reads=[Rp[b]], writes=[RsgT])
                        gate_items.append(item)
            b = nbank()
            kb.group("pe", [(lambda e, ch=ch, b=b: e.matmul(pbank[b][0:NS, 0:NS], kT[:, ch, NPR:NTOK], qT[:, ch, NPR:NTOK], start=(ch == 0), stop=(ch == 1))) for ch in range(2)],
                     reads=[RkT, RqT], writes=[Rp[b]])
            kb.op("dve", lambda e, b=b, h=h: e.tensor_tensor(out=scms[0:NS, :], in0=pbank[b][0:NS, 0:NS], in1=dmasks[:, h, :], op=ALU.mult), reads=[Rp[b], Rc], writes=[Rscms])
            bo = 6
            psb = pbank[bo][:, :].rearrange("p (a b) -> p a b", a=4)
            kb.group("pe", [(lambda e, v=v, psb=psb: e.matmul(psb[:, v, 0:NS], vtok[0:NS, 8, v * 128:(v + 1) * 128], scms[0:NS, :], start=(v == 0), stop=False)) for v in range(4)],
                     reads=[Rvt, Rscms], writes=[Rp[bo]])
            for bq in range(NSQ):
                si = cnt["ss"] % 3
                cnt["ss"] += 1
                sbi = cnt["sb"] % 2
                cnt["sb"] += 1
                kb.dma("sp", Ssm[si][:, :, :], sret[bq, h].rearrange("(c p) v -> p c v", p=128), writes=[RSsm[si]])
                kb.op("act", lambda e, si=si, sbi=sbi: e.activation(out=Ssb[sbi][:, :, :], in_=Ssm[si][:, :, :], func=AF.Copy), reads=[RSsm[si]], writes=[RSsb[sbi]])
                fns = []
                for v in range(4):
                    for ch in range(2):
                        fns.append(lambda e, v=v, ch=ch, psb=psb, sbi=sbi, bq=bq: e.matmul(psb[:, v, 4 * bq:4 * bq + 4], Ssb[sbi][:, ch, v * 128:(v + 1) * 128], qcT[:, ch, NPR + 4 * bq:NPR + 4 * bq + 4],
                                                                                         start=False, stop=(ch == 1 and bq == NSQ - 1)))
                kb.group("pe", fns, reads=[RSsb[sbi], RqcT], writes=[Rp[bo]])
                ki = cnt["kd"] % 2
                cnt["kd"] += 1
                kb.op("act", lambda e, ki=ki, h=h, bq=bq: e.activation(out=kdsb[ki][0:NS, :], in_=ktok_s[0:NS, :], func=AF.Copy, scale=kds[:, h, bq:bq + 1]), reads=[Rkts, Rc], writes=[Rkdsb[ki]])
                bs = [nbank(), nbank()]
                for ch in range(2):
                    kb.op("pe", lambda e, ch=ch, bs=bs, ki=ki: e.matmul(pbank[bs[ch]][:, :], kdsb[ki][0:NS, ch * 128:(ch + 1) * 128], vtok[0:NS, 8, :], start=True, stop=True),
                          reads=[Rkdsb[ki], Rvt], writes=[Rp[bs[ch]]])
                    kb.op("dve", lambda e, ch=ch, bs=bs, si=si, g4=g4: e.scalar_tensor_tensor(out=Ssm[si][:, ch, :], in0=Ssm[si][:, ch, :], scalar=g4, in1=pbank[bs[ch]][:, :], op0=ALU.mult, op1=ALU.add),
                          reads=[RSsm[si], Rp[bs[ch]]], writes=[RSsm[si]])
                kb.dma("sp", rets_o[bq, h].rearrange("(c p) v -> p c v", p=128), Ssm[si][:, :, :], reads=[RSsm[si]], writes=[res("outdram")])
                if gate_items:
                    gate_items.pop(0)()
            while gate_items:
                gate_items.pop(0)()
            groupnorm(psb, bo, NS, NPR)
            kb.dma("sp", Sin[:, :, :], cc_out[h].ap()[0:DK, :].rearrange("(c p) v -> p c v", p=128), reads=[Rcco], writes=[RSin])
            kb.op("dve", lambda e: e.tensor_scalar(out=Sst[:, :, :], in0=Sin[:, :, :], scalar1=flagt[:, 0:1], scalar2=None, op0=ALU.mult), reads=[RSin, Rc], writes=[RSst])
            for j in range(8):
                tj = 128 * j
                kb.op("act", lambda e: e.activation(out=Sbf[:, :, :], in_=Sst[:, :, :], func=AF.Copy), reads=[RSst], writes=[RSbf])
                b = nbank()
                kb.group("pe", [(lambda e, ch=ch, b=b, tj=tj: e.matmul(pbank[b][:, 0:128], kT[:, ch, tj:tj + 128], qT[:, ch, tj:tj + 128], start=(ch == 0), stop=(ch == 1))) for ch in range(2)],
                         reads=[RkT, RqT], writes=[Rp[b]])
                kb.op("dve", lambda e, b=b, h=h: e.tensor_tensor(out=scm[:, :], in0=pbank[b][:, 0:128], in1=dmask[:, h, :], op=ALU.mult), reads=[Rp[b], Rc], writes=[Rscm])
                bo = nbank()
                psb = pbank[bo][:, :].rearrange("p (a b) -> p a b", a=4)
                fns = []
                for v in range(4):
                    fns.append(lambda e, v=v, psb=psb, j=j: e.matmul(psb[:, v, :], vtok[:, j, v * 128:(v + 1) * 128], scm[:, :], start=True, stop=False))
                    for ch in range(2):
                        fns.append(lambda e, v=v, ch=ch, psb=psb, tj=tj: e.matmul(psb[:, v, :], Sbf[:, ch, v * 128:(v + 1) * 128], qcT[:, ch, tj:tj + 128], start=False, stop=(ch == 1)))
                kb.group("pe", fns, reads=[Rvt, Rscm, RSbf, RqcT], writes=[Rp[bo]])
                chain_step(j, Sst, RSst)
                groupnorm(psb, bo, 128, tj)
            kb.dma("sp", retp_o[h].rearrange("(c p) v -> p c v", p=128), Sst[:, :, :], reads=[RSst], writes=[res("outdram")])
            kb.op("dve", lambda e: e.tensor_tensor(out=ogT[:, :, :], in0=ogT[:, :, :], in1=sgT[:, :, :], op=ALU.mult), reads=[Rog, RsgT], writes=[Rog])
            so = []
            for hv in range(2):
                s_ = wload([(lambda sl: sl[:, :].rearrange("p (k d) -> p k d", k=2), w_out_v[:, 4 * h + 2 * hv:4 * h + 2 * hv + 2, :])])
                so.append((s_, wsl[s_][:, :].rearrange("p (k d) -> p k d", k=2)))
            Ryp = res(f"ypart{h}")
            for o in range(DC):
                for (t0, n) in ctiles(0, NTOK):
                    b = nbank()
                    kb.group("pe", [(lambda e, v=v, b=b, o=o, t0=t0, n=n, so=so: e.matmul(pbank[b][:, 0:n], so[v // 2][1][:, v % 2, o * 128:(o + 1) * 128], ogT[:, v, t0:t0 + n], start=(v == 0), stop=(v == 3)))
                                    for v in range(4)], reads=[Rw[so[0][0]], Rw[so[1][0]], Rog], writes=[Rp[b]])
                    yi = cnt["ys"] % 3
                    cnt["ys"] += 1
                    kb.op("act", lambda e, b=b, n=n, yi=yi: e.activation(out=yst[yi][:, 0:n], in_=pbank[b][:, 0:n], func=AF.Copy), reads=[Rp[b]], writes=[Ryst[yi]])
                    kb.dma("sp", ypart[h][:, o * NTOK + t0:o * NTOK + t0 + n], yst[yi][:, 0:n], reads=[Ryst[yi]], writes=[Ryp])
        kb.barrier()
        kb.dma("sp", xflat, xspill, reads=[Rxsp], writes=[Rx])
        for h in range(NH):
            for o in range(DC):
                for (t0, n) in ctiles(0, NTOK):
                    yi = cnt["ys"] % 3
                    cnt["ys"] += 1
                    kb.dma("sp", yst[yi][:, 0:n], ypart[h][:, o * NTOK + t0:o * NTOK + t0 + n], reads=[res(f"ypart{h}")], writes=[Ryst[yi]])
                    kb.op("dve", lambda e, o=o, t0=t0, n=n, yi=yi: e.tensor_tensor(out=xT[:, o, P0 + t0:P0 + t0 + n], in0=xT[:, o, P0 + t0:P0 + t0 + n], in1=yst[yi][:, 0:n], op=ALU.add),
                          reads=[Rx, Ryst[yi]], writes=[Rx])
        hx = carve(20480, (128, DC, 2), F32)
        Rhx, Rcxi, Rcxo = res("hx"), res("cxin"), res("cxout")
        cxi = nc.dram_tensor("cxi", [128, 2 * DC], F32)
        cxo = nc.dram_tensor("cxo", [256, 2 * DC], F32)
        kb.op("act", lambda e: e.activation(out=hx[:, :, :], in_=xT[:, :, S0 - 2:S0], func=AF.Copy), reads=[Rx], writes=[Rhx])
        kb.dma("sp", cxi.ap(), hx[:, :, :].rearrange("p a b -> p (a b)"), reads=[Rhx], writes=[Rcxi])
        kb.custom("pool", lambda e: e.collective_compute("AllGather", ALU.bypass, replica_groups=PAIRS, ins=[cxi.ap().opt()], outs=[cxo.ap().opt()]), ccsem, reads=[Rcxi], writes=[Rcxo])
        kb.dma("sp", hx[:, :, :].rearrange("p a b -> p (a b)"), cxo.ap()[0:128, :], reads=[Rcxo], writes=[Rhx])
        kb.op("dve", lambda e: e.tensor_scalar(out=xT[:, :, 15:17], in0=hx[:, :, :], scalar1=flagt[:, 0:1], scalar2=None, op0=ALU.mult), reads=[Rhx, Rc, Rx], writes=[Rx])
        kb.barrier()
        conv_ffn(1)
        ple(1)

        ta2 = carve(16384, (128, NX), F32)
        yT = carve(20864, (128, DC, 128), F32)
        ostg_y = carve(29056, (128, D), F32)
        Rta2, RyT, Rosy = res("ta2"), res("yT"), res("ostgy")
        sqv2 = carve(0, (128, DC, 512), BF16)
        sumsq_rstd(ta2, Rta2, P0, NX, sqv2, Rsq5, 512)
        for (r0, n) in ctiles(0, NTOK, 128):
            for c in range(DC):
                kb.op("dve", lambda e, c=c, r0=r0, n=n: e.scalar_tensor_tensor(out=yT[:, c, 0:n], in0=xT[:, c, P0 + r0:P0 + r0 + n], scalar=gamT[:, 96 + c:97 + c], in1=ta2[:, P0 + r0:P0 + r0 + n],
                                                                              op0=ALU.mult, op1=ALU.mult), reads=[Rx, Rta2, Rc], writes=[RyT])
            tr_out(lambda c, n=n: yT[:, c, 0:n], RyT, n, DC, ostg_y, Rosy, y_o[r0:r0 + n, :])
        kb.wait_all("sp", [res("outdram")])
        kb.barrier()
        with nc.Block() as block:
            kb.replay(block)
    return nc


def _tables(half):
    log_g = np.log1p(-(2.0 ** (-5.0 - np.arange(NH, dtype=np.float64))))
    halfd = DK // 2
    inv = 10000.0 ** (-np.arange(halfd, dtype=np.float64) / halfd)
    pos = np.concatenate([half * NPR + np.arange(NPR), np.tile(16384 + np.arange(4), NSQ)]).astype(np.float64)
    ang = inv[:, None] * pos[None, :]
    cs = np.stack([np.cos(ang), np.sin(ang)], axis=1).astype(np.float32)
    n_in = np.concatenate([np.arange(NPR) % 128, np.tile(np.arange(4), NSQ)]).astype(np.float64)
    crossq = np.exp(log_g[:, None] * (n_in[None, :] + 1.0)).astype(np.float32)
    m = np.arange(128)
    dm = np.where(m[None, :, None] <= m[None, None, :], np.exp(log_g[:, None, None] * np.maximum(m[None, None, :] - m[None, :, None], 0)), 0.0) * DK ** -0.5
    dmask = np.ascontiguousarray(dm.transpose(1, 0, 2)).astype(np.float32)
    ms = np.arange(64)
    same = (ms[:, None] // 4) == (ms[None, :] // 4)
    dms = np.where(same[None] & (ms[None, :, None] <= ms[None, None, :]), np.exp(log_g[:, None, None] * np.maximum(ms[None, None, :] - ms[None, :, None], 0)), 0.0) * DK ** -0.5
    dmasks = np.ascontiguousarray(dms.transpose(1, 0, 2)).astype(np.float32)
    kdt = (np.exp(log_g[None, :] * (127.0 - m[:, None])) * DK ** -0.5).astype(np.float32)
    kds = np.zeros((64, NH, NSQ), np.float32)
    for t in range(64):
        kds[t, :, t // 4] = np.exp(log_g * (3.0 - (t % 4))) * DK ** -0.5
    invc = np.zeros((4, 15), np.float32)
    for gi, w in enumerate((2, 4, 8, 16)):
        p = np.arange(15)
        invc[gi] = 1.0 / np.minimum(p + 1, w) if half == 0 else 1.0 / w
    flagv = np.concatenate([[float(half)], float(half) * np.exp(log_g * 1024.0)]).astype(np.float32)[None, :]
    return dict(cs=cs, crossq=crossq, dmask=dmask, dmasks=dmasks, kdt=kdt, kds=kds, invc=invc.reshape(1, 60), flagv=flagv)


_NC_CACHE = {}


def kernel(x_prompt, x_sample, p_prompt, p_sample, state_pool, state_ret, state_conv,
           norm_mix, norm_ffn, norm_ple, norm_final, pool_w, pool_scale, ret_w_in, ret_w_out,
           ffn_w_up, ffn_conv_w, ffn_conv_b, ffn_w_down, ple_w_proj, ple_w_gate):
    f32 = lambda a: np.ascontiguousarray(np.asarray(a, dtype=np.float32))
    x_prompt, x_sample, p_prompt, p_sample = map(f32, (x_prompt, x_sample, p_prompt, p_sample))
    state_pool, state_ret, state_conv = map(f32, (state_pool, state_ret, state_conv))
    if "nc" not in _NC_CACHE:
        _NC_CACHE["nc"] = build_program()
    nc = _NC_CACHE["nc"]
    small = np.concatenate([f32(norm_mix).reshape(32, 128), f32(norm_ffn).reshape(32, 128), f32(norm_ple).reshape(32, 128),
                            f32(norm_final).reshape(16, 128), f32(pool_scale).reshape(16, 128)], axis=0)
    convp = np.zeros((768, 128), np.float32)
    convp[0:528] = f32(ffn_conv_w).reshape(528, 128)
    convp[528:704] = f32(ffn_conv_b).reshape(176, 128)
    shared = dict(small=small, convp=convp, identd=np.eye(128, dtype=np.float32), pool_w=f32(pool_w)[0],
                  ret_w_in=f32(ret_w_in)[0], ret_w_out=f32(ret_w_out)[0], ffn_w_up=f32(ffn_w_up), ffn_w_down=f32(ffn_w_down),
                  ple_w_proj=f32(ple_w_proj), ple_w_gate=f32(ple_w_gate))
    in_maps = []
    for core in range(8):
        s, half = core // 2, core % 2
        halo = x_prompt[s, NPR - HALO:NPR] if half else np.zeros((HALO, D), np.float32)
        sl = slice(NSQ * core, NSQ * (core + 1))
        xin = np.concatenate([halo, x_prompt[s, half * NPR:(half + 1) * NPR], x_sample[sl].reshape(NS, D)], axis=0)
        pin = np.concatenate([p_prompt[:, s, half * NPR:(half + 1) * NPR], p_sample[:, sl].reshape(2, NS, 256)], axis=1)
        m = dict(shared)
        m.update(xin=np.ascontiguousarray(xin), pin=np.ascontiguousarray(pin), spool=state_pool[0, sl].reshape(240, D),
                 sret=state_ret[0, sl], sconv=np.ascontiguousarray(state_conv[:, sl].reshape(2, 32, F2)))
        m.update(_tables(half))
        in_maps.append(m)
    res = run_bass_kernel_spmd(nc, in_maps, core_ids=list(range(8)))
    r = res.results
    if DEBUG:
        _NC_CACHE["dbg"] = r
    y_prompt = np.stack([np.concatenate([r[2 * s]["y"][0:NPR], r[2 * s + 1]["y"][0:NPR]], axis=0) for s in range(4)])
    y_sample = np.concatenate([r[c]["y"][NPR:].reshape(NSQ, 4, D) for c in range(8)], axis=0)
    pool_p = np.stack([r[2 * s + 1]["pool_p"] for s in range(4)])[None]
    pool_s = np.concatenate([r[c]["pool_s"].reshape(NSQ, 15, D) for c in range(8)], axis=0)[None]
    ret_p = np.stack([r[2 * s + 1]["ret_p"] for s in range(4)])[None]
    ret_s = np.concatenate([r[c]["ret_s"] for c in range(8)], axis=0)[None]
    conv_p = np.stack([r[2 * s + 1]["conv_p"] for s in range(4)], axis=1)
    conv_s = np.concatenate([r[c]["conv_s"].reshape(2, NSQ, 2, F2) for c in range(8)], axis=1)
    return (y_prompt, y_sample, pool_p, pool_s, ret_p, ret_s, conv_p, conv_s)
```

```python
import contextlib
import numpy as np
import concourse.bass as bass
import concourse.mybir as mybir
from concourse.bass_utils import run_bass_kernel_spmd

F32 = mybir.dt.float32
BF16 = mybir.dt.bfloat16
AF = mybir.ActivationFunctionType
ALU = mybir.AluOpType

D = 2048
DC = 16
FF = 5632
F2 = 11264
NH = 8
DK = 256
DV = 512
NPR = 1024
NSQ = 16
NS = 64
HALO = 17
P0 = HALO
S0 = P0 + NPR
NX = S0 + NS
NTOK = NPR + NS
EPS = 1e-6
DEBUG = False
ENGS = ("pe", "act", "dve", "pool", "sp")
PAIRS = [[0, 1], [2, 3], [4, 5], [6, 7]]


class Res:
    __slots__ = ("name", "w", "r")

    def __init__(self, name):
        self.name = name
        self.w = None
        self.r = []


class KB:
    def __init__(self, nc, stack, n_sp=24, n_pool=8):
        self.nc = nc
        self.ops = {e: [] for e in ENGS}
        self.sem = {}
        self.cnt = {}
        for e in ("pe", "act", "dve", "pool"):
            self.sem[e] = stack.enter_context(nc.semaphore("prog_" + e))
            self.cnt[e] = 0
        self.dsems = {}
        for q, n in (("sp", n_sp), ("pool", n_pool)):
            self.dsems[q] = [[stack.enter_context(nc.semaphore(f"d_{q}_{i}")), 0] for i in range(n)]
        self.dnext = {"sp": 0, "pool": 0}
        self.waited = {e: {} for e in ENGS}
        self.semobj = {}
        self.csems = []

    def _collect(self, eng, reads, writes):
        need = {}

        def add(tok, same_ok):
            if tok is None:
                return
            key, val = tok
            if same_ok and key == eng:
                return
            if need.get(key, 0) < val:
                need[key] = val
        for r in reads:
            add(r.w, False)
        for w in writes:
            add(w.w, True)
            for t in w.r:
                add(t, True)
        out = []
        for key, val in need.items():
            if self.waited[eng].get(key, 0) >= val:
                continue
            self.waited[eng][key] = val
            out.append((key, val))
        return out

    def _emit_waits(self, eng, waits):
        for key, val in waits:
            semh = self.sem[key] if isinstance(key, str) else self.semobj[key]
            self.ops[eng].append(("wait", semh, val))

    def _commit(self, tok, reads, writes):
        for r in reads:
            r.r.append(tok)
        for w in writes:
            w.w = tok
            w.r = []

    def op(self, eng, fn, reads=(), writes=()):
        self._emit_waits(eng, self._collect(eng, reads, writes))
        self.cnt[eng] += 1
        tok = (eng, self.cnt[eng])
        self.ops[eng].append(("ins", fn, self.sem[eng], 1))
        self._commit(tok, reads, writes)
        return tok

    def group(self, eng, fns, reads=(), writes=()):
        self._emit_waits(eng, self._collect(eng, reads, writes))
        for fn in fns[:-1]:
            self.ops[eng].append(("ins", fn, None, 0))
        self.cnt[eng] += 1
        tok = (eng, self.cnt[eng])
        self.ops[eng].append(("ins", fns[-1], self.sem[eng], 1))
        self._commit(tok, reads, writes)
        return tok

    def dma(self, q, out, in_, reads=(), writes=()):
        ring = self.dsems[q]
        i = self.dnext[q]
        self.dnext[q] = (i + 1) % len(ring)
        ent = ring[i]
        semh = ent[0]
        key = ("d", q, i)
        self.semobj[key] = semh
        if ent[1] > 0 and self.waited[q].get(key, 0) < ent[1]:
            self.waited[q][key] = ent[1]
            self.ops[q].append(("wait", semh, ent[1]))
        self._emit_waits(q, self._collect(q, reads, writes))
        ent[1] += 16
        tok = (key, ent[1])
        self.ops[q].append(("dma", out, in_, semh))
        self._commit(tok, reads, writes)
        return tok

    def custom(self, q, fn, cs, reads=(), writes=()):
        self._emit_waits(q, self._collect(q, reads, writes))
        key = ("c", id(cs))
        self.semobj[key] = cs[0]
        cs[1] += 1
        tok = (key, cs[1])
        self.ops[q].append(("ins", fn, cs[0], 1))
        self._commit(tok, reads, writes)
        return tok

    def barrier(self):
        toks = [(e, self.cnt[e]) for e in ("pe", "act", "dve", "pool") if self.cnt[e] > 0]
        for q, ring in self.dsems.items():
            for i, ent in enumerate(ring):
                if ent[1] > 0:
                    key = ("d", q, i)
                    self.semobj[key] = ent[0]
                    toks.append((key, ent[1]))
        for e in ENGS:
            for key, val in toks:
                if key == e:
                    continue
                if self.waited[e].get(key, 0) >= val:
                    continue
                self.waited[e][key] = val
                semh = self.sem[key] if isinstance(key, str) else self.semobj[key]
                self.ops[e].append(("wait", semh, val))

    def wait_all(self, q, resources):
        self._emit_waits(q, self._collect(q, resources, resources))

    def replay(self, block):
        names = {"pe": "tensor", "act": "scalar", "dve": "vector", "pool": "gpsimd", "sp": "sync"}

        def make(e):
            def run(eng):
                for item in self.ops[e]:
                    k = item[0]
                    if k == "wait":
                        eng.wait_ge(item[1], item[2])
                    elif k == "ins":
                        ins = item[1](eng)
                        if item[2] is not None:
                            ins.then_inc(item[2], item[3])
                    else:
                        eng.dma_start(out=item[1], in_=item[2]).then_inc(item[3], 16)
            return run
        for e in ENGS:
            getattr(block, names[e])(make(e))


def ctiles(c0, c1, step=512):
    out = []
    if step == 512:
        nt = -(-(c1 - c0) // step)
        base, rem = divmod(c1 - c0, nt)
        for i in range(nt):
            n = base + (1 if i < rem else 0)
            out.append((c0, n))
            c0 += n
        return out
    while c0 < c1:
        n = min(step, c1 - c0)
        out.append((c0, n))
        c0 += n
    return out


def build_program():
    nc = bass.Bass("TRN2", target_bir_lowering=False)

    def din(name, shape):
        return nc.dram_tensor(name, list(shape), F32, kind="ExternalInput").ap()

    def dout(name, shape):
        return nc.dram_tensor(name, list(shape), F32, kind="ExternalOutput").ap()

    xin = din("xin", (NX, D))
    pin = din("pin", (2, NTOK, 256))
    spool = din("spool", (240, D))
    sret = din("sret", (NSQ, NH, DK, DV))
    sconv = din("sconv", (2, 32, F2))
    small = din("small", (128, 128))
    convp = din("convp", (768, 128))
    identd = din("identd", (128, 128))
    pool_w = din("pool_w", (4, 512, 512))
    w_in = din("ret_w_in", (D, 12288))
    w_out = din("ret_w_out", (4096, D))
    w_up = din("ffn_w_up", (2, D, F2))
    w_down = din("ffn_w_down", (2, FF, D))
    w_proj = din("ple_w_proj", (2, 256, D))
    w_gate = din("ple_w_gate", (2, D, D))
    cs_d = din("cs", (128, 2, NTOK))
    crossq_d = din("crossq", (NH, NTOK))
    dmask_d = din("dmask", (128, NH, 128))
    dmasks_d = din("dmasks", (64, NH, 64))
    kdt_d = din("kdt", (128, NH))
    kds_d = din("kds", (64, NH, NSQ))
    invc_d = din("invc", (1, 60))
    flag_d = din("flagv", (1, 1 + NH))

    y_o = dout("y", (NTOK, D))
    poolp_o = dout("pool_p", (15, D))
    pools_o = dout("pool_s", (240, D))
    retp_o = dout("ret_p", (NH, DK, DV))
    rets_o = dout("ret_s", (NSQ, NH, DK, DV))
    convp_o = dout("conv_p", (2, 2, F2))
    convs_o = dout("conv_s", (2, 32, F2))

    xspill = nc.dram_tensor("xspill", [128, DC * NX], F32).ap()
    cc_in = [nc.dram_tensor(f"cc_in{h}", [DK, DV], F32) for h in range(NH)]
    cc_out = [nc.dram_tensor(f"cc_out{h}", [2 * DK, DV], F32) for h in range(NH)]
    cx_in = nc.dram_tensor("cx_in", [2, D], F32)
    cx_out = nc.dram_tensor("cx_out", [4, D], F32)

    with contextlib.ExitStack() as st:
        E = st.enter_context
        kb = KB(nc, st)
        ccsem = [E(nc.semaphore("ccsem")), 0]

        def sb(name, shape, dt):
            return E(nc.sbuf_tensor("sb_" + name, list(shape), dt))

        xT = sb("xT", (128, DC, NX), F32)
        hT = sb("hT", (128, DC, NX), BF16)
        NWS = 4
        wsl = [sb(f"wsl{i}", (128, 4096), BF16) for i in range(NWS)]
        gamT = sb("gamT", (128, 128), F32)
        convT = sb("convT", (128, 768), F32)
        identf = sb("identf", (128, 128), F32)
        identb = sb("identb", (128, 128), BF16)
        onesb = sb("onesb", (128, 128), BF16)
        epst = sb("epst", (128, 1), F32)
        flagt = sb("flagt", (128, 1 + NH), F32)
        cst = sb("cst", (128, 2, NTOK), F32)
        dmask = sb("dmask", (128, NH, 128), F32)
        dmasks = sb("dmasks", (64, NH, 64), F32)
        kdt = sb("kdt", (128, NH), F32)
        kds = sb("kds", (64, NH, NSQ), F32)
        invc = sb("invc", (128, 60), F32)
        rstd = sb("rstd", (128, 512), F32)
        SCRW = 12544
        scr = sb("scr", (128, SCRW), F32)
        pbank = [E(nc.psum_tensor(f"pb{i}", [128, 512], F32)) for i in range(8)]

        R = {}

        def res(name):
            if name not in R:
                R[name] = Res(name)
            return R[name]

        Rx, Rh = res("xT"), res("hT")
        Rw = [res(f"w{i}") for i in range(NWS)]
        Rp = [res(f"pb{i}") for i in range(8)]
        Rc = res("consts")
        Rrstd = res("rstd")
        Rscr = {}

        def carve(off_b, shape, dt, base=None):
            t = scr if base is None else base
            n = int(np.prod(shape[1:]))
            nb = n * (4 if dt == F32 else 2)
            assert off_b % 4 == 0 and nb % 4 == 0
            ap = t[:, off_b // 4: off_b // 4 + nb // 4]
            if dt != F32:
                ap = ap.bitcast(dt)
            if len(shape) == 3:
                ap = ap.rearrange("p (a b) -> p a b", a=shape[1])
            elif len(shape) == 4:
                ap = ap.rearrange("p (a b c) -> p a b c", a=shape[1], b=shape[2])
            return ap[0:shape[0]]

        state = {"bank": 0, "ws": 0}

        def nbank(lo=0, hi=6):
            b = lo + state["bank"] % (hi - lo)
            state["bank"] += 1
            return b

        def wslot():
            s = state["ws"] % NWS
            state["ws"] += 1
            return s

        kb.dma("sp", identf[:], identd, writes=[Rc])
        kb.dma("pool", identb[:], identd, writes=[Rc])
        kb.dma("sp", cst[:], cs_d, writes=[Rc])
        kb.dma("sp", dmask[:], dmask_d, writes=[Rc])
        kb.dma("sp", dmasks[:], dmasks_d, writes=[Rc])
        kb.dma("sp", kdt[:], kdt_d, writes=[Rc])
        kb.dma("sp", kds[:], kds_d, writes=[Rc])
        kb.dma("sp", invc[:], invc_d.partition_broadcast(128), writes=[Rc])
        kb.dma("sp", flagt[:], flag_d.partition_broadcast(128), writes=[Rc])
        kb.op("dve", lambda e: e.memset(onesb[:], 1.0), writes=[Rc])
        kb.op("dve", lambda e: e.memset(epst[:], EPS), writes=[Rc])

        stg = [carve(0, (128, D), F32), carve(8192, (128, D), F32)]
        Rstg = [res("stg0"), res("stg1")]
        stgi = {"i": 0}

        def tr_in(src_rows, nrows, dst_fn, rdst, width=D, evac="act"):
            i = stgi["i"] % 2
            stgi["i"] += 1
            kb.dma("sp", stg[i][0:nrows, 0:width], src_rows, writes=[Rstg[i]])
            for c4 in range(0, width // 128, 4):
                nch = min(4, width // 128 - c4)
                b = nbank()
                pv = pbank[b][:, :].rearrange("p (a b) -> p a b", a=4)
                kb.group("pe", [
                    (lambda e, k=k, c4=c4, i=i, pv=pv: e.transpose(pv[:, k, 0:nrows], stg[i][0:nrows, (c4 + k) * 128:(c4 + k + 1) * 128], identf[0:nrows, 0:nrows]))
                    for k in range(nch)], reads=[Rstg[i], Rc], writes=[Rp[b]])
                dst = dst_fn(c4, nch)
                if evac == "act":
                    kb.op("act", lambda e, dst=dst, pv=pv, nch=nch: e.activation(out=dst, in_=pv[:, 0:nch, 0:nrows], func=AF.Copy), reads=[Rp[b]], writes=[rdst])
                else:
                    kb.op("dve", lambda e, dst=dst, pv=pv, nch=nch: e.tensor_copy(out=dst, in_=pv[:, 0:nch, 0:nrows]), reads=[Rp[b]], writes=[rdst])

        def tr_out(src_fn, rsrc, ncols, nch_total, ostg, rostg, dst_rows):
            for c4 in range(0, nch_total, 4):
                nch = min(4, nch_total - c4)
                b = nbank()
                kb.group("pe", [
                    (lambda e, k=k, c4=c4, b=b: e.transpose(pbank[b][0:ncols, k * 128:(k + 1) * 128], src_fn(c4 + k), identf[:, :]))
                    for k in range(nch)], reads=[rsrc, Rc], writes=[Rp[b]])
                kb.op("act", lambda e, c4=c4, nch=nch, b=b: e.activation(out=ostg[0:ncols, c4 * 128:(c4 + nch) * 128], in_=pbank[b][0:ncols, 0:nch * 128], func=AF.Copy),
                      reads=[Rp[b]], writes=[rostg])
            kb.dma("sp", dst_rows, ostg[0:ncols, 0:nch_total * 128], reads=[rostg], writes=[res("outdram")])

        def wload(pieces):
            s = wslot()
            for dst_fn, src in pieces:
                kb.dma("pool", dst_fn(wsl[s]), src, writes=[Rw[s]])
            return s

        def wview_d(s, ncols):
            return wsl[s][:, 0:16 * ncols].rearrange("p (c f) -> p c f", c=16)

        def rmsnorm(gcol, c0, c1, sqv, rsq):
            for (t0, n) in ctiles(c0, c1):
                kb.op("act", lambda e, t0=t0, n=n: e.activation(out=sqv[:, :, 0:n], in_=xT[:, :, t0:t0 + n], func=AF.Square, scale=float(D) ** -0.5),
                      reads=[Rx], writes=[rsq])
                b = nbank()
                kb.group("pe", [(lambda e, c=c, n=n, b=b: e.matmul(pbank[b][:, 0:n], onesb[:, :], sqv[:, c, 0:n], start=(c == 0), stop=(c == DC - 1))) for c in range(DC)],
                         reads=[rsq, Rc], writes=[Rp[b]])
                kb.op("act", lambda e, n=n, b=b: e.activation(out=rstd[:, 0:n], in_=pbank[b][:, 0:n], func=AF.Sqrt, bias=epst[:, 0:1]),
                      reads=[Rp[b], Rc], writes=[Rrstd])
                kb.op("dve", lambda e, n=n: e.reciprocal(out=rstd[:, 0:n], in_=rstd[:, 0:n]), reads=[Rrstd], writes=[Rrstd])
                for c in range(DC):
                    kb.op("dve", lambda e, c=c, t0=t0, n=n: e.scalar_tensor_tensor(out=hT[:, c, t0:t0 + n], in0=xT[:, c, t0:t0 + n], scalar=gamT[:, gcol + c:gcol + c + 1],
                                                                                  in1=rstd[:, 0:n], op0=ALU.mult, op1=ALU.mult),
                          reads=[Rx, Rrstd, Rc], writes=[Rh])

        for i in range(1):
            kb.dma("sp", stg[0][:, 0:128], small, writes=[Rstg[0]])
            b = nbank()
            kb.op("pe", lambda e, b=b: e.transpose(pbank[b][:, 0:128], stg[0][:, 0:128], identf[:, :]), reads=[Rstg[0], Rc], writes=[Rp[b]])
            kb.op("act", lambda e, b=b: e.activation(out=gamT[:, :], in_=pbank[b][:, 0:128], func=AF.Copy), reads=[Rp[b]], writes=[Rc])
            for j in range(6):
                kb.dma("sp", stg[1][:, 0:128], convp[j * 128:(j + 1) * 128, :], writes=[Rstg[1]])
                b = nbank()
                kb.op("pe", lambda e, b=b: e.transpose(pbank[b][:, 0:128], stg[1][:, 0:128], identf[:, :]), reads=[Rstg[1], Rc], writes=[Rp[b]])
                kb.op("act", lambda e, b=b, j=j: e.activation(out=convT[:, j * 128:(j + 1) * 128], in_=pbank[b][:, 0:128], func=AF.Copy), reads=[Rp[b]], writes=[Rc])
        for (r0, n) in ctiles(0, NX, 128):
            tr_in(xin[r0:r0 + n, :], n, lambda c4, nch, r0=r0, n=n: xT[:, c4:c4 + nch, r0:r0 + n], Rx)
        kb.barrier()

        def tr_in2(stg_ap, rstg, src_rows, nrows, width, copies):
            kb.dma("sp", stg_ap[0:nrows, 0:width], src_rows, writes=[rstg])
            nch = width // 128
            b = nbank()
            pv = pbank[b][:, :].rearrange("p (a b) -> p a b", a=4)
            kb.group("pe", [(lambda e, k=k, pv=pv: e.transpose(pv[:, k, 0:nrows], stg_ap[0:nrows, k * 128:(k + 1) * 128], identf[0:nrows, 0:nrows])) for k in range(nch)],
                     reads=[rstg, Rc], writes=[Rp[b]])
            copies(pv, b)

        def sumsq_rstd(dst, rdst, c0, c1, sqv, rsq, step):
            for (t0, n) in ctiles(c0, c1, step):
                kb.op("act", lambda e, t0=t0, n=n: e.activation(out=sqv[:, :, 0:n], in_=xT[:, :, t0:t0 + n], func=AF.Square, scale=float(D) ** -0.5), reads=[Rx], writes=[rsq])
                b = nbank()
                kb.group("pe", [(lambda e, c=c, n=n, b=b: e.matmul(pbank[b][:, 0:n], onesb[:, :], sqv[:, c, 0:n], start=(c == 0), stop=(c == DC - 1))) for c in range(DC)],
                         reads=[rsq, Rc], writes=[Rp[b]])
                kb.op("act", lambda e, t0=t0, n=n, b=b: e.activation(out=dst[:, t0:t0 + n], in_=pbank[b][:, 0:n], func=AF.Sqrt, bias=epst[:, 0:1]), reads=[Rp[b], Rc], writes=[rdst])
            kb.op("dve", lambda e: e.reciprocal(out=dst[:, c0:c1], in_=dst[:, c0:c1]), reads=[rdst], writes=[rdst])

        POOLW = (2, 4, 8, 16)
        L = S0
        ta = carve(0, (128, NX), F32)
        hf = carve(4480, (128, NX), F32)
        tb = carve(8960, (128, NX), F32)
        pp = carve(13440, (128, NX), F32)
        hs_g = carve(17920, (128, 4, NSQ, 19), F32)
        hst = [carve(22784, (128, NSQ, 19), F32), carve(24000, (128, NSQ, 19), F32)]
        fixt = carve(25216, (128, 16), F32)
        pend_g = carve(25280, (128, 4, 15), F32)
        hcmp_g = carve(25536, (128, 4, 120), F32)
        stg2 = [carve(27456, (128, 512), F32), carve(29504, (128, 512), F32)]
        ostg_g = carve(31552, (128, 512), F32)
        sqv = carve(33600, (128, DC, 64), BF16)
        Rta, Rhf, Rtb, Rpp, Rhsg, Rhst, Rfix, Rpend, Rhcmp, Rostg, Rsq = [res(n) for n in "ta hf tb pp hsg hst fix pend hcmp ostg sq".split()]
        Rstg2 = [res("stg2a"), res("stg2b")]
        sumsq_rstd(ta, Rta, 0, NX, sqv, Rsq, 64)
        for g in range(4):
            win = POOLW[g]
            steps = {2: 0, 4: 1, 8: 2, 16: 3}[win]
            for half in range(2):
                def cp(pv, b, half=half):
                    for k in range(4):
                        kb.op("act", lambda e, k=k, pv=pv: e.activation(out=hs_g[:, k, half * 8:(half + 1) * 8, 0:15], in_=pv[:, k, 0:120].rearrange("p (b c) -> p b c", b=8), func=AF.Copy),
                              reads=[Rp[b]], writes=[Rhsg])
                tr_in2(stg2[half], Rstg2[half], spool[half * 120:(half + 1) * 120, g * 512:(g + 1) * 512], 120, 512, cp)
            for k in range(4):
                c = 4 * g + k
                kb.op("dve", lambda e, c=c: e.scalar_tensor_tensor(out=hf[:, :], in0=xT[:, c, :], scalar=gamT[:, c:c + 1], in1=ta[:, :], op0=ALU.mult, op1=ALU.mult),
                      reads=[Rx, Rta, Rc], writes=[Rhf])
                kb.op("act", lambda e, k=k: e.activation(out=hs_g[:, k, :, 15:19], in_=hf[:, S0:NX].rearrange("p (b t) -> p b t", b=NSQ), func=AF.Copy), reads=[Rhf], writes=[Rhsg])
                kb.op("act", lambda e, k=k: e.activation(out=pend_g[:, k, :], in_=hf[:, S0 - 15:S0], func=AF.Copy), reads=[Rhf], writes=[Rpend])
                kb.op("dve", lambda e: e.tensor_tensor(out=tb[:, 1:L], in0=hf[:, 1:L], in1=hf[:, 0:L - 1], op=ALU.add), reads=[Rhf], writes=[Rtb])
                src, rsrc = tb, Rtb
                sh = 2
                for _ in range(steps):
                    dst, rdst = (pp, Rpp) if src is tb else (tb, Rtb)
                    kb.op("dve", lambda e, src=src, dst=dst, sh=sh: e.tensor_tensor(out=dst[:, 2 * sh - 1:L], in0=src[:, 2 * sh - 1:L], in1=src[:, sh - 1:L - sh], op=ALU.add),
                          reads=[rsrc], writes=[rdst])
                    src, rsrc = dst, rdst
                    sh *= 2
                kb.op("dve", lambda e, c=c, src=src, win=win: e.scalar_tensor_tensor(out=hT[:, c, 15:L], in0=src[:, 15:L], scalar=1.0 / win, in1=hf[:, 15:L], op0=ALU.mult, op1=ALU.subtract),
                      reads=[rsrc, Rhf], writes=[Rh])
                kb.op("dve", lambda e, src=src, g=g: e.tensor_tensor(out=fixt[:, 0:15], in0=src[:, P0:P0 + 15], in1=invc[:, g * 15:(g + 1) * 15], op=ALU.mult), reads=[rsrc, Rc], writes=[Rfix])
                kb.op("dve", lambda e, c=c: e.tensor_tensor(out=hT[:, c, P0:P0 + 15], in0=fixt[:, 0:15], in1=hf[:, P0:P0 + 15], op=ALU.subtract), reads=[Rfix, Rhf], writes=[Rh])
                hsrc = hs_g[:, k, :, :]
                kb.op("dve", lambda e, hsrc=hsrc: e.tensor_tensor(out=hst[0][:, :, 1:19], in0=hsrc[:, :, 1:19], in1=hsrc[:, :, 0:18], op=ALU.add), reads=[Rhsg], writes=[Rhst])
                a_i = 0
                sh = 2
                for _ in range(steps):
                    kb.op("dve", lambda e, a=a_i, sh=sh: e.tensor_tensor(out=hst[1 - a][:, :, 2 * sh - 1:19], in0=hst[a][:, :, 2 * sh - 1:19], in1=hst[a][:, :, sh - 1:19 - sh], op=ALU.add),
                          reads=[Rhst], writes=[Rhst])
                    a_i = 1 - a_i
                    sh *= 2
                kb.op("dve", lambda e, c=c, a=a_i, win=win, hsrc=hsrc: e.scalar_tensor_tensor(out=hT[:, c, S0:NX].rearrange("p (b t) -> p b t", b=NSQ), in0=hst[a][:, :, 15:19], scalar=1.0 / win,
                                                                                          in1=hsrc[:, :, 15:19], op0=ALU.mult, op1=ALU.subtract), reads=[Rhst, Rhsg], writes=[Rh])
            tr_out(lambda k: pend_g[:, k, :], Rpend, 15, 4, ostg_g, Rostg, poolp_o[:, g * 512:(g + 1) * 512])
            for half in range(2):
                for k in range(4):
                    kb.op("act", lambda e, k=k, half=half: e.activation(out=hcmp_g[:, k, :].rearrange("p (b t) -> p b t", b=8), in_=hs_g[:, k, half * 8:(half + 1) * 8, 4:19], func=AF.Copy),
                          reads=[Rhsg], writes=[Rhcmp])
                tr_out(lambda k: hcmp_g[:, k, :], Rhcmp, 120, 4, ostg_g, Rostg, pools_o[half * 120:(half + 1) * 120, g * 512:(g + 1) * 512])
            s_ = wload([(lambda sl: sl[:, 0:2048].rearrange("p (c f) -> p c f", c=4), pool_w[g].rearrange("(c p) f -> p c f", p=128))])
            wv = wsl[s_][:, 0:2048].rearrange("p (c f) -> p c f", c=4)
            for o in range(4):
                for (t0, n) in ctiles(15, NX):
                    b = nbank()
                    kb.group("pe", [(lambda e, ci=ci, o=o, t0=t0, n=n, b=b, wv=wv, g=g: e.matmul(pbank[b][:, 0:n], wv[:, ci, o * 128:(o + 1) * 128], hT[:, 4 * g + ci, t0:t0 + n], start=(ci == 0), stop=(ci == 3)))
                                    for ci in range(4)], reads=[Rw[s_], Rh], writes=[Rp[b]])
                    kb.op("dve", lambda e, o=o, t0=t0, n=n, b=b, g=g: e.scalar_tensor_tensor(out=xT[:, 4 * g + o, t0:t0 + n], in0=pbank[b][:, 0:n], scalar=gamT[:, 112 + 4 * g + o:113 + 4 * g + o],
                                                                                         in1=xT[:, 4 * g + o, t0:t0 + n], op0=ALU.mult, op1=ALU.add), reads=[Rp[b], Rx, Rc], writes=[Rx])
        kb.barrier()

        sq512 = carve(0, (128, DC, 512), BF16)
        Rsq5 = res("sq512")

        def conv_ffn(l):
            rmsnorm(32 + l * 16, 15, NX, sq512, Rsq5)
            kb.barrier()
            NU = NX - 15
            ub = [carve(0, (128, NU), F32), carve(4416, (128, NU), F32)]
            cbf = [carve(8832, (128, NTOK), F32), carve(8832 + 4352, (128, NTOK), F32)]
            sgb = carve(17536, (128, NTOK), F32)
            uext = carve(21888, (128, 2, NSQ, 6), F32)
            aT = [carve(22656, (128, 4, NTOK), BF16), carve(22656 + 8704, (128, 4, NTOK), BF16)]
            stg_s = [carve(40064, (128, 512), F32), carve(42112, (128, 512), F32)]
            uh_g = carve(44160, (128, 4, 32), F32)
            un_g = carve(44672, (128, 4, 34), F32)
            ostg_s = carve(45248, (128, 512), F32)
            Rub, Rcb = [res("ub0"), res("ub1")], [res("cb0"), res("cb1")]
            Rsg, Rue, RaT = res("sgb"), res("uext"), [res("aT0"), res("aT1")]
            Rss, Ruh, Run, Ros = [res("stgs0"), res("stgs1")], res("uhg"), res("ung"), res("ostgs")
            wupv = w_up[l].rearrange("(c p) f -> p c f", p=128)
            wdnv = w_down[l].rearrange("(k p) d -> p k d", p=128)
            for jj in range(22):
                sg_ = wload([(lambda sl: sl[:, :].rearrange("p (c f) -> p c f", c=16), wupv[:, :, jj * 256:(jj + 1) * 256])])
                su_ = wload([(lambda sl: sl[:, :].rearrange("p (c f) -> p c f", c=16), wupv[:, :, FF + jj * 256:FF + (jj + 1) * 256])])
                wg_v = wsl[sg_][:, :].rearrange("p (c f) -> p c f", c=16)
                wu_v = wsl[su_][:, :].rearrange("p (c f) -> p c f", c=16)
                for gu in range(2):
                    def cp(pv, b, gu=gu):
                        kb.op("act", lambda e, pv=pv: e.activation(out=uh_g[:, 2 * gu:2 * gu + 2, :], in_=pv[:, 0:2, 0:32], func=AF.Copy), reads=[Rp[b]], writes=[Ruh])
                    tr_in2(stg_s[gu], Rss[gu], sconv[l, :, gu * FF + jj * 256:gu * FF + (jj + 1) * 256], 32, 256, cp)
                a_i = (jj // 2) % 2
                kk0 = 2 * (jj % 2)
                for k in range(2):
                    j = 2 * jj + k
                    for gu, wv_ in ((0, wg_v), (1, wu_v)):
                        fch = gu * 44 + j
                        for (t0, n) in ctiles(15, NX):
                            b = nbank()
                            kb.group("pe", [(lambda e, c=c, k=k, t0=t0, n=n, b=b, wv_=wv_: e.matmul(pbank[b][:, 0:n], wv_[:, c, k * 128:(k + 1) * 128], hT[:, c, t0:t0 + n], start=(c == 0), stop=(c == DC - 1)))
                                            for c in range(DC)], reads=[Rw[sg_ if gu == 0 else su_], Rh], writes=[Rp[b]])
                            kb.op("act", lambda e, gu=gu, t0=t0, n=n, b=b: e.activation(out=ub[gu][:, t0 - 15:t0 - 15 + n], in_=pbank[b][:, 0:n], func=AF.Copy), reads=[Rp[b]], writes=[Rub[gu]])
                        cw = [convT[:, (l * 3 + q) * 88 + fch:(l * 3 + q) * 88 + fch + 1] for q in range(3)]
                        cbias = convT[:, 528 + l * 88 + fch:528 + l * 88 + fch + 1]
                        u = ub[gu]
                        kb.op("act", lambda e, gu=gu, u=u, cw=cw, cbias=cbias: e.activation(out=cbf[gu][:, 0:NPR], in_=u[:, 2:2 + NPR], func=AF.Identity, scale=cw[2], bias=cbias), reads=[Rub[gu], Rc], writes=[Rcb[gu]])
                        kb.op("dve", lambda e, gu=gu, u=u, cw=cw: e.scalar_tensor_tensor(out=cbf[gu][:, 0:NPR], in0=u[:, 1:1 + NPR], scalar=cw[1], in1=cbf[gu][:, 0:NPR], op0=ALU.mult, op1=ALU.add),
                              reads=[Rub[gu], Rcb[gu], Rc], writes=[Rcb[gu]])
                        kb.op("dve", lambda e, gu=gu, u=u, cw=cw: e.scalar_tensor_tensor(out=cbf[gu][:, 0:NPR], in0=u[:, 0:NPR], scalar=cw[0], in1=cbf[gu][:, 0:NPR], op0=ALU.mult, op1=ALU.add),
                              reads=[Rub[gu], Rcb[gu], Rc], writes=[Rcb[gu]])
                        ue = uext[:, gu, :, :]
                        kb.op("act", lambda e, ue=ue, gu=gu, k=k: e.activation(out=ue[:, :, 0:2], in_=uh_g[:, 2 * gu + k, :].rearrange("p (b t) -> p b t", b=NSQ), func=AF.Copy), reads=[Ruh], writes=[Rue])
                        kb.op("act", lambda e, ue=ue, u=u: e.activation(out=ue[:, :, 2:6], in_=u[:, 2 + NPR:2 + NPR + NS].rearrange("p (b t) -> p b t", b=NSQ), func=AF.Copy), reads=[Rub[gu]], writes=[Rue])
                        cs_ = cbf[gu][:, NPR:NTOK].rearrange("p (b t) -> p b t", b=NSQ)
                        kb.op("act", lambda e, ue=ue, cs_=cs_, cw=cw, cbias=cbias: e.activation(out=cs_, in_=ue[:, :, 2:6], func=AF.Identity, scale=cw[2], bias=cbias), reads=[Rue, Rc], writes=[Rcb[gu]])
                        kb.op("dve", lambda e, ue=ue, cs_=cs_, cw=cw: e.scalar_tensor_tensor(out=cs_, in0=ue[:, :, 1:5], scalar=cw[1], in1=cs_, op0=ALU.mult, op1=ALU.add), reads=[Rue, Rcb[gu], Rc], writes=[Rcb[gu]])
                        kb.op("dve", lambda e, ue=ue, cs_=cs_, cw=cw: e.scalar_tensor_tensor(out=cs_, in0=ue[:, :, 0:4], scalar=cw[0], in1=cs_, op0=ALU.mult, op1=ALU.add), reads=[Rue, Rcb[gu], Rc], writes=[Rcb[gu]])
                        kb.op("act", lambda e, u=u, gu=gu, k=k: e.activation(out=un_g[:, 2 * gu + k, 0:2], in_=u[:, NPR:NPR + 2], func=AF.Copy), reads=[Rub[gu]], writes=[Run])
                        kb.op("act", lambda e, ue=ue, gu=gu, k=k: e.activation(out=un_g[:, 2 * gu + k, 2:34].rearrange("p (b t) -> p b t", b=NSQ), in_=ue[:, :, 4:6], func=AF.Copy), reads=[Rue], writes=[Run])
                    kb.op("act", lambda e: e.activation(out=sgb[:, :], in_=cbf[0][:, :], func=AF.Silu), reads=[Rcb[0]], writes=[Rsg])
                    kb.op("dve", lambda e, a_i=a_i, k=k, kk0=kk0: e.tensor_tensor(out=aT[a_i][:, kk0 + k, :], in0=sgb[:, :], in1=cbf[1][:, :], op=ALU.mult), reads=[Rsg, Rcb[1]], writes=[RaT[a_i]])
                for gu in range(2):
                    b = nbank()
                    kb.group("pe", [(lambda e, k=k, gu=gu, b=b: e.transpose(pbank[b][0:34, k * 128:(k + 1) * 128], un_g[:, 2 * gu + k, :], identf[:, :])) for k in range(2)], reads=[Run, Rc], writes=[Rp[b]])
                    kb.op("act", lambda e, b=b: e.activation(out=ostg_s[0:34, 0:256], in_=pbank[b][0:34, 0:256], func=AF.Copy), reads=[Rp[b]], writes=[Ros])
                    kb.dma("sp", convp_o[l, :, gu * FF + jj * 256:gu * FF + (jj + 1) * 256], ostg_s[0:2, 0:256], reads=[Ros], writes=[res("outdram")])
                    kb.dma("sp", convs_o[l, :, gu * FF + jj * 256:gu * FF + (jj + 1) * 256], ostg_s[2:34, 0:256], reads=[Ros], writes=[res("outdram")])
                if jj % 2 == 0:
                    continue
                sds = []
                for hh in range(2):
                    sd_ = wload([(lambda sl: sl[:, :].rearrange("p (k d) -> p k d", k=2), wdnv[:, 2 * (jj - 1 + hh):2 * (jj - 1 + hh) + 2, :])])
                    sds.append((sd_, wsl[sd_][:, :].rearrange("p (k d) -> p k d", k=2)))
                for o in range(DC):
                    for (t0, n) in ctiles(0, NTOK):
                        b = nbank()
                        kb.group("pe", [(lambda e, k=k, o=o, t0=t0, n=n, b=b, a_i=a_i, sds=sds: e.matmul(pbank[b][:, 0:n], sds[k // 2][1][:, k % 2, o * 128:(o + 1) * 128], aT[a_i][:, k, t0:t0 + n], start=(k == 0), stop=(k == 3)))
                                        for k in range(4)], reads=[Rw[sds[0][0]], Rw[sds[1][0]], RaT[a_i]], writes=[Rp[b]])
                        kb.op("dve", lambda e, o=o, t0=t0, n=n, b=b: e.tensor_tensor(out=xT[:, o, P0 + t0:P0 + t0 + n], in0=pbank[b][:, 0:n], in1=xT[:, o, P0 + t0:P0 + t0 + n], op=ALU.add),
                              reads=[Rp[b], Rx], writes=[Rx])
            kb.barrier()

        def ple(l):
            rmsnorm(64 + l * 16, P0, NX, sq512, Rsq5)
            kb.barrier()
            pT = carve(0, (128, 2, NTOK), BF16)
            stg_p = [carve(4352, (128, 256), F32), carve(5376, (128, 256), F32)]
            tg = [carve(6400, (128, 512), F32), carve(8448, (128, 512), F32)]
            RpT, Rsp, Rtg = res("pT"), [res("stgp0"), res("stgp1")], [res("tg0"), res("tg1")]
            for i_, (r0, n) in enumerate(ctiles(0, NTOK, 128)):
                def cp(pv, b, r0=r0, n=n):
                    kb.op("act", lambda e, pv=pv: e.activation(out=pT[:, :, r0:r0 + n], in_=pv[:, 0:2, 0:n], func=AF.Copy), reads=[Rp[b]], writes=[RpT])
                tr_in2(stg_p[i_ % 2], Rsp[i_ % 2], pin[l, r0:r0 + n, :], n, 256, cp)
            wp_v = carve(10496, (128, 2, D), BF16)
            Rwp = res("wp_v")
            kb.dma("pool", wp_v, w_proj[l].rearrange("(k p) d -> p k d", p=128), writes=[Rwp])
            wgv = w_gate[l].rearrange("(c p) f -> p c f", p=128)
            it = 0
            for gq in range(8):
                sg_ = wload([(lambda sl: sl[:, :].rearrange("p (c f) -> p c f", c=16), wgv[:, :, gq * 256:(gq + 1) * 256])])
                wg_v = wsl[sg_][:, :].rearrange("p (c f) -> p c f", c=16)
                for o2 in range(2):
                    o = 2 * gq + o2
                    for (t0, n) in ctiles(0, NTOK):
                        b1 = nbank()
                        kb.group("pe", [(lambda e, c=c, o2=o2, t0=t0, n=n, b1=b1, wg_v=wg_v: e.matmul(pbank[b1][:, 0:n], wg_v[:, c, o2 * 128:(o2 + 1) * 128], hT[:, c, P0 + t0:P0 + t0 + n], start=(c == 0), stop=(c == DC - 1)))
                                        for c in range(DC)], reads=[Rw[sg_], Rh], writes=[Rp[b1]])
                        b2 = nbank()
                        kb.group("pe", [(lambda e, k=k, o=o, t0=t0, n=n, b2=b2: e.matmul(pbank[b2][:, 0:n], wp_v[:, k, o * 128:(o + 1) * 128], pT[:, k, t0:t0 + n], start=(k == 0), stop=(k == 1)))
                                        for k in range(2)], reads=[Rwp, RpT], writes=[Rp[b2]])
                        ti = it % 2
                        it += 1
                        kb.op("act", lambda e, n=n, b1=b1, ti=ti: e.activation(out=tg[ti][:, 0:n], in_=pbank[b1][:, 0:n], func=AF.Sigmoid), reads=[Rp[b1]], writes=[Rtg[ti]])
                        kb.op("dve", lambda e, n=n, b2=b2, ti=ti: e.tensor_tensor(out=tg[ti][:, 0:n], in0=pbank[b2][:, 0:n], in1=tg[ti][:, 0:n], op=ALU.mult), reads=[Rp[b2], Rtg[ti]], writes=[Rtg[ti]])
                        kb.op("dve", lambda e, o=o, t0=t0, n=n, ti=ti: e.tensor_tensor(out=xT[:, o, P0 + t0:P0 + t0 + n], in0=tg[ti][:, 0:n], in1=xT[:, o, P0 + t0:P0 + t0 + n], op=ALU.add),
                              reads=[Rtg[ti], Rx], writes=[Rx])
            kb.barrier()

        conv_ffn(0)
        ple(0)
        if DEBUG:
            dbg0 = dout("dbg0", (NTOK, D))
            d_st = carve(0, (128, D), F32)
            for (r0, n) in ctiles(0, NTOK, 128):
                tr_out(lambda c, r0=r0, n=n: xT[:, c, P0 + r0:P0 + r0 + n], Rx, n, DC, d_st, res("dst_dbg"), dbg0[r0:r0 + n, :])
            kb.barrier()

        LOGG = [float(np.log1p(-(2.0 ** (-5.0 - h)))) for h in range(NH)]
        xflat = xT[:, :, :].rearrange("p a b -> p (a b)")
        ypart = [nc.dram_tensor(f"ypart{h}", [128, DC * NTOK], F32).ap() for h in range(NH)]
        rmsnorm(16, P0, NX, sq512, Rsq5)
        Rxsp = res("xspill")
        kb.dma("sp", xspill, xflat, reads=[Rx], writes=[Rxsp])
        kb.barrier()

        def xr(off, shape, dt):
            return carve(off, shape, dt, base=xflat)
        qT, qcT, kT = xr(0, (128, 2, NTOK), BF16), xr(4352, (128, 2, NTOK), BF16), xr(8704, (128, 2, NTOK), BF16)
        kdtok = xr(13056, (128, 9, 256), BF16)
        vtok = xr(17664, (128, 9, 512), BF16)
        ogT = xr(26880, (128, 4, NTOK), BF16)
        crq = xr(35584, (128, NTOK), F32)
        Sst = xr(39936, (128, 2, 512), F32)
        Sbf = xr(44032, (128, 2, 512), BF16)
        Sin = xr(46080, (128, 2, 512), F32)
        ktok_s = xr(50176, (128, 256), BF16)
        kdsb = [xr(50688, (128, 256), BF16), xr(51200, (128, 256), BF16)]
        Ssm = [xr(51712 + 4096 * i, (128, 2, 512), F32) for i in range(3)]
        Ssb = [xr(64000 + 2048 * i, (128, 2, 512), BF16) for i in range(2)]
        scm = xr(68096, (128, 128), BF16)
        scms = xr(68352, (128, 64), BF16)
        T = [carve(2048 * i, (128, 512), F32) for i in range(3)]
        ob = carve(6144, (128, 4, 128), BF16)
        osq = carve(7168, (128, 4, 128), BF16)
        mu, msq, var = carve(8192, (128, 128), F32), carve(8704, (128, 128), F32), carve(9216, (128, 128), F32)
        sgt = [carve(10240, (128, 512), F32), carve(12288, (128, 512), F32)]
        yst = [carve(14336 + 2048 * i, (128, 512), F32) for i in range(3)]
        sgT = carve(26624, (128, 4, NTOK), BF16)
        RsgT = res("sgT")
        RqT, RqcT, RkT, Rkd, Rvt, Rog, Rcrq, RSst, RSbf, RSin, Rkts = [res(n) for n in "qT qcT kT kdtok vtok ogT crq Sst Sbf Sin ktoks".split()]
        Rkdsb, RSsm, RSsb = [res("kdsb0"), res("kdsb1")], [res(f"Ssm{i}") for i in range(3)], [res(f"Ssb{i}") for i in range(2)]
        Rscm, Rscms, RT = res("scm"), res("scms"), [res(f"T{i}") for i in range(3)]
        Rob, Rosq, Rmu, Rmsq, Rvar = res("ob"), res("osq"), res("mu"), res("msq"), res("var")
        Rsgt, Ryst = [res("sgt0"), res("sgt1")], [res(f"yst{i}") for i in range(3)]
        w_in_v = w_in.rearrange("(c p) f -> p c f", p=128)
        w_out_v = w_out.rearrange("(k p) d -> p k d", p=128)
        cnt = {"sg": 0, "ys": 0, "ss": 0, "sb": 0, "kd": 0}

        def wd16(col0):
            s_ = wload([(lambda sl: sl[:, :].rearrange("p (c f) -> p c f", c=16), w_in_v[:, :, col0:col0 + 256])])
            return s_, wsl[s_][:, :].rearrange("p (c f) -> p c f", c=16)

        def groupnorm(psb, bo, n, tok0):
            kb.op("act", lambda e: e.activation(out=ob[:, :, 0:n], in_=psb[:, :, 0:n], func=AF.Copy), reads=[Rp[bo]], writes=[Rob])
            kb.op("act", lambda e: e.activation(out=osq[:, :, 0:n], in_=psb[:, :, 0:n], func=AF.Square), reads=[Rp[bo]], writes=[Rosq])
            b = nbank()
            kb.group("pe", [(lambda e, v=v, b=b: e.matmul(pbank[b][:, 0:n], onesb[:, :], ob[:, v, 0:n], start=(v == 0), stop=(v == 3))) for v in range(4)] +
                     [(lambda e, v=v, b=b: e.matmul(pbank[b][:, 128:128 + n], onesb[:, :], osq[:, v, 0:n], start=(v == 0), stop=(v == 3))) for v in range(4)],
                     reads=[Rob, Rosq, Rc], writes=[Rp[b]])
            kb.op("dve", lambda e, b=b: e.tensor_scalar(out=mu[:, 0:n], in0=pbank[b][:, 0:n], scalar1=1.0 / DV, scalar2=None, op0=ALU.mult), reads=[Rp[b]], writes=[Rmu])
            kb.op("dve", lambda e: e.tensor_tensor(out=msq[:, 0:n], in0=mu[:, 0:n], in1=mu[:, 0:n], op=ALU.mult), reads=[Rmu], writes=[Rmsq])
            kb.op("dve", lambda e, b=b: e.scalar_tensor_tensor(out=var[:, 0:n], in0=pbank[b][:, 128:128 + n], scalar=1.0 / DV, in1=msq[:, 0:n], op0=ALU.mult, op1=ALU.subtract),
                  reads=[Rp[b], Rmsq], writes=[Rvar])
            kb.op("act", lambda e: e.activation(out=var[:, 0:n], in_=var[:, 0:n], func=AF.Sqrt, bias=epst[:, 0:1]), reads=[Rvar, Rc], writes=[Rvar])
            kb.op("dve", lambda e: e.reciprocal(out=var[:, 0:n], in_=var[:, 0:n]), reads=[Rvar], writes=[Rvar])
            kb.op("dve", lambda e: e.tensor_tensor(out=psb[:, :, 0:n], in0=psb[:, :, 0:n], in1=mu[:, 0:n].unsqueeze(1).to_broadcast([128, 4, n]), op=ALU.subtract),
                  reads=[Rp[bo], Rmu], writes=[Rp[bo]])
            kb.op("dve", lambda e: e.tensor_tensor(out=ogT[:, :, tok0:tok0 + n], in0=psb[:, :, 0:n], in1=var[:, 0:n].unsqueeze(1).to_broadcast([128, 4, n]), op=ALU.mult),
                  reads=[Rp[bo], Rvar], writes=[Rog])

        for h in range(NH):
            g128 = float(np.exp(LOGG[h] * 128.0))
            g4 = float(np.exp(LOGG[h] * 4.0))
            kb.dma("sp", crq[:, :], crossq_d[h:h + 1, :].partition_broadcast(128), writes=[Rcrq])
            for which in range(2):
                s_, wv_ = wd16(which * 2048 + h * 256)
                for (t0, n) in ctiles(0, NTOK):
                    b1, b2 = nbank(), nbank()
                    for bb, half_ in ((b1, 0), (b2, 1)):
                        kb.group("pe", [(lambda e, c=c, bb=bb, half_=half_, t0=t0, n=n, wv_=wv_: e.matmul(pbank[bb][:, 0:n], wv_[:, c, half_ * 128:(half_ + 1) * 128], hT[:, c, P0 + t0:P0 + t0 + n], start=(c == 0), stop=(c == DC - 1)))
                                        for c in range(DC)], reads=[Rw[s_], Rh], writes=[Rp[bb]])
                    cosv, sinv = cst[:, 0, t0:t0 + n], cst[:, 1, t0:t0 + n]
                    dst = qT if which == 0 else kT
                    rdst = RqT if which == 0 else RkT
                    kb.op("dve", lambda e, b1=b1, n=n, cosv=cosv: e.tensor_tensor(out=T[0][:, 0:n], in0=pbank[b1][:, 0:n], in1=cosv, op=ALU.mult), reads=[Rp[b1], Rc], writes=[RT[0]])
                    kb.op("dve", lambda e, b2=b2, n=n, sinv=sinv: e.tensor_tensor(out=T[1][:, 0:n], in0=pbank[b2][:, 0:n], in1=sinv, op=ALU.mult), reads=[Rp[b2], Rc], writes=[RT[1]])
                    kb.op("dve", lambda e, n=n: e.tensor_tensor(out=T[0][:, 0:n], in0=T[0][:, 0:n], in1=T[1][:, 0:n], op=ALU.subtract), reads=[RT[0], RT[1]], writes=[RT[0]])
                    kb.op("act", lambda e, n=n, t0=t0, dst=dst: e.activation(out=dst[:, 0, t0:t0 + n], in_=T[0][:, 0:n], func=AF.Copy), reads=[RT[0]], writes=[rdst])
                    if which == 0:
                        kb.op("dve", lambda e, n=n, t0=t0: e.tensor_tensor(out=qcT[:, 0, t0:t0 + n], in0=T[0][:, 0:n], in1=crq[:, t0:t0 + n], op=ALU.mult), reads=[RT[0], Rcrq], writes=[RqcT])
                    kb.op("dve", lambda e, b1=b1, n=n, sinv=sinv: e.tensor_tensor(out=T[1][:, 0:n], in0=pbank[b1][:, 0:n], in1=sinv, op=ALU.mult), reads=[Rp[b1], Rc], writes=[RT[1]])
                    kb.op("dve", lambda e, b2=b2, n=n, cosv=cosv: e.tensor_tensor(out=T[2][:, 0:n], in0=pbank[b2][:, 0:n], in1=cosv, op=ALU.mult), reads=[Rp[b2], Rc], writes=[RT[2]])
                    kb.op("dve", lambda e, n=n: e.tensor_tensor(out=T[1][:, 0:n], in0=T[1][:, 0:n], in1=T[2][:, 0:n], op=ALU.add), reads=[RT[1], RT[2]], writes=[RT[1]])
                    kb.op("act", lambda e, n=n, t0=t0, dst=dst: e.activation(out=dst[:, 1, t0:t0 + n], in_=T[1][:, 0:n], func=AF.Copy), reads=[RT[1]], writes=[rdst])
                    if which == 0:
                        kb.op("dve", lambda e, n=n, t0=t0: e.tensor_tensor(out=qcT[:, 1, t0:t0 + n], in0=T[1][:, 0:n], in1=crq[:, t0:t0 + n], op=ALU.mult), reads=[RT[1], Rcrq], writes=[RqcT])
            sv = [wd16(4096 + h * 512 + hv * 256) for hv in range(2)]
            for i_, (r0, n) in enumerate(ctiles(0, NTOK, 128)):
                b = nbank()
                for hv in range(2):
                    s_, wv_ = sv[hv]
                    kb.group("pe", [(lambda e, c=c, b=b, hv=hv, r0=r0, n=n, wv_=wv_: e.matmul(pbank[b][0:n, hv * 256:(hv + 1) * 256], hT[:, c, P0 + r0:P0 + r0 + n], wv_[:, c, :], start=(c == 0), stop=(c == DC - 1)))
                                    for c in range(DC)], reads=[Rw[s_], Rh], writes=[Rp[b]])
                kb.op("act", lambda e, b=b, i_=i_, n=n: e.activation(out=vtok[0:n, i_, :], in_=pbank[b][0:n, :], func=AF.Copy), reads=[Rp[b]], writes=[Rvt])
            for i_, (r0, n) in enumerate(ctiles(0, NTOK, 128)):
                b = nbank()
                kb.group("pe", [(lambda e, ch=ch, b=b, r0=r0, n=n: e.matmul(pbank[b][0:n, ch * 128:(ch + 1) * 128], kT[:, ch, r0:r0 + n], identb[:, :], start=True, stop=True)) for ch in range(2)],
                         reads=[RkT, Rc], writes=[Rp[b]])
                if i_ < 8:
                    kb.op("act", lambda e, b=b, i_=i_, h=h: e.activation(out=kdtok[:, i_, :], in_=pbank[b][:, 0:256], func=AF.Copy, scale=kdt[:, h:h + 1]), reads=[Rp[b], Rc], writes=[Rkd])
                else:
                    kb.op("act", lambda e, b=b: e.activation(out=ktok_s[0:NS, :], in_=pbank[b][0:NS, 0:256], func=AF.Copy), reads=[Rp[b]], writes=[Rkts])

            def chain_step(j, Sdst, Rdst):
                bs = [nbank(), nbank()]
                for ch in range(2):
                    kb.op("pe", lambda e, ch=ch, j=j, bs=bs: e.matmul(pbank[bs[ch]][:, :], kdtok[:, j, ch * 128:(ch + 1) * 128], vtok[:, j, :], start=True, stop=True), reads=[Rkd, Rvt], writes=[Rp[bs[ch]]])
                    kb.op("dve", lambda e, ch=ch, bs=bs, g128=g128: e.scalar_tensor_tensor(out=Sdst[:, ch, :], in0=Sdst[:, ch, :], scalar=g128, in1=pbank[bs[ch]][:, :], op0=ALU.mult, op1=ALU.add),
                          reads=[Rdst, Rp[bs[ch]]], writes=[Rdst])

            kb.op("dve", lambda e: e.memset(Sst[:, :, :], 0.0), writes=[RSst])
            for j in range(8):
                chain_step(j, Sst, RSst)
            Rcci, Rcco = res(f"ccin{h}"), res(f"ccout{h}")
            kb.dma("sp", cc_in[h].ap().rearrange("(c p) v -> p c v", p=128), Sst[:, :, :], reads=[RSst], writes=[Rcci])
            kb.custom("pool", lambda e, h=h: e.collective_compute("AllGather", ALU.bypass, replica_groups=PAIRS, ins=[cc_in[h].ap().opt()], outs=[cc_out[h].ap().opt()]),
                      ccsem, reads=[Rcci], writes=[Rcco])
            gate_items = []
            gslots = [wd16(8192 + h * 512 + hv * 256) for hv in range(2)]
            for hv in range(2):
                for o2 in range(2):
                    for (t0, n) in ctiles(0, NTOK):
                        def item(hv=hv, o2=o2, t0=t0, n=n):
                            s_, wv_ = gslots[hv]
                            v = 2 * hv + o2
                            b = nbank()
                            kb.group("pe", [(lambda e, c=c, b=b, o2=o2, t0=t0, n=n, wv_=wv_: e.matmul(pbank[b][:, 0:n], wv_[:, c, o2 * 128:(o2 + 1) * 128], hT[:, c, P0 + t0:P0 + t0 + n], start=(c == 0), stop=(c == DC - 1)))
                                            for c in range(DC)], reads=[Rw[s_], Rh], writes=[Rp[b]])
                            kb.op("act", lambda e, b=b, n=n, v=v, t0=t0: e.activation(out=sgT[:, v, t0:t0 + n], in_=pbank[b][:, 0:n], func=AF.Silu), reads=[Rp[b]], writes=[RsgT])
                        gate_items.append(item)
            b = nbank()
            kb.group("pe", [(lambda e, ch=ch, b=b: e.matmul(pbank[b][0:NS, 0:NS], kT[:, ch, NPR:NTOK], qT[:, ch, NPR:NTOK], start=(ch == 0), stop=(ch == 1))) for ch in range(2)],
                     reads=[RkT, RqT], writes=[Rp[b]])
            kb.op("dve", lambda e, b=b, h=h: e.tensor_tensor(out=scms[0:NS, :], in0=pbank[b][0:NS, 0:NS], in1=dmasks[:, h, :], op=ALU.mult), reads=[Rp[b], Rc], writes=[Rscms])
            bo = 6
            psb = pbank[bo][:, :].rearrange("p (a b) -> p a b", a=4)
            kb.group("pe", [(lambda e, v=v, psb=psb: e.matmul(psb[:, v, 0:NS], vtok[0:NS, 8, v * 128:(v + 1) * 128], scms[0:NS, :], start=(v == 0), stop=False)) for v in range(4)],
                     reads=[Rvt, Rscms], writes=[Rp[bo]])
            for bq in range(NSQ):
                si = cnt["ss"] % 3
                cnt["ss"] += 1
                sbi = cnt["sb"] % 2
                cnt["sb"] += 1
                kb.dma("sp", Ssm[si][:, :, :], sret[bq, h].rearrange("(c p) v -> p c v", p=128), writes=[RSsm[si]])
                kb.op("act", lambda e, si=si, sbi=sbi: e.activation(out=Ssb[sbi][:, :, :], in_=Ssm[si][:, :, :], func=AF.Copy), reads=[RSsm[si]], writes=[RSsb[sbi]])
                fns = []
                for v in range(4):
                    for ch in range(2):
                        fns.append(lambda e, v=v, ch=ch, psb=psb, sbi=sbi, bq=bq: e.matmul(psb[:, v, 4 * bq:4 * bq + 4], Ssb[sbi][:, ch, v * 128:(v + 1) * 128], qcT[:, ch, NPR + 4 * bq:NPR + 4 * bq + 4],
                                                                                         start=False, stop=(ch == 1 and bq == NSQ - 1)))
                kb.group("pe", fns, reads=[RSsb[sbi], RqcT], writes=[Rp[bo]])
                ki = cnt["kd"] % 2
                cnt["kd"] += 1
                kb.op("act", lambda e, ki=ki, h=h, bq=bq: e.activation(out=kdsb[ki][0:NS, :], in_=ktok_s[0:NS, :], func=AF.Copy, scale=kds[:, h, bq:bq + 1]), reads=[Rkts, Rc], writes=[Rkdsb[ki]])
                bs = [nbank(), nbank()]
                for ch in range(2):
                    kb.op("pe", lambda e, ch=ch, bs=bs, ki=ki: e.matmul(pbank[bs[ch]][:, :], kdsb[ki][0:NS, ch * 128:(ch + 1) * 128], vtok[0:NS, 8, :], start=True, stop=True),
                          reads=[Rkdsb[ki], Rvt], writes=[Rp[bs[ch]]])
                    kb.op("dve", lambda e, ch=ch, bs=bs, si=si, g4=g4: e.scalar_tensor_tensor(out=Ssm[si][:, ch, :], in0=Ssm[si][:, ch, :], scalar=g4, in1=pbank[bs[ch]][:, :], op0=ALU.mult, op1=ALU.add),
                          reads=[RSsm[si], Rp[bs[ch]]], writes=[RSsm[si]])
                kb.dma("sp", rets_o[bq, h].rearrange("(c p) v -> p c v", p=128), Ssm[si][:, :, :], reads=[RSsm[si]], writes=[res("outdram")])
                if gate_items:
                    gate_items.pop(0)()
            while gate_items:
                gate_items.pop(0)()
            groupnorm(psb, bo, NS, NPR)
            kb.dma("sp", Sin[:, :, :], cc_out[h].ap()[0:DK, :].rearrange("(c p) v -> p c v", p=128), reads=[Rcco], writes=[RSin])
            kb.op("dve", lambda e: e.tensor_scalar(out=Sst[:, :, :], in0=Sin[:, :, :], scalar1=flagt[:, 0:1], scalar2=None, op0=ALU.mult), reads=[RSin, Rc], writes=[RSst])
            for j in range(8):
                tj = 128 * j
                kb.op("act", lambda e: e.activation(out=Sbf[:, :, :], in_=Sst[:, :, :], func=AF.Copy), reads=[RSst], writes=[RSbf])
                b = nbank()
                kb.group("pe", [(lambda e, ch=ch, b=b, tj=tj: e.matmul(pbank[b][:, 0:128], kT[:, ch, tj:tj + 128], qT[:, ch, tj:tj + 128], start=(ch == 0), stop=(ch == 1))) for ch in range(2)],
                         reads=[RkT, RqT], writes=[Rp[b]])
                kb.op("dve", lambda e, b=b, h=h: e.tensor_tensor(out=scm[:, :], in0=pbank[b][:, 0:128], in1=dmask[:, h, :], op=ALU.mult), reads=[Rp[b], Rc], writes=[Rscm])
                bo = nbank()
                psb = pbank[bo][:, :].rearrange("p (a b) -> p a b", a=4)
                fns = []
                for v in range(4):
                    fns.append(lambda e, v=v, psb=psb, j=j: e.matmul(psb[:, v, :], vtok[:, j, v * 128:(v + 1) * 128], scm[:, :], start=True, stop=False))
                    for ch in range(2):
                        fns.append(lambda e, v=v, ch=ch, psb=psb, tj=tj: e.matmul(psb[:, v, :], Sbf[:, ch, v * 128:(v + 1) * 128], qcT[:, ch, tj:tj + 128], start=False, stop=(ch == 1)))
                kb.group("pe", fns, reads=[Rvt, Rscm, RSbf, RqcT], writes=[Rp[bo]])
                chain_step(j, Sst, RSst)
                groupnorm(psb, bo, 128, tj)
            kb.dma("sp", retp_o[h].rearrange("(c p) v -> p c v", p=128), Sst[:, :, :], reads=[RSst], writes=[res("outdram")])
            kb.op("dve", lambda e: e.tensor_tensor(out=ogT[:, :, :], in0=ogT[:, :, :], in1=sgT[:, :, :], op=ALU.mult), reads=[Rog, RsgT], writes=[Rog])
            so = []
            for hv in range(2):
                s_ = wload([(lambda sl: sl[:, :].rearrange("p (k d) -> p k d", k=2), w_out_v[:, 4 * h + 2 * hv:4 * h + 2 * hv + 2, :])])
                so.append((s_, wsl[s_][:, :].rearrange("p (k d) -> p k d", k=2)))
            Ryp = res(f"ypart{h}")
            for o in range(DC):
                for (t0, n) in ctiles(0, NTOK):
                    b = nbank()
                    kb.group("pe", [(lambda e, v=v, b=b, o=o, t0=t0, n=n, so=so: e.matmul(pbank[b][:, 0:n], so[v // 2][1][:, v % 2, o * 128:(o + 1) * 128], ogT[:, v, t0:t0 + n], start=(v == 0), stop=(v == 3)))
                                    for v in range(4)], reads=[Rw[so[0][0]], Rw[so[1][0]], Rog], writes=[Rp[b]])
                    yi = cnt["ys"] % 3
                    cnt["ys"] += 1
                    kb.op("act", lambda e, b=b, n=n, yi=yi: e.activation(out=yst[yi][:, 0:n], in_=pbank[b][:, 0:n], func=AF.Copy), reads=[Rp[b]], writes=[Ryst[yi]])
                    kb.dma("sp", ypart[h][:, o * NTOK + t0:o * NTOK + t0 + n], yst[yi][:, 0:n], reads=[Ryst[yi]], writes=[Ryp])
        kb.barrier()
        kb.dma("sp", xflat, xspill, reads=[Rxsp], writes=[Rx])
        yrow = [carve(26624 + 8704 * i, (128, 2, NTOK), F32) for i in range(2)]
        Ryrow = [res("yrow0"), res("yrow1")]
        it_ = 0
        for h in range(NH):
            for o in range(0, DC, 2):
                yi = it_ % 2
                it_ += 1
                kb.dma("sp", yrow[yi][:, :, :], ypart[h][:, o * NTOK:(o + 2) * NTOK].rearrange("p (a b) -> p a b", a=2), reads=[res(f"ypart{h}")], writes=[Ryrow[yi]])
                kb.op("dve", lambda e, o=o, yi=yi: e.tensor_tensor(out=xT[:, o:o + 2, P0:NX], in0=xT[:, o:o + 2, P0:NX], in1=yrow[yi][:, :, :], op=ALU.add),
                      reads=[Rx, Ryrow[yi]], writes=[Rx])
        hx = carve(20480, (128, DC, 2), F32)
        Rhx, Rcxi, Rcxo = res("hx"), res("cxin"), res("cxout")
        cxi = nc.dram_tensor("cxi", [128, 2 * DC], F32)
        cxo = nc.dram_tensor("cxo", [256, 2 * DC], F32)
        kb.op("act", lambda e: e.activation(out=hx[:, :, :], in_=xT[:, :, S0 - 2:S0], func=AF.Copy), reads=[Rx], writes=[Rhx])
        kb.dma("sp", cxi.ap(), hx[:, :, :].rearrange("p a b -> p (a b)"), reads=[Rhx], writes=[Rcxi])
        kb.custom("pool", lambda e: e.collective_compute("AllGather", ALU.bypass, replica_groups=PAIRS, ins=[cxi.ap().opt()], outs=[cxo.ap().opt()]), ccsem, reads=[Rcxi], writes=[Rcxo])
        kb.dma("sp", hx[:, :, :].rearrange("p a b -> p (a b)"), cxo.ap()[0:128, :], reads=[Rcxo], writes=[Rhx])
        kb.op("dve", lambda e: e.tensor_scalar(out=xT[:, :, 15:17], in0=hx[:, :, :], scalar1=flagt[:, 0:1], scalar2=None, op0=ALU.mult), reads=[Rhx, Rc, Rx], writes=[Rx])
        kb.barrier()
        conv_ffn(1)
        ple(1)

        ta2 = carve(16384, (128, NX), F32)
        yT = carve(20864, (128, DC, 128), F32)
        ostg_y = carve(29056, (128, D), F32)
        Rta2, RyT, Rosy = res("ta2"), res("yT"), res("ostgy")
        sqv2 = carve(0, (128, DC, 512), BF16)
        sumsq_rstd(ta2, Rta2, P0, NX, sqv2, Rsq5, 512)
        for (r0, n) in ctiles(0, NTOK, 128):
            for c in range(DC):
                kb.op("dve", lambda e, c=c, r0=r0, n=n: e.scalar_tensor_tensor(out=yT[:, c, 0:n], in0=xT[:, c, P0 + r0:P0 + r0 + n], scalar=gamT[:, 96 + c:97 + c], in1=ta2[:, P0 + r0:P0 + r0 + n],
                                                                              op0=ALU.mult, op1=ALU.mult), reads=[Rx, Rta2, Rc], writes=[RyT])
            tr_out(lambda c, n=n: yT[:, c, 0:n], RyT, n, DC, ostg_y, Rosy, y_o[r0:r0 + n, :])
        kb.wait_all("sp", [res("outdram")])
        kb.barrier()
        with nc.Block() as block:
            kb.replay(block)
    return nc


def _tables(half):
    log_g = np.log1p(-(2.0 ** (-5.0 - np.arange(NH, dtype=np.float64))))
    halfd = DK // 2
    inv = 10000.0 ** (-np.arange(halfd, dtype=np.float64) / halfd)
    pos = np.concatenate([half * NPR + np.arange(NPR), np.tile(16384 + np.arange(4), NSQ)]).astype(np.float64)
    ang = inv[:, None] * pos[None, :]
    cs = np.stack([np.cos(ang), np.sin(ang)], axis=1).astype(np.float32)
    n_in = np.concatenate([np.arange(NPR) % 128, np.tile(np.arange(4), NSQ)]).astype(np.float64)
    crossq = np.exp(log_g[:, None] * (n_in[None, :] + 1.0)).astype(np.float32)
    m = np.arange(128)
    dm = np.where(m[None, :, None] <= m[None, None, :], np.exp(log_g[:, None, None] * np.maximum(m[None, None, :] - m[None, :, None], 0)), 0.0) * DK ** -0.5
    dmask = np.ascontiguousarray(dm.transpose(1, 0, 2)).astype(np.float32)
    ms = np.arange(64)
    same = (ms[:, None] // 4) == (ms[None, :] // 4)
    dms = np.where(same[None] & (ms[None, :, None] <= ms[None, None, :]), np.exp(log_g[:, None, None] * np.maximum(ms[None, None, :] - ms[None, :, None], 0)), 0.0) * DK ** -0.5
    dmasks = np.ascontiguousarray(dms.transpose(1, 0, 2)).astype(np.float32)
    kdt = (np.exp(log_g[None, :] * (127.0 - m[:, None])) * DK ** -0.5).astype(np.float32)
    kds = np.zeros((64, NH, NSQ), np.float32)
    for t in range(64):
        kds[t, :, t // 4] = np.exp(log_g * (3.0 - (t % 4))) * DK ** -0.5
    invc = np.zeros((4, 15), np.float32)
    for gi, w in enumerate((2, 4, 8, 16)):
        p = np.arange(15)
        invc[gi] = 1.0 / np.minimum(p + 1, w) if half == 0 else 1.0 / w
    flagv = np.concatenate([[float(half)], float(half) * np.exp(log_g * 1024.0)]).astype(np.float32)[None, :]
    return dict(cs=cs, crossq=crossq, dmask=dmask, dmasks=dmasks, kdt=kdt, kds=kds, invc=invc.reshape(1, 60), flagv=flagv)


_NC_CACHE = {}


def kernel(x_prompt, x_sample, p_prompt, p_sample, state_pool, state_ret, state_conv,
           norm_mix, norm_ffn, norm_ple, norm_final, pool_w, pool_scale, ret_w_in, ret_w_out,
           ffn_w_up, ffn_conv_w, ffn_conv_b, ffn_w_down, ple_w_proj, ple_w_gate):
    f32 = lambda a: np.ascontiguousarray(np.asarray(a, dtype=np.float32))
    x_prompt, x_sample, p_prompt, p_sample = map(f32, (x_prompt, x_sample, p_prompt, p_sample))
    state_pool, state_ret, state_conv = map(f32, (state_pool, state_ret, state_conv))
    if "nc" not in _NC_CACHE:
        _NC_CACHE["nc"] = build_program()
    nc = _NC_CACHE["nc"]
    small = np.concatenate([f32(norm_mix).reshape(32, 128), f32(norm_ffn).reshape(32, 128), f32(norm_ple).reshape(32, 128),
                            f32(norm_final).reshape(16, 128), f32(pool_scale).reshape(16, 128)], axis=0)
    convp = np.zeros((768, 128), np.float32)
    convp[0:528] = f32(ffn_conv_w).reshape(528, 128)
    convp[528:704] = f32(ffn_conv_b).reshape(176, 128)
    shared = dict(small=small, convp=convp, identd=np.eye(128, dtype=np.float32), pool_w=f32(pool_w)[0],
                  ret_w_in=f32(ret_w_in)[0], ret_w_out=f32(ret_w_out)[0], ffn_w_up=f32(ffn_w_up), ffn_w_down=f32(ffn_w_down),
                  ple_w_proj=f32(ple_w_proj), ple_w_gate=f32(ple_w_gate))
    in_maps = []
    for core in range(8):
        s, half = core // 2, core % 2
        halo = x_prompt[s, NPR - HALO:NPR] if half else np.zeros((HALO, D), np.float32)
        sl = slice(NSQ * core, NSQ * (core + 1))
        xin = np.concatenate([halo, x_prompt[s, half * NPR:(half + 1) * NPR], x_sample[sl].reshape(NS, D)], axis=0)
        pin = np.concatenate([p_prompt[:, s, half * NPR:(half + 1) * NPR], p_sample[:, sl].reshape(2, NS, 256)], axis=1)
        m = dict(shared)
        m.update(xin=np.ascontiguousarray(xin), pin=np.ascontiguousarray(pin), spool=state_pool[0, sl].reshape(240, D),
                 sret=state_ret[0, sl], sconv=np.ascontiguousarray(state_conv[:, sl].reshape(2, 32, F2)))
        m.update(_tables(half))
        in_maps.append(m)
    res = run_bass_kernel_spmd(nc, in_maps, core_ids=list(range(8)))
    r = res.results
    if DEBUG:
        _NC_CACHE["dbg"] = r
    y_prompt = np.stack([np.concatenate([r[2 * s]["y"][0:NPR], r[2 * s + 1]["y"][0:NPR]], axis=0) for s in range(4)])
    y_sample = np.concatenate([r[c]["y"][NPR:].reshape(NSQ, 4, D) for c in range(8)], axis=0)
    pool_p = np.stack([r[2 * s + 1]["pool_p"] for s in range(4)])[None]
    pool_s = np.concatenate([r[c]["pool_s"].reshape(NSQ, 15, D) for c in range(8)], axis=0)[None]
    ret_p = np.stack([r[2 * s + 1]["ret_p"] for s in range(4)])[None]
    ret_s = np.concatenate([r[c]["ret_s"] for c in range(8)], axis=0)[None]
    conv_p = np.stack([r[2 * s + 1]["conv_p"] for s in range(4)], axis=1)
    conv_s = np.concatenate([r[c]["conv_s"].reshape(2, NSQ, 2, F2) for c in range(8)], axis=1)
    return (y_prompt, y_sample, pool_p, pool_s, ret_p, ret_s, conv_p, conv_s)
```

```python
import contextlib
import numpy as np
import concourse.bass as bass
import concourse.mybir as mybir
from concourse.bass_utils import run_bass_kernel_spmd

F32 = mybir.dt.float32
BF16 = mybir.dt.bfloat16
AF = mybir.ActivationFunctionType
ALU = mybir.AluOpType

D = 2048
DC = 16
FF = 5632
F2 = 11264
NH = 8
DK = 256
DV = 512
NPR = 1024
NSQ = 16
NS = 64
HALO = 17
P0 = HALO
S0 = P0 + NPR
NX = S0 + NS
NTOK = NPR + NS
EPS = 1e-6
DEBUG = False
PROFILE = False
ENGS = ("pe", "act", "dve", "pool", "sp")
PAIRS = [[0, 1], [2, 3], [4, 5], [6, 7]]


class Res:
    __slots__ = ("name", "w", "r")

    def __init__(self, name):
        self.name = name
        self.w = None
        self.r = []


class KB:
    def __init__(self, nc, stack, n_sp=24, n_pool=8):
        self.nc = nc
        self.ops = {e: [] for e in ENGS}
        self.sem = {}
        self.cnt = {}
        for e in ("pe", "act", "dve", "pool"):
            self.sem[e] = stack.enter_context(nc.semaphore("prog_" + e))
            self.cnt[e] = 0
        self.dsems = {}
        for q, n in (("sp", n_sp), ("pool", n_pool)):
            self.dsems[q] = [[stack.enter_context(nc.semaphore(f"d_{q}_{i}")), 0] for i in range(n)]
        self.dnext = {"sp": 0, "pool": 0}
        self.waited = {e: {} for e in ENGS}
        self.semobj = {}
        self.csems = []

    def _collect(self, eng, reads, writes):
        need = {}

        def add(tok, same_ok):
            if tok is None:
                return
            key, val = tok
            if same_ok and key == eng:
                return
            if need.get(key, 0) < val:
                need[key] = val
        for r in reads:
            add(r.w, False)
        for w in writes:
            add(w.w, True)
            for t in w.r:
                add(t, True)
        out = []
        for key, val in need.items():
            if self.waited[eng].get(key, 0) >= val:
                continue
            self.waited[eng][key] = val
            out.append((key, val))
        return out

    def _emit_waits(self, eng, waits):
        for key, val in waits:
            semh = self.sem[key] if isinstance(key, str) else self.semobj[key]
            self.ops[eng].append(("wait", semh, val))

    def _commit(self, tok, reads, writes):
        for r in reads:
            r.r.append(tok)
        for w in writes:
            w.w = tok
            w.r = []

    def op(self, eng, fn, reads=(), writes=()):
        self._emit_waits(eng, self._collect(eng, reads, writes))
        self.cnt[eng] += 1
        tok = (eng, self.cnt[eng])
        self.ops[eng].append(("ins", fn, self.sem[eng], 1))
        self._commit(tok, reads, writes)
        return tok

    def group(self, eng, fns, reads=(), writes=()):
        self._emit_waits(eng, self._collect(eng, reads, writes))
        for fn in fns[:-1]:
            self.ops[eng].append(("ins", fn, None, 0))
        self.cnt[eng] += 1
        tok = (eng, self.cnt[eng])
        self.ops[eng].append(("ins", fns[-1], self.sem[eng], 1))
        self._commit(tok, reads, writes)
        return tok

    def dma(self, q, out, in_, reads=(), writes=()):
        ring = self.dsems[q]
        i = self.dnext[q]
        self.dnext[q] = (i + 1) % len(ring)
        ent = ring[i]
        semh = ent[0]
        key = ("d", q, i)
        self.semobj[key] = semh
        if ent[1] > 0 and self.waited[q].get(key, 0) < ent[1]:
            self.waited[q][key] = ent[1]
            self.ops[q].append(("wait", semh, ent[1]))
        self._emit_waits(q, self._collect(q, reads, writes))
        ent[1] += 16
        tok = (key, ent[1])
        self.ops[q].append(("dma", out, in_, semh))
        self._commit(tok, reads, writes)
        return tok

    def custom(self, q, fn, cs, reads=(), writes=()):
        self._emit_waits(q, self._collect(q, reads, writes))
        key = ("c", id(cs))
        self.semobj[key] = cs[0]
        cs[1] += 1
        tok = (key, cs[1])
        self.ops[q].append(("ins", fn, cs[0], 1))
        self._commit(tok, reads, writes)
        return tok

    def barrier(self):
        toks = [(e, self.cnt[e]) for e in ("pe", "act", "dve", "pool") if self.cnt[e] > 0]
        for q, ring in self.dsems.items():
            for i, ent in enumerate(ring):
                if ent[1] > 0:
                    key = ("d", q, i)
                    self.semobj[key] = ent[0]
                    toks.append((key, ent[1]))
        for e in ENGS:
            for key, val in toks:
                if key == e:
                    continue
                if self.waited[e].get(key, 0) >= val:
                    continue
                self.waited[e][key] = val
                semh = self.sem[key] if isinstance(key, str) else self.semobj[key]
                self.ops[e].append(("wait", semh, val))

    def scope(self, name):
        for e in ENGS:
            self.ops[e].append(("scope", name))

    def wait_all(self, q, resources):
        self._emit_waits(q, self._collect(q, resources, resources))

    def replay(self, block):
        names = {"pe": "tensor", "act": "scalar", "dve": "vector", "pool": "gpsimd", "sp": "sync"}

        def make(e):
            def run(eng):
                cur = None
                for item in self.ops[e]:
                    k = item[0]
                    if k == "scope":
                        if not PROFILE:
                            continue
                        if cur is not None:
                            cur.__exit__(None, None, None)
                        cur = self.nc.named_scope(item[1])
                        cur.__enter__()
                        continue
                    if k == "wait":
                        eng.wait_ge(item[1], item[2])
                    elif k == "ins":
                        ins = item[1](eng)
                        if item[2] is not None:
                            ins.then_inc(item[2], item[3])
                    else:
                        eng.dma_start(out=item[1], in_=item[2]).then_inc(item[3], 16)
                if cur is not None:
                    cur.__exit__(None, None, None)
            return run
        for e in ENGS:
            getattr(block, names[e])(make(e))


def ctiles(c0, c1, step=512):
    out = []
    if step == 512:
        nt = -(-(c1 - c0) // step)
        base, rem = divmod(c1 - c0, nt)
        for i in range(nt):
            n = base + (1 if i < rem else 0)
            out.append((c0, n))
            c0 += n
        return out
    while c0 < c1:
        n = min(step, c1 - c0)
        out.append((c0, n))
        c0 += n
    return out


def build_program():
    nc = bass.Bass("TRN2", target_bir_lowering=False)

    def din(name, shape):
        return nc.dram_tensor(name, list(shape), F32, kind="ExternalInput").ap()

    def dout(name, shape):
        return nc.dram_tensor(name, list(shape), F32, kind="ExternalOutput").ap()

    xin = din("xin", (NX, D))
    pin = din("pin", (2, NTOK, 256))
    spool = din("spool", (240, D))
    sret = din("sret", (NSQ, NH, DK, DV))
    sconv = din("sconv", (2, 32, F2))
    small = din("small", (128, 128))
    convp = din("convp", (768, 128))
    identd = din("identd", (128, 128))
    pool_w = din("pool_w", (4, 512, 512))
    w_in = din("ret_w_in", (D, 12288))
    w_out = din("ret_w_out", (4096, D))
    w_up = din("ffn_w_up", (2, D, F2))
    w_down = din("ffn_w_down", (2, FF, D))
    w_proj = din("ple_w_proj", (2, 256, D))
    w_gate = din("ple_w_gate", (2, D, D))
    cs_d = din("cs", (128, 2, NTOK))
    crossq_d = din("crossq", (NH, NTOK))
    dmask_d = din("dmask", (128, NH, 128))
    dmasks_d = din("dmasks", (64, NH, 64))
    kdt_d = din("kdt", (128, NH))
    kds_d = din("kds", (64, NH, NSQ))
    invc_d = din("invc", (1, 60))
    flag_d = din("flagv", (1, 1 + NH))

    y_o = dout("y", (NTOK, D))
    poolp_o = dout("pool_p", (15, D))
    pools_o = dout("pool_s", (240, D))
    retp_o = dout("ret_p", (NH, DK, DV))
    rets_o = dout("ret_s", (NSQ, NH, DK, DV))
    convp_o = dout("conv_p", (2, 2, F2))
    convs_o = dout("conv_s", (2, 32, F2))

    xspill = nc.dram_tensor("xspill", [128, DC * NX], F32).ap()
    cc_in = [nc.dram_tensor(f"cc_in{h}", [DK, DV], F32) for h in range(NH)]
    cc_out = [nc.dram_tensor(f"cc_out{h}", [2 * DK, DV], F32) for h in range(NH)]
    cx_in = nc.dram_tensor("cx_in", [2, D], F32)
    cx_out = nc.dram_tensor("cx_out", [4, D], F32)

    with contextlib.ExitStack() as st:
        E = st.enter_context
        kb = KB(nc, st)
        ccsem = [E(nc.semaphore("ccsem")), 0]

        def sb(name, shape, dt):
            return E(nc.sbuf_tensor("sb_" + name, list(shape), dt))

        xT = sb("xT", (128, DC, NX), F32)
        hT = sb("hT", (128, DC, NX), BF16)
        NWS = 4
        wsl = [sb(f"wsl{i}", (128, 4096), BF16) for i in range(NWS)]
        gamT = sb("gamT", (128, 128), F32)
        convT = sb("convT", (128, 768), F32)
        identf = sb("identf", (128, 128), F32)
        identb = sb("identb", (128, 128), BF16)
        onesb = sb("onesb", (128, 128), BF16)
        epst = sb("epst", (128, 1), F32)
        flagt = sb("flagt", (128, 1 + NH), F32)
        cst = sb("cst", (128, 2, NTOK), F32)
        dmask = sb("dmask", (128, NH, 128), F32)
        dmasks = sb("dmasks", (64, NH, 64), F32)
        kdt = sb("kdt", (128, NH), F32)
        kds = sb("kds", (64, NH, NSQ), F32)
        invc = sb("invc", (128, 60), F32)
        rstd = sb("rstd", (128, 512), F32)
        SCRW = 12544
        scr = sb("scr", (128, SCRW), F32)
        pbank = [E(nc.psum_tensor(f"pb{i}", [128, 512], F32)) for i in range(8)]

        R = {}

        def res(name):
            if name not in R:
                R[name] = Res(name)
            return R[name]

        Rx, Rh = res("xT"), res("hT")
        Rw = [res(f"w{i}") for i in range(NWS)]
        Rp = [res(f"pb{i}") for i in range(8)]
        Rc = res("consts")
        Rrstd = res("rstd")
        Rscr = {}

        def carve(off_b, shape, dt, base=None):
            t = scr if base is None else base
            n = int(np.prod(shape[1:]))
            nb = n * (4 if dt == F32 else 2)
            assert off_b % 4 == 0 and nb % 4 == 0
            ap = t[:, off_b // 4: off_b // 4 + nb // 4]
            if dt != F32:
                ap = ap.bitcast(dt)
            if len(shape) == 3:
                ap = ap.rearrange("p (a b) -> p a b", a=shape[1])
            elif len(shape) == 4:
                ap = ap.rearrange("p (a b c) -> p a b c", a=shape[1], b=shape[2])
            return ap[0:shape[0]]

        state = {"bank": 0, "ws": 0}

        def nbank(lo=0, hi=6):
            b = lo + state["bank"] % (hi - lo)
            state["bank"] += 1
            return b

        def wslot():
            s = state["ws"] % NWS
            state["ws"] += 1
            return s

        kb.dma("sp", identf[:], identd, writes=[Rc])
        kb.dma("pool", identb[:], identd, writes=[Rc])
        kb.dma("sp", cst[:], cs_d, writes=[Rc])
        kb.dma("sp", dmask[:], dmask_d, writes=[Rc])
        kb.dma("sp", dmasks[:], dmasks_d, writes=[Rc])
        kb.dma("sp", kdt[:], kdt_d, writes=[Rc])
        kb.dma("sp", kds[:], kds_d, writes=[Rc])
        kb.dma("sp", invc[:], invc_d.partition_broadcast(128), writes=[Rc])
        kb.dma("sp", flagt[:], flag_d.partition_broadcast(128), writes=[Rc])
        kb.op("dve", lambda e: e.memset(onesb[:], 1.0), writes=[Rc])
        kb.op("dve", lambda e: e.memset(epst[:], EPS), writes=[Rc])

        stg = [carve(0, (128, D), F32), carve(8192, (128, D), F32)]
        Rstg = [res("stg0"), res("stg1")]
        stgi = {"i": 0}

        def tr_in(src_rows, nrows, dst_fn, rdst, width=D, evac="act"):
            i = stgi["i"] % 2
            stgi["i"] += 1
            kb.dma("sp", stg[i][0:nrows, 0:width], src_rows, writes=[Rstg[i]])
            for c4 in range(0, width // 128, 4):
                nch = min(4, width // 128 - c4)
                b = nbank()
                pv = pbank[b][:, :].rearrange("p (a b) -> p a b", a=4)
                kb.group("pe", [
                    (lambda e, k=k, c4=c4, i=i, pv=pv: e.transpose(pv[:, k, 0:nrows], stg[i][0:nrows, (c4 + k) * 128:(c4 + k + 1) * 128], identf[0:nrows, 0:nrows]))
                    for k in range(nch)], reads=[Rstg[i], Rc], writes=[Rp[b]])
                dst = dst_fn(c4, nch)
                if evac == "act":
                    kb.op("act", lambda e, dst=dst, pv=pv, nch=nch: e.activation(out=dst, in_=pv[:, 0:nch, 0:nrows], func=AF.Copy), reads=[Rp[b]], writes=[rdst])
                else:
                    kb.op("dve", lambda e, dst=dst, pv=pv, nch=nch: e.tensor_copy(out=dst, in_=pv[:, 0:nch, 0:nrows]), reads=[Rp[b]], writes=[rdst])

        def tr_out(src_fn, rsrc, ncols, nch_total, ostg, rostg, dst_rows):
            for c4 in range(0, nch_total, 4):
                nch = min(4, nch_total - c4)
                b = nbank()
                kb.group("pe", [
                    (lambda e, k=k, c4=c4, b=b: e.transpose(pbank[b][0:ncols, k * 128:(k + 1) * 128], src_fn(c4 + k), identf[:, :]))
                    for k in range(nch)], reads=[rsrc, Rc], writes=[Rp[b]])
                kb.op("act", lambda e, c4=c4, nch=nch, b=b: e.activation(out=ostg[0:ncols, c4 * 128:(c4 + nch) * 128], in_=pbank[b][0:ncols, 0:nch * 128], func=AF.Copy),
                      reads=[Rp[b]], writes=[rostg])
            kb.dma("sp", dst_rows, ostg[0:ncols, 0:nch_total * 128], reads=[rostg], writes=[res("outdram")])

        def wload(pieces):
            s = wslot()
            for dst_fn, src in pieces:
                kb.dma("pool", dst_fn(wsl[s]), src, writes=[Rw[s]])
            return s

        def wview_d(s, ncols):
            return wsl[s][:, 0:16 * ncols].rearrange("p (c f) -> p c f", c=16)

        def rmsnorm(gcol, c0, c1, sqv, rsq):
            for (t0, n) in ctiles(c0, c1):
                kb.op("act", lambda e, t0=t0, n=n: e.activation(out=sqv[:, :, 0:n], in_=xT[:, :, t0:t0 + n], func=AF.Square, scale=float(D) ** -0.5),
                      reads=[Rx], writes=[rsq])
                b = nbank()
                kb.group("pe", [(lambda e, c=c, n=n, b=b: e.matmul(pbank[b][:, 0:n], onesb[:, :], sqv[:, c, 0:n], start=(c == 0), stop=(c == DC - 1))) for c in range(DC)],
                         reads=[rsq, Rc], writes=[Rp[b]])
                kb.op("act", lambda e, n=n, b=b: e.activation(out=rstd[:, 0:n], in_=pbank[b][:, 0:n], func=AF.Sqrt, bias=epst[:, 0:1]),
                      reads=[Rp[b], Rc], writes=[Rrstd])
                kb.op("dve", lambda e, n=n: e.reciprocal(out=rstd[:, 0:n], in_=rstd[:, 0:n]), reads=[Rrstd], writes=[Rrstd])
                for c in range(DC):
                    kb.op("dve", lambda e, c=c, t0=t0, n=n: e.scalar_tensor_tensor(out=hT[:, c, t0:t0 + n], in0=xT[:, c, t0:t0 + n], scalar=gamT[:, gcol + c:gcol + c + 1],
                                                                                  in1=rstd[:, 0:n], op0=ALU.mult, op1=ALU.mult),
                          reads=[Rx, Rrstd, Rc], writes=[Rh])

        kb.scope("load")
        for i in range(1):
            kb.dma("sp", stg[0][:, 0:128], small, writes=[Rstg[0]])
            b = nbank()
            kb.op("pe", lambda e, b=b: e.transpose(pbank[b][:, 0:128], stg[0][:, 0:128], identf[:, :]), reads=[Rstg[0], Rc], writes=[Rp[b]])
            kb.op("act", lambda e, b=b: e.activation(out=gamT[:, :], in_=pbank[b][:, 0:128], func=AF.Copy), reads=[Rp[b]], writes=[Rc])
            for j in range(6):
                kb.dma("sp", stg[1][:, 0:128], convp[j * 128:(j + 1) * 128, :], writes=[Rstg[1]])
                b = nbank()
                kb.op("pe", lambda e, b=b: e.transpose(pbank[b][:, 0:128], stg[1][:, 0:128], identf[:, :]), reads=[Rstg[1], Rc], writes=[Rp[b]])
                kb.op("act", lambda e, b=b, j=j: e.activation(out=convT[:, j * 128:(j + 1) * 128], in_=pbank[b][:, 0:128], func=AF.Copy), reads=[Rp[b]], writes=[Rc])
        for (r0, n) in ctiles(0, NX, 128):
            tr_in(xin[r0:r0 + n, :], n, lambda c4, nch, r0=r0, n=n: xT[:, c4:c4 + nch, r0:r0 + n], Rx)
        kb.barrier()

        def tr_in2(stg_ap, rstg, src_rows, nrows, width, copies):
            kb.dma("sp", stg_ap[0:nrows, 0:width], src_rows, writes=[rstg])
            nch = width // 128
            b = nbank()
            pv = pbank[b][:, :].rearrange("p (a b) -> p a b", a=4)
            kb.group("pe", [(lambda e, k=k, pv=pv: e.transpose(pv[:, k, 0:nrows], stg_ap[0:nrows, k * 128:(k + 1) * 128], identf[0:nrows, 0:nrows])) for k in range(nch)],
                     reads=[rstg, Rc], writes=[Rp[b]])
            copies(pv, b)

        def sumsq_rstd(dst, rdst, c0, c1, sqv, rsq, step):
            for (t0, n) in ctiles(c0, c1, step):
                kb.op("act", lambda e, t0=t0, n=n: e.activation(out=sqv[:, :, 0:n], in_=xT[:, :, t0:t0 + n], func=AF.Square, scale=float(D) ** -0.5), reads=[Rx], writes=[rsq])
                b = nbank()
                kb.group("pe", [(lambda e, c=c, n=n, b=b: e.matmul(pbank[b][:, 0:n], onesb[:, :], sqv[:, c, 0:n], start=(c == 0), stop=(c == DC - 1))) for c in range(DC)],
                         reads=[rsq, Rc], writes=[Rp[b]])
                kb.op("act", lambda e, t0=t0, n=n, b=b: e.activation(out=dst[:, t0:t0 + n], in_=pbank[b][:, 0:n], func=AF.Sqrt, bias=epst[:, 0:1]), reads=[Rp[b], Rc], writes=[rdst])
            kb.op("dve", lambda e: e.reciprocal(out=dst[:, c0:c1], in_=dst[:, c0:c1]), reads=[rdst], writes=[rdst])

        kb.scope("pool")
        POOLW = (2, 4, 8, 16)
        L = S0
        ta = carve(0, (128, NX), F32)
        hf = carve(4480, (128, NX), F32)
        tb = carve(8960, (128, NX), F32)
        pp = carve(13440, (128, NX), F32)
        hs_g = carve(17920, (128, 4, NSQ, 19), F32)
        hst = [carve(22784, (128, NSQ, 19), F32), carve(24000, (128, NSQ, 19), F32)]
        fixt = carve(25216, (128, 16), F32)
        pend_g = carve(25280, (128, 4, 15), F32)
        hcmp_g = carve(25536, (128, 4, 120), F32)
        stg2 = [carve(27456, (128, 512), F32), carve(29504, (128, 512), F32)]
        ostg_g = carve(31552, (128, 512), F32)
        sqv = carve(33600, (128, DC, 64), BF16)
        Rta, Rhf, Rtb, Rpp, Rhsg, Rhst, Rfix, Rpend, Rhcmp, Rostg, Rsq = [res(n) for n in "ta hf tb pp hsg hst fix pend hcmp ostg sq".split()]
        Rstg2 = [res("stg2a"), res("stg2b")]
        sumsq_rstd(ta, Rta, 0, NX, sqv, Rsq, 64)
        for g in range(4):
            win = POOLW[g]
            steps = {2: 0, 4: 1, 8: 2, 16: 3}[win]
            for half in range(2):
                def cp(pv, b, half=half):
                    for k in range(4):
                        kb.op("act", lambda e, k=k, pv=pv: e.activation(out=hs_g[:, k, half * 8:(half + 1) * 8, 0:15], in_=pv[:, k, 0:120].rearrange("p (b c) -> p b c", b=8), func=AF.Copy),
                              reads=[Rp[b]], writes=[Rhsg])
                tr_in2(stg2[half], Rstg2[half], spool[half * 120:(half + 1) * 120, g * 512:(g + 1) * 512], 120, 512, cp)
            for k in range(4):
                c = 4 * g + k
                kb.op("dve", lambda e, c=c: e.scalar_tensor_tensor(out=hf[:, :], in0=xT[:, c, :], scalar=gamT[:, c:c + 1], in1=ta[:, :], op0=ALU.mult, op1=ALU.mult),
                      reads=[Rx, Rta, Rc], writes=[Rhf])
                kb.op("act", lambda e, k=k: e.activation(out=hs_g[:, k, :, 15:19], in_=hf[:, S0:NX].rearrange("p (b t) -> p b t", b=NSQ), func=AF.Copy), reads=[Rhf], writes=[Rhsg])
                kb.op("act", lambda e, k=k: e.activation(out=pend_g[:, k, :], in_=hf[:, S0 - 15:S0], func=AF.Copy), reads=[Rhf], writes=[Rpend])
                kb.op("dve", lambda e: e.tensor_tensor(out=tb[:, 1:L], in0=hf[:, 1:L], in1=hf[:, 0:L - 1], op=ALU.add), reads=[Rhf], writes=[Rtb])
                src, rsrc = tb, Rtb
                sh = 2
                for _ in range(steps):
                    dst, rdst = (pp, Rpp) if src is tb else (tb, Rtb)
                    kb.op("dve", lambda e, src=src, dst=dst, sh=sh: e.tensor_tensor(out=dst[:, 2 * sh - 1:L], in0=src[:, 2 * sh - 1:L], in1=src[:, sh - 1:L - sh], op=ALU.add),
                          reads=[rsrc], writes=[rdst])
                    src, rsrc = dst, rdst
                    sh *= 2
                kb.op("dve", lambda e, c=c, src=src, win=win: e.scalar_tensor_tensor(out=hT[:, c, 15:L], in0=src[:, 15:L], scalar=1.0 / win, in1=hf[:, 15:L], op0=ALU.mult, op1=ALU.subtract),
                      reads=[rsrc, Rhf], writes=[Rh])
                kb.op("dve", lambda e, src=src, g=g: e.tensor_tensor(out=fixt[:, 0:15], in0=src[:, P0:P0 + 15], in1=invc[:, g * 15:(g + 1) * 15], op=ALU.mult), reads=[rsrc, Rc], writes=[Rfix])
                kb.op("dve", lambda e, c=c: e.tensor_tensor(out=hT[:, c, P0:P0 + 15], in0=fixt[:, 0:15], in1=hf[:, P0:P0 + 15], op=ALU.subtract), reads=[Rfix, Rhf], writes=[Rh])
                hsrc = hs_g[:, k, :, :]
                kb.op("dve", lambda e, hsrc=hsrc: e.tensor_tensor(out=hst[0][:, :, 1:19], in0=hsrc[:, :, 1:19], in1=hsrc[:, :, 0:18], op=ALU.add), reads=[Rhsg], writes=[Rhst])
                a_i = 0
                sh = 2
                for _ in range(steps):
                    kb.op("dve", lambda e, a=a_i, sh=sh: e.tensor_tensor(out=hst[1 - a][:, :, 2 * sh - 1:19], in0=hst[a][:, :, 2 * sh - 1:19], in1=hst[a][:, :, sh - 1:19 - sh], op=ALU.add),
                          reads=[Rhst], writes=[Rhst])
                    a_i = 1 - a_i
                    sh *= 2
                kb.op("dve", lambda e, c=c, a=a_i, win=win, hsrc=hsrc: e.scalar_tensor_tensor(out=hT[:, c, S0:NX].rearrange("p (b t) -> p b t", b=NSQ), in0=hst[a][:, :, 15:19], scalar=1.0 / win,
                                                                                          in1=hsrc[:, :, 15:19], op0=ALU.mult, op1=ALU.subtract), reads=[Rhst, Rhsg], writes=[Rh])
            tr_out(lambda k: pend_g[:, k, :], Rpend, 15, 4, ostg_g, Rostg, poolp_o[:, g * 512:(g + 1) * 512])
            for half in range(2):
                for k in range(4):
                    kb.op("act", lambda e, k=k, half=half: e.activation(out=hcmp_g[:, k, :].rearrange("p (b t) -> p b t", b=8), in_=hs_g[:, k, half * 8:(half + 1) * 8, 4:19], func=AF.Copy),
                          reads=[Rhsg], writes=[Rhcmp])
                tr_out(lambda k: hcmp_g[:, k, :], Rhcmp, 120, 4, ostg_g, Rostg, pools_o[half * 120:(half + 1) * 120, g * 512:(g + 1) * 512])
            s_ = wload([(lambda sl: sl[:, 0:2048].rearrange("p (c f) -> p c f", c=4), pool_w[g].rearrange("(c p) f -> p c f", p=128))])
            wv = wsl[s_][:, 0:2048].rearrange("p (c f) -> p c f", c=4)
            for o in range(4):
                for (t0, n) in ctiles(15, NX):
                    b = nbank()
                    kb.group("pe", [(lambda e, ci=ci, o=o, t0=t0, n=n, b=b, wv=wv, g=g: e.matmul(pbank[b][:, 0:n], wv[:, ci, o * 128:(o + 1) * 128], hT[:, 4 * g + ci, t0:t0 + n], start=(ci == 0), stop=(ci == 3)))
                                    for ci in range(4)], reads=[Rw[s_], Rh], writes=[Rp[b]])
                    kb.op("dve", lambda e, o=o, t0=t0, n=n, b=b, g=g: e.scalar_tensor_tensor(out=xT[:, 4 * g + o, t0:t0 + n], in0=pbank[b][:, 0:n], scalar=gamT[:, 112 + 4 * g + o:113 + 4 * g + o],
                                                                                         in1=xT[:, 4 * g + o, t0:t0 + n], op0=ALU.mult, op1=ALU.add), reads=[Rp[b], Rx, Rc], writes=[Rx])
        kb.barrier()

        sq512 = carve(0, (128, DC, 512), BF16)
        Rsq5 = res("sq512")

        def conv_ffn(l):
            rmsnorm(32 + l * 16, 15, NX, sq512, Rsq5)
            kb.barrier()
            NU = NX - 15
            ub = [carve(0, (128, NU), F32), carve(4416, (128, NU), F32)]
            cbf = [carve(8832, (128, NTOK), F32), carve(8832 + 4352, (128, NTOK), F32)]
            sgb = carve(17536, (128, NTOK), F32)
            uext = carve(21888, (128, 2, NSQ, 6), F32)
            aT = [carve(22656, (128, 4, NTOK), BF16), carve(22656 + 8704, (128, 4, NTOK), BF16)]
            stg_s = [carve(40064, (128, 512), F32), carve(42112, (128, 512), F32)]
            uh_g = carve(44160, (128, 4, 32), F32)
            un_g = carve(44672, (128, 4, 34), F32)
            ostg_s = carve(45248, (128, 512), F32)
            Rub, Rcb = [res("ub0"), res("ub1")], [res("cb0"), res("cb1")]
            Rsg, Rue, RaT = res("sgb"), res("uext"), [res("aT0"), res("aT1")]
            Rss, Ruh, Run, Ros = [res("stgs0"), res("stgs1")], res("uhg"), res("ung"), res("ostgs")
            wupv = w_up[l].rearrange("(c p) f -> p c f", p=128)
            wdnv = w_down[l].rearrange("(k p) d -> p k d", p=128)
            for jj in range(22):
                sg_ = wload([(lambda sl: sl[:, :].rearrange("p (c f) -> p c f", c=16), wupv[:, :, jj * 256:(jj + 1) * 256])])
                su_ = wload([(lambda sl: sl[:, :].rearrange("p (c f) -> p c f", c=16), wupv[:, :, FF + jj * 256:FF + (jj + 1) * 256])])
                wg_v = wsl[sg_][:, :].rearrange("p (c f) -> p c f", c=16)
                wu_v = wsl[su_][:, :].rearrange("p (c f) -> p c f", c=16)
                for gu in range(2):
                    def cp(pv, b, gu=gu):
                        kb.op("act", lambda e, pv=pv: e.activation(out=uh_g[:, 2 * gu:2 * gu + 2, :], in_=pv[:, 0:2, 0:32], func=AF.Copy), reads=[Rp[b]], writes=[Ruh])
                    tr_in2(stg_s[gu], Rss[gu], sconv[l, :, gu * FF + jj * 256:gu * FF + (jj + 1) * 256], 32, 256, cp)
                a_i = (jj // 2) % 2
                kk0 = 2 * (jj % 2)
                for k in range(2):
                    j = 2 * jj + k
                    for gu, wv_ in ((0, wg_v), (1, wu_v)):
                        fch = gu * 44 + j
                        for (t0, n) in ctiles(15, NX):
                            b = nbank()
                            kb.group("pe", [(lambda e, c=c, k=k, t0=t0, n=n, b=b, wv_=wv_: e.matmul(pbank[b][:, 0:n], wv_[:, c, k * 128:(k + 1) * 128], hT[:, c, t0:t0 + n], start=(c == 0), stop=(c == DC - 1)))
                                            for c in range(DC)], reads=[Rw[sg_ if gu == 0 else su_], Rh], writes=[Rp[b]])
                            kb.op("act", lambda e, gu=gu, t0=t0, n=n, b=b: e.activation(out=ub[gu][:, t0 - 15:t0 - 15 + n], in_=pbank[b][:, 0:n], func=AF.Copy), reads=[Rp[b]], writes=[Rub[gu]])
                        cw = [convT[:, (l * 3 + q) * 88 + fch:(l * 3 + q) * 88 + fch + 1] for q in range(3)]
                        cbias = convT[:, 528 + l * 88 + fch:528 + l * 88 + fch + 1]
                        u = ub[gu]
                        kb.op("act", lambda e, gu=gu, u=u, cw=cw, cbias=cbias: e.activation(out=cbf[gu][:, 0:NPR], in_=u[:, 2:2 + NPR], func=AF.Identity, scale=cw[2], bias=cbias), reads=[Rub[gu], Rc], writes=[Rcb[gu]])
                        kb.op("dve", lambda e, gu=gu, u=u, cw=cw: e.scalar_tensor_tensor(out=cbf[gu][:, 0:NPR], in0=u[:, 1:1 + NPR], scalar=cw[1], in1=cbf[gu][:, 0:NPR], op0=ALU.mult, op1=ALU.add),
                              reads=[Rub[gu], Rcb[gu], Rc], writes=[Rcb[gu]])
                        kb.op("dve", lambda e, gu=gu, u=u, cw=cw: e.scalar_tensor_tensor(out=cbf[gu][:, 0:NPR], in0=u[:, 0:NPR], scalar=cw[0], in1=cbf[gu][:, 0:NPR], op0=ALU.mult, op1=ALU.add),
                              reads=[Rub[gu], Rcb[gu], Rc], writes=[Rcb[gu]])
                        ue = uext[:, gu, :, :]
                        kb.op("act", lambda e, ue=ue, gu=gu, k=k: e.activation(out=ue[:, :, 0:2], in_=uh_g[:, 2 * gu + k, :].rearrange("p (b t) -> p b t", b=NSQ), func=AF.Copy), reads=[Ruh], writes=[Rue])
                        kb.op("act", lambda e, ue=ue, u=u: e.activation(out=ue[:, :, 2:6], in_=u[:, 2 + NPR:2 + NPR + NS].rearrange("p (b t) -> p b t", b=NSQ), func=AF.Copy), reads=[Rub[gu]], writes=[Rue])
                        cs_ = cbf[gu][:, NPR:NTOK].rearrange("p (b t) -> p b t", b=NSQ)
                        kb.op("act", lambda e, ue=ue, cs_=cs_, cw=cw, cbias=cbias: e.activation(out=cs_, in_=ue[:, :, 2:6], func=AF.Identity, scale=cw[2], bias=cbias), reads=[Rue, Rc], writes=[Rcb[gu]])
                        kb.op("dve", lambda e, ue=ue, cs_=cs_, cw=cw: e.scalar_tensor_tensor(out=cs_, in0=ue[:, :, 1:5], scalar=cw[1], in1=cs_, op0=ALU.mult, op1=ALU.add), reads=[Rue, Rcb[gu], Rc], writes=[Rcb[gu]])
                        kb.op("dve", lambda e, ue=ue, cs_=cs_, cw=cw: e.scalar_tensor_tensor(out=cs_, in0=ue[:, :, 0:4], scalar=cw[0], in1=cs_, op0=ALU.mult, op1=ALU.add), reads=[Rue, Rcb[gu], Rc], writes=[Rcb[gu]])
                        kb.op("act", lambda e, u=u, gu=gu, k=k: e.activation(out=un_g[:, 2 * gu + k, 0:2], in_=u[:, NPR:NPR + 2], func=AF.Copy), reads=[Rub[gu]], writes=[Run])
                        kb.op("act", lambda e, ue=ue, gu=gu, k=k: e.activation(out=un_g[:, 2 * gu + k, 2:34].rearrange("p (b t) -> p b t", b=NSQ), in_=ue[:, :, 4:6], func=AF.Copy), reads=[Rue], writes=[Run])
                    kb.op("act", lambda e: e.activation(out=sgb[:, :], in_=cbf[0][:, :], func=AF.Silu), reads=[Rcb[0]], writes=[Rsg])
                    kb.op("dve", lambda e, a_i=a_i, k=k, kk0=kk0: e.tensor_tensor(out=aT[a_i][:, kk0 + k, :], in0=sgb[:, :], in1=cbf[1][:, :], op=ALU.mult), reads=[Rsg, Rcb[1]], writes=[RaT[a_i]])
                for gu in range(2):
                    b = nbank()
                    kb.group("pe", [(lambda e, k=k, gu=gu, b=b: e.transpose(pbank[b][0:34, k * 128:(k + 1) * 128], un_g[:, 2 * gu + k, :], identf[:, :])) for k in range(2)], reads=[Run, Rc], writes=[Rp[b]])
                    kb.op("act", lambda e, b=b: e.activation(out=ostg_s[0:34, 0:256], in_=pbank[b][0:34, 0:256], func=AF.Copy), reads=[Rp[b]], writes=[Ros])
                    kb.dma("sp", convp_o[l, :, gu * FF + jj * 256:gu * FF + (jj + 1) * 256], ostg_s[0:2, 0:256], reads=[Ros], writes=[res("outdram")])
                    kb.dma("sp", convs_o[l, :, gu * FF + jj * 256:gu * FF + (jj + 1) * 256], ostg_s[2:34, 0:256], reads=[Ros], writes=[res("outdram")])
                if jj % 2 == 0:
                    continue
                sds = []
                for hh in range(2):
                    sd_ = wload([(lambda sl: sl[:, :].rearrange("p (k d) -> p k d", k=2), wdnv[:, 2 * (jj - 1 + hh):2 * (jj - 1 + hh) + 2, :])])
                    sds.append((sd_, wsl[sd_][:, :].rearrange("p (k d) -> p k d", k=2)))
                for o in range(DC):
                    for (t0, n) in ctiles(0, NTOK):
                        b = nbank()
                        kb.group("pe", [(lambda e, k=k, o=o, t0=t0, n=n, b=b, a_i=a_i, sds=sds: e.matmul(pbank[b][:, 0:n], sds[k // 2][1][:, k % 2, o * 128:(o + 1) * 128], aT[a_i][:, k, t0:t0 + n], start=(k == 0), stop=(k == 3)))
                                        for k in range(4)], reads=[Rw[sds[0][0]], Rw[sds[1][0]], RaT[a_i]], writes=[Rp[b]])
                        kb.op("dve", lambda e, o=o, t0=t0, n=n, b=b: e.tensor_tensor(out=xT[:, o, P0 + t0:P0 + t0 + n], in0=pbank[b][:, 0:n], in1=xT[:, o, P0 + t0:P0 + t0 + n], op=ALU.add),
                              reads=[Rp[b], Rx], writes=[Rx])
            kb.barrier()

        def ple(l):
            rmsnorm(64 + l * 16, P0, NX, sq512, Rsq5)
            kb.barrier()
            pT = carve(0, (128, 2, NTOK), BF16)
            stg_p = [carve(4352, (128, 256), F32), carve(5376, (128, 256), F32)]
            tg = [carve(6400, (128, 512), F32), carve(8448, (128, 512), F32)]
            RpT, Rsp, Rtg = res("pT"), [res("stgp0"), res("stgp1")], [res("tg0"), res("tg1")]
            for i_, (r0, n) in enumerate(ctiles(0, NTOK, 128)):
                def cp(pv, b, r0=r0, n=n):
                    kb.op("act", lambda e, pv=pv: e.activation(out=pT[:, :, r0:r0 + n], in_=pv[:, 0:2, 0:n], func=AF.Copy), reads=[Rp[b]], writes=[RpT])
                tr_in2(stg_p[i_ % 2], Rsp[i_ % 2], pin[l, r0:r0 + n, :], n, 256, cp)
            wp_v = carve(10496, (128, 2, D), BF16)
            Rwp = res("wp_v")
            kb.dma("pool", wp_v, w_proj[l].rearrange("(k p) d -> p k d", p=128), writes=[Rwp])
            wgv = w_gate[l].rearrange("(c p) f -> p c f", p=128)
            it = 0
            for gq in range(8):
                sg_ = wload([(lambda sl: sl[:, :].rearrange("p (c f) -> p c f", c=16), wgv[:, :, gq * 256:(gq + 1) * 256])])
                wg_v = wsl[sg_][:, :].rearrange("p (c f) -> p c f", c=16)
                for o2 in range(2):
                    o = 2 * gq + o2
                    for (t0, n) in ctiles(0, NTOK):
                        b1 = nbank()
                        kb.group("pe", [(lambda e, c=c, o2=o2, t0=t0, n=n, b1=b1, wg_v=wg_v: e.matmul(pbank[b1][:, 0:n], wg_v[:, c, o2 * 128:(o2 + 1) * 128], hT[:, c, P0 + t0:P0 + t0 + n], start=(c == 0), stop=(c == DC - 1)))
                                        for c in range(DC)], reads=[Rw[sg_], Rh], writes=[Rp[b1]])
                        b2 = nbank()
                        kb.group("pe", [(lambda e, k=k, o=o, t0=t0, n=n, b2=b2: e.matmul(pbank[b2][:, 0:n], wp_v[:, k, o * 128:(o + 1) * 128], pT[:, k, t0:t0 + n], start=(k == 0), stop=(k == 1)))
                                        for k in range(2)], reads=[Rwp, RpT], writes=[Rp[b2]])
                        ti = it % 2
                        it += 1
                        kb.op("act", lambda e, n=n, b1=b1, ti=ti: e.activation(out=tg[ti][:, 0:n], in_=pbank[b1][:, 0:n], func=AF.Sigmoid), reads=[Rp[b1]], writes=[Rtg[ti]])
                        kb.op("dve", lambda e, n=n, b2=b2, ti=ti: e.tensor_tensor(out=tg[ti][:, 0:n], in0=pbank[b2][:, 0:n], in1=tg[ti][:, 0:n], op=ALU.mult), reads=[Rp[b2], Rtg[ti]], writes=[Rtg[ti]])
                        kb.op("dve", lambda e, o=o, t0=t0, n=n, ti=ti: e.tensor_tensor(out=xT[:, o, P0 + t0:P0 + t0 + n], in0=tg[ti][:, 0:n], in1=xT[:, o, P0 + t0:P0 + t0 + n], op=ALU.add),
                              reads=[Rtg[ti], Rx], writes=[Rx])
            kb.barrier()

        kb.scope("ffn0")
        conv_ffn(0)
        kb.scope("ple0")
        ple(0)
        if DEBUG:
            dbg0 = dout("dbg0", (NTOK, D))
            d_st = carve(0, (128, D), F32)
            for (r0, n) in ctiles(0, NTOK, 128):
                tr_out(lambda c, r0=r0, n=n: xT[:, c, P0 + r0:P0 + r0 + n], Rx, n, DC, d_st, res("dst_dbg"), dbg0[r0:r0 + n, :])
            kb.barrier()

        kb.scope("retnorm")
        LOGG = [float(np.log1p(-(2.0 ** (-5.0 - h)))) for h in range(NH)]
        xflat = xT[:, :, :].rearrange("p a b -> p (a b)")
        ypart = [nc.dram_tensor(f"ypart{h}", [128, DC * NTOK], F32).ap() for h in range(NH)]
        rmsnorm(16, P0, NX, sq512, Rsq5)
        Rxsp = res("xspill")
        kb.dma("sp", xspill, xflat, reads=[Rx], writes=[Rxsp])
        kb.barrier()

        def xr(off, shape, dt):
            return carve(off, shape, dt, base=xflat)
        qT, qcT, kT = xr(0, (128, 2, NTOK), BF16), xr(4352, (128, 2, NTOK), BF16), xr(8704, (128, 2, NTOK), BF16)
        kdtok = xr(13056, (128, 9, 256), BF16)
        vtok = xr(17664, (128, 9, 512), BF16)
        ogT = xr(26880, (128, 4, NTOK), BF16)
        crq = xr(35584, (128, NTOK), F32)
        Sst = xr(39936, (128, 2, 512), F32)
        Sbf = xr(44032, (128, 2, 512), BF16)
        Sin = xr(46080, (128, 2, 512), F32)
        ktok_s = xr(50176, (128, 256), BF16)
        kdsb = [xr(50688, (128, 256), BF16), xr(51200, (128, 256), BF16)]
        Ssm = [xr(51712 + 4096 * i, (128, 2, 512), F32) for i in range(3)]
        Ssb = [xr(64000 + 2048 * i, (128, 2, 512), BF16) for i in range(2)]
        scm = xr(68096, (128, 128), BF16)
        scms = xr(68352, (128, 64), BF16)
        T = [carve(2048 * i, (128, 512), F32) for i in range(3)]
        ob = carve(6144, (128, 4, 128), BF16)
        osq = carve(7168, (128, 4, 128), BF16)
        mu, msq, var = carve(8192, (128, 128), F32), carve(8704, (128, 128), F32), carve(9216, (128, 128), F32)
        sgt = [carve(10240, (128, 512), F32), carve(12288, (128, 512), F32)]
        yst = [carve(14336 + 2048 * i, (128, 512), F32) for i in range(3)]
        sgT = carve(26624, (128, 4, NTOK), BF16)
        RsgT = res("sgT")
        RqT, RqcT, RkT, Rkd, Rvt, Rog, Rcrq, RSst, RSbf, RSin, Rkts = [res(n) for n in "qT qcT kT kdtok vtok ogT crq Sst Sbf Sin ktoks".split()]
        Rkdsb, RSsm, RSsb = [res("kdsb0"), res("kdsb1")], [res(f"Ssm{i}") for i in range(3)], [res(f"Ssb{i}") for i in range(2)]
        Rscm, Rscms, RT = res("scm"), res("scms"), [res(f"T{i}") for i in range(3)]
        Rob, Rosq, Rmu, Rmsq, Rvar = res("ob"), res("osq"), res("mu"), res("msq"), res("var")
        Rsgt, Ryst = [res("sgt0"), res("sgt1")], [res(f"yst{i}") for i in range(3)]
        w_in_v = w_in.rearrange("(c p) f -> p c f", p=128)
        w_out_v = w_out.rearrange("(k p) d -> p k d", p=128)
        cnt = {"sg": 0, "ys": 0, "ss": 0, "sb": 0, "kd": 0}

        def wd16(col0):
            s_ = wload([(lambda sl: sl[:, :].rearrange("p (c f) -> p c f", c=16), w_in_v[:, :, col0:col0 + 256])])
            return s_, wsl[s_][:, :].rearrange("p (c f) -> p c f", c=16)

        def groupnorm(psb, bo, n, tok0):
            kb.op("act", lambda e: e.activation(out=ob[:, :, 0:n], in_=psb[:, :, 0:n], func=AF.Copy), reads=[Rp[bo]], writes=[Rob])
            kb.op("act", lambda e: e.activation(out=osq[:, :, 0:n], in_=psb[:, :, 0:n], func=AF.Square), reads=[Rp[bo]], writes=[Rosq])
            b = nbank()
            kb.group("pe", [(lambda e, v=v, b=b: e.matmul(pbank[b][:, 0:n], onesb[:, :], ob[:, v, 0:n], start=(v == 0), stop=(v == 3))) for v in range(4)] +
                     [(lambda e, v=v, b=b: e.matmul(pbank[b][:, 128:128 + n], onesb[:, :], osq[:, v, 0:n], start=(v == 0), stop=(v == 3))) for v in range(4)],
                     reads=[Rob, Rosq, Rc], writes=[Rp[b]])
            kb.op("dve", lambda e, b=b: e.tensor_scalar(out=mu[:, 0:n], in0=pbank[b][:, 0:n], scalar1=1.0 / DV, scalar2=None, op0=ALU.mult), reads=[Rp[b]], writes=[Rmu])
            kb.op("dve", lambda e: e.tensor_tensor(out=msq[:, 0:n], in0=mu[:, 0:n], in1=mu[:, 0:n], op=ALU.mult), reads=[Rmu], writes=[Rmsq])
            kb.op("dve", lambda e, b=b: e.scalar_tensor_tensor(out=var[:, 0:n], in0=pbank[b][:, 128:128 + n], scalar=1.0 / DV, in1=msq[:, 0:n], op0=ALU.mult, op1=ALU.subtract),
                  reads=[Rp[b], Rmsq], writes=[Rvar])
            kb.op("act", lambda e: e.activation(out=var[:, 0:n], in_=var[:, 0:n], func=AF.Sqrt, bias=epst[:, 0:1]), reads=[Rvar, Rc], writes=[Rvar])
            kb.op("dve", lambda e: e.reciprocal(out=var[:, 0:n], in_=var[:, 0:n]), reads=[Rvar], writes=[Rvar])
            kb.op("dve", lambda e: e.tensor_tensor(out=psb[:, :, 0:n], in0=psb[:, :, 0:n], in1=mu[:, 0:n].unsqueeze(1).to_broadcast([128, 4, n]), op=ALU.subtract),
                  reads=[Rp[bo], Rmu], writes=[Rp[bo]])
            kb.op("dve", lambda e: e.tensor_tensor(out=ogT[:, :, tok0:tok0 + n], in0=psb[:, :, 0:n], in1=var[:, 0:n].unsqueeze(1).to_broadcast([128, 4, n]), op=ALU.mult),
                  reads=[Rp[bo], Rvar], writes=[Rog])

        for h in range(NH):
            kb.scope(f"h{h}a_proj")
            g128 = float(np.exp(LOGG[h] * 128.0))
            g4 = float(np.exp(LOGG[h] * 4.0))
            kb.dma("sp", crq[:, :], crossq_d[h:h + 1, :].partition_broadcast(128), writes=[Rcrq])
            for which in range(2):
                s_, wv_ = wd16(which * 2048 + h * 256)
                for (t0, n) in ctiles(0, NTOK):
                    b1, b2 = nbank(), nbank()
                    for bb, half_ in ((b1, 0), (b2, 1)):
                        kb.group("pe", [(lambda e, c=c, bb=bb, half_=half_, t0=t0, n=n, wv_=wv_: e.matmul(pbank[bb][:, 0:n], wv_[:, c, half_ * 128:(half_ + 1) * 128], hT[:, c, P0 + t0:P0 + t0 + n], start=(c == 0), stop=(c == DC - 1)))
                                        for c in range(DC)], reads=[Rw[s_], Rh], writes=[Rp[bb]])
                    cosv, sinv = cst[:, 0, t0:t0 + n], cst[:, 1, t0:t0 + n]
                    dst = qT if which == 0 else kT
                    rdst = RqT if which == 0 else RkT
                    kb.op("dve", lambda e, b1=b1, n=n, cosv=cosv: e.tensor_tensor(out=T[0][:, 0:n], in0=pbank[b1][:, 0:n], in1=cosv, op=ALU.mult), reads=[Rp[b1], Rc], writes=[RT[0]])
                    kb.op("dve", lambda e, b2=b2, n=n, sinv=sinv: e.tensor_tensor(out=T[1][:, 0:n], in0=pbank[b2][:, 0:n], in1=sinv, op=ALU.mult), reads=[Rp[b2], Rc], writes=[RT[1]])
                    kb.op("dve", lambda e, n=n: e.tensor_tensor(out=T[0][:, 0:n], in0=T[0][:, 0:n], in1=T[1][:, 0:n], op=ALU.subtract), reads=[RT[0], RT[1]], writes=[RT[0]])
                    kb.op("act", lambda e, n=n, t0=t0, dst=dst: e.activation(out=dst[:, 0, t0:t0 + n], in_=T[0][:, 0:n], func=AF.Copy), reads=[RT[0]], writes=[rdst])
                    if which == 0:
                        kb.op("dve", lambda e, n=n, t0=t0: e.tensor_tensor(out=qcT[:, 0, t0:t0 + n], in0=T[0][:, 0:n], in1=crq[:, t0:t0 + n], op=ALU.mult), reads=[RT[0], Rcrq], writes=[RqcT])
                    kb.op("dve", lambda e, b1=b1, n=n, sinv=sinv: e.tensor_tensor(out=T[1][:, 0:n], in0=pbank[b1][:, 0:n], in1=sinv, op=ALU.mult), reads=[Rp[b1], Rc], writes=[RT[1]])
                    kb.op("dve", lambda e, b2=b2, n=n, cosv=cosv: e.tensor_tensor(out=T[2][:, 0:n], in0=pbank[b2][:, 0:n], in1=cosv, op=ALU.mult), reads=[Rp[b2], Rc], writes=[RT[2]])
                    kb.op("dve", lambda e, n=n: e.tensor_tensor(out=T[1][:, 0:n], in0=T[1][:, 0:n], in1=T[2][:, 0:n], op=ALU.add), reads=[RT[1], RT[2]], writes=[RT[1]])
                    kb.op("act", lambda e, n=n, t0=t0, dst=dst: e.activation(out=dst[:, 1, t0:t0 + n], in_=T[1][:, 0:n], func=AF.Copy), reads=[RT[1]], writes=[rdst])
                    if which == 0:
                        kb.op("dve", lambda e, n=n, t0=t0: e.tensor_tensor(out=qcT[:, 1, t0:t0 + n], in0=T[1][:, 0:n], in1=crq[:, t0:t0 + n], op=ALU.mult), reads=[RT[1], Rcrq], writes=[RqcT])
            sv = [wd16(4096 + h * 512 + hv * 256) for hv in range(2)]
            for i_, (r0, n) in enumerate(ctiles(0, NTOK, 128)):
                b = nbank()
                for hv in range(2):
                    s_, wv_ = sv[hv]
                    kb.group("pe", [(lambda e, c=c, b=b, hv=hv, r0=r0, n=n, wv_=wv_: e.matmul(pbank[b][0:n, hv * 256:(hv + 1) * 256], hT[:, c, P0 + r0:P0 + r0 + n], wv_[:, c, :], start=(c == 0), stop=(c == DC - 1)))
                                    for c in range(DC)], reads=[Rw[s_], Rh], writes=[Rp[b]])
                kb.op("act", lambda e, b=b, i_=i_, n=n: e.activation(out=vtok[0:n, i_, :], in_=pbank[b][0:n, :], func=AF.Copy), reads=[Rp[b]], writes=[Rvt])
            for i_, (r0, n) in enumerate(ctiles(0, NTOK, 128)):
                b = nbank()
                kb.group("pe", [(lambda e, ch=ch, b=b, r0=r0, n=n: e.matmul(pbank[b][0:n, ch * 128:(ch + 1) * 128], kT[:, ch, r0:r0 + n], identb[:, :], start=True, stop=True)) for ch in range(2)],
                         reads=[RkT, Rc], writes=[Rp[b]])
                if i_ < 8:
                    kb.op("act", lambda e, b=b, i_=i_, h=h: e.activation(out=kdtok[:, i_, :], in_=pbank[b][:, 0:256], func=AF.Copy, scale=kdt[:, h:h + 1]), reads=[Rp[b], Rc], writes=[Rkd])
                else:
                    kb.op("act", lambda e, b=b: e.activation(out=ktok_s[0:NS, :], in_=pbank[b][0:NS, 0:256], func=AF.Copy), reads=[Rp[b]], writes=[Rkts])

            def chain_step(j, Sdst, Rdst):
                bs = [nbank(), nbank()]
                for ch in range(2):
                    kb.op("pe", lambda e, ch=ch, j=j, bs=bs: e.matmul(pbank[bs[ch]][:, :], kdtok[:, j, ch * 128:(ch + 1) * 128], vtok[:, j, :], start=True, stop=True), reads=[Rkd, Rvt], writes=[Rp[bs[ch]]])
                    kb.op("dve", lambda e, ch=ch, bs=bs, g128=g128: e.scalar_tensor_tensor(out=Sdst[:, ch, :], in0=Sdst[:, ch, :], scalar=g128, in1=pbank[bs[ch]][:, :], op0=ALU.mult, op1=ALU.add),
                          reads=[Rdst, Rp[bs[ch]]], writes=[Rdst])

            kb.op("dve", lambda e: e.memset(Sst[:, :, :], 0.0), writes=[RSst])
            for j in range(8):
                chain_step(j, Sst, RSst)
            Rcci, Rcco = res(f"ccin{h}"), res(f"ccout{h}")
            kb.dma("sp", cc_in[h].ap().rearrange("(c p) v -> p c v", p=128), Sst[:, :, :], reads=[RSst], writes=[Rcci])
            kb.custom("pool", lambda e, h=h: e.collective_compute("AllGather", ALU.bypass, replica_groups=PAIRS, ins=[cc_in[h].ap().opt()], outs=[cc_out[h].ap().opt()]),
                      ccsem, reads=[Rcci], writes=[Rcco])
            kb.scope(f"h{h}b_sample_gate")
            gate_items = []
            gslots = [wd16(8192 + h * 512 + hv * 256) for hv in range(2)]
            for hv in range(2):
                for o2 in range(2):
                    for (t0, n) in ctiles(0, NTOK):
                        def item(hv=hv, o2=o2, t0=t0, n=n):
                            s_, wv_ = gslots[hv]
                            v = 2 * hv + o2
                            b = nbank()
                            kb.group("pe", [(lambda e, c=c, b=b, o2=o2, t0=t0, n=n, wv_=wv_: e.matmul(pbank[b][:, 0:n], wv_[:, c, o2 * 128:(o2 + 1) * 128], hT[:, c, P0 + t0:P0 + t0 + n], start=(c == 0), stop=(c == DC - 1)))
                                            for c in range(DC)], reads=[Rw[s_], Rh], writes=[Rp[b]])
                            kb.op("act", lambda e, b=b, n=n, v=v, t0=t0: e.activation(out=sgT[:, v, t0:t0 + n], in_=pbank[b][:, 0:n], func=AF.Silu), reads=[Rp[b]], writes=[RsgT])
                        gate_items.append(item)
            b = nbank()
            kb.group("pe", [(lambda e, ch=ch, b=b: e.matmul(pbank[b][0:NS, 0:NS], kT[:, ch, NPR:NTOK], qT[:, ch, NPR:NTOK], start=(ch == 0), stop=(ch == 1))) for ch in range(2)],
                     reads=[RkT, RqT], writes=[Rp[b]])
            kb.op("dve", lambda e, b=b, h=h: e.tensor_tensor(out=scms[0:NS, :], in0=pbank[b][0:NS, 0:NS], in1=dmasks[:, h, :], op=ALU.mult), reads=[Rp[b], Rc], writes=[Rscms])
            bo = 6
            psb = pbank[bo][:, :].rearrange("p (a b) -> p a b", a=4)
            kb.group("pe", [(lambda e, v=v, psb=psb: e.matmul(psb[:, v, 0:NS], vtok[0:NS, 8, v * 128:(v + 1) * 128], scms[0:NS, :], start=(v == 0), stop=False)) for v in range(4)],
                     reads=[Rvt, Rscms], writes=[Rp[bo]])
            ss_base = cnt["ss"]
            cnt["ss"] += NSQ

            def sload(bq_):
                si_ = (ss_base + bq_) % 3
                kb.dma("sp", Ssm[si_][:, :, :], sret[bq_, h].rearrange("(c p) v -> p c v", p=128), writes=[RSsm[si_]])
            sload(0)
            sload(1)
            for bq in range(NSQ):
                si = (ss_base + bq) % 3
                sbi = cnt["sb"] % 2
                cnt["sb"] += 1
                if bq + 2 < NSQ:
                    sload(bq + 2)
                kb.op("act", lambda e, si=si, sbi=sbi: e.activation(out=Ssb[sbi][:, :, :], in_=Ssm[si][:, :, :], func=AF.Copy), reads=[RSsm[si]], writes=[RSsb[sbi]])
                fns = []
                for v in range(4):
                    for ch in range(2):
                        fns.append(lambda e, v=v, ch=ch, psb=psb, sbi=sbi, bq=bq: e.matmul(psb[:, v, 4 * bq:4 * bq + 4], Ssb[sbi][:, ch, v * 128:(v + 1) * 128], qcT[:, ch, NPR + 4 * bq:NPR + 4 * bq + 4],
                                                                                         start=False, stop=(ch == 1 and bq == NSQ - 1)))
                kb.group("pe", fns, reads=[RSsb[sbi], RqcT], writes=[Rp[bo]])
                ki = cnt["kd"] % 2
                cnt["kd"] += 1
                kb.op("act", lambda e, ki=ki, h=h, bq=bq: e.activation(out=kdsb[ki][0:NS, :], in_=ktok_s[0:NS, :], func=AF.Copy, scale=kds[:, h, bq:bq + 1]), reads=[Rkts, Rc], writes=[Rkdsb[ki]])
                bs = [nbank(), nbank()]
                for ch in range(2):
                    kb.op("pe", lambda e, ch=ch, bs=bs, ki=ki: e.matmul(pbank[bs[ch]][:, :], kdsb[ki][0:NS, ch * 128:(ch + 1) * 128], vtok[0:NS, 8, :], start=True, stop=True),
                          reads=[Rkdsb[ki], Rvt], writes=[Rp[bs[ch]]])
                    kb.op("dve", lambda e, ch=ch, bs=bs, si=si, g4=g4: e.scalar_tensor_tensor(out=Ssm[si][:, ch, :], in0=Ssm[si][:, ch, :], scalar=g4, in1=pbank[bs[ch]][:, :], op0=ALU.mult, op1=ALU.add),
                          reads=[RSsm[si], Rp[bs[ch]]], writes=[RSsm[si]])
                kb.dma("sp", rets_o[bq, h].rearrange("(c p) v -> p c v", p=128), Ssm[si][:, :, :], reads=[RSsm[si]], writes=[res("outdram")])
                if gate_items:
                    gate_items.pop(0)()
            while gate_items:
                gate_items.pop(0)()
            groupnorm(psb, bo, NS, NPR)
            kb.dma("sp", Sin[:, :, :], cc_out[h].ap()[0:DK, :].rearrange("(c p) v -> p c v", p=128), reads=[Rcco], writes=[RSin])
            kb.scope(f"h{h}c_rec")
            kb.op("dve", lambda e: e.tensor_scalar(out=Sst[:, :, :], in0=Sin[:, :, :], scalar1=flagt[:, 0:1], scalar2=None, op0=ALU.mult), reads=[RSin, Rc], writes=[RSst])
            for j in range(8):
                tj = 128 * j
                kb.op("act", lambda e: e.activation(out=Sbf[:, :, :], in_=Sst[:, :, :], func=AF.Copy), reads=[RSst], writes=[RSbf])
                b = nbank()
                kb.group("pe", [(lambda e, ch=ch, b=b, tj=tj: e.matmul(pbank[b][:, 0:128], kT[:, ch, tj:tj + 128], qT[:, ch, tj:tj + 128], start=(ch == 0), stop=(ch == 1))) for ch in range(2)],
                         reads=[RkT, RqT], writes=[Rp[b]])
                kb.op("dve", lambda e, b=b, h=h: e.tensor_tensor(out=scm[:, :], in0=pbank[b][:, 0:128], in1=dmask[:, h, :], op=ALU.mult), reads=[Rp[b], Rc], writes=[Rscm])
                bo = nbank()
                psb = pbank[bo][:, :].rearrange("p (a b) -> p a b", a=4)
                fns = []
                for v in range(4):
                    fns.append(lambda e, v=v, psb=psb, j=j: e.matmul(psb[:, v, :], vtok[:, j, v * 128:(v + 1) * 128], scm[:, :], start=True, stop=False))
                    for ch in range(2):
                        fns.append(lambda e, v=v, ch=ch, psb=psb, tj=tj: e.matmul(psb[:, v, :], Sbf[:, ch, v * 128:(v + 1) * 128], qcT[:, ch, tj:tj + 128], start=False, stop=(ch == 1)))
                kb.group("pe", fns, reads=[Rvt, Rscm, RSbf, RqcT], writes=[Rp[bo]])
                chain_step(j, Sst, RSst)
                groupnorm(psb, bo, 128, tj)
            kb.dma("sp", retp_o[h].rearrange("(c p) v -> p c v", p=128), Sst[:, :, :], reads=[RSst], writes=[res("outdram")])
            kb.op("dve", lambda e: e.tensor_tensor(out=ogT[:, :, :], in0=ogT[:, :, :], in1=sgT[:, :, :], op=ALU.mult), reads=[Rog, RsgT], writes=[Rog])
            kb.scope(f"h{h}d_wout")
            so = []
            for hv in range(2):
                s_ = wload([(lambda sl: sl[:, :].rearrange("p (k d) -> p k d", k=2), w_out_v[:, 4 * h + 2 * hv:4 * h + 2 * hv + 2, :])])
                so.append((s_, wsl[s_][:, :].rearrange("p (k d) -> p k d", k=2)))
            Ryp = res(f"ypart{h}")
            for o in range(DC):
                for (t0, n) in ctiles(0, NTOK):
                    b = nbank()
                    kb.group("pe", [(lambda e, v=v, b=b, o=o, t0=t0, n=n, so=so: e.matmul(pbank[b][:, 0:n], so[v // 2][1][:, v % 2, o * 128:(o + 1) * 128], ogT[:, v, t0:t0 + n], start=(v == 0), stop=(v == 3)))
                                    for v in range(4)], reads=[Rw[so[0][0]], Rw[so[1][0]], Rog], writes=[Rp[b]])
                    yi = cnt["ys"] % 3
                    cnt["ys"] += 1
                    kb.op("act", lambda e, b=b, n=n, yi=yi: e.activation(out=yst[yi][:, 0:n], in_=pbank[b][:, 0:n], func=AF.Copy), reads=[Rp[b]], writes=[Ryst[yi]])
                    kb.dma("sp", ypart[h][:, o * NTOK + t0:o * NTOK + t0 + n], yst[yi][:, 0:n], reads=[Ryst[yi]], writes=[Ryp])
        kb.barrier()
        kb.scope("restore")
        kb.dma("sp", xflat, xspill, reads=[Rxsp], writes=[Rx])
        yrow = [carve(26624 + 8704 * i, (128, 2, NTOK), F32) for i in range(2)]
        Ryrow = [res("yrow0"), res("yrow1")]
        it_ = 0
        for h in range(NH):
            for o in range(0, DC, 2):
                yi = it_ % 2
                it_ += 1
                kb.dma("sp", yrow[yi][:, :, :], ypart[h][:, o * NTOK:(o + 2) * NTOK].rearrange("p (a b) -> p a b", a=2), reads=[res(f"ypart{h}")], writes=[Ryrow[yi]])
                kb.op("dve", lambda e, o=o, yi=yi: e.tensor_tensor(out=xT[:, o:o + 2, P0:NX], in0=xT[:, o:o + 2, P0:NX], in1=yrow[yi][:, :, :], op=ALU.add),
                      reads=[Rx, Ryrow[yi]], writes=[Rx])
        hx = carve(20480, (128, DC, 2), F32)
        Rhx, Rcxi, Rcxo = res("hx"), res("cxin"), res("cxout")
        cxi = nc.dram_tensor("cxi", [128, 2 * DC], F32)
        cxo = nc.dram_tensor("cxo", [256, 2 * DC], F32)
        kb.op("act", lambda e: e.activation(out=hx[:, :, :], in_=xT[:, :, S0 - 2:S0], func=AF.Copy), reads=[Rx], writes=[Rhx])
        kb.dma("sp", cxi.ap(), hx[:, :, :].rearrange("p a b -> p (a b)"), reads=[Rhx], writes=[Rcxi])
        kb.custom("pool", lambda e: e.collective_compute("AllGather", ALU.bypass, replica_groups=PAIRS, ins=[cxi.ap().opt()], outs=[cxo.ap().opt()]), ccsem, reads=[Rcxi], writes=[Rcxo])
        kb.dma("sp", hx[:, :, :].rearrange("p a b -> p (a b)"), cxo.ap()[0:128, :], reads=[Rcxo], writes=[Rhx])
        kb.op("dve", lambda e: e.tensor_scalar(out=xT[:, :, 15:17], in0=hx[:, :, :], scalar1=flagt[:, 0:1], scalar2=None, op0=ALU.mult), reads=[Rhx, Rc, Rx], writes=[Rx])
        kb.barrier()
        kb.scope("ffn1")
        conv_ffn(1)
        kb.scope("ple1")
        ple(1)

        kb.scope("final")
        ta2 = carve(16384, (128, NX), F32)
        yT = carve(20864, (128, DC, 128), F32)
        ostg_y = carve(29056, (128, D), F32)
        Rta2, RyT, Rosy = res("ta2"), res("yT"), res("ostgy")
        sqv2 = carve(0, (128, DC, 512), BF16)
        sumsq_rstd(ta2, Rta2, P0, NX, sqv2, Rsq5, 512)
        for (r0, n) in ctiles(0, NTOK, 128):
            for c in range(DC):
                kb.op("dve", lambda e, c=c, r0=r0, n=n: e.scalar_tensor_tensor(out=yT[:, c, 0:n], in0=xT[:, c, P0 + r0:P0 + r0 + n], scalar=gamT[:, 96 + c:97 + c], in1=ta2[:, P0 + r0:P0 + r0 + n],
                                                                              op0=ALU.mult, op1=ALU.mult), reads=[Rx, Rta2, Rc], writes=[RyT])
            tr_out(lambda c, n=n: yT[:, c, 0:n], RyT, n, DC, ostg_y, Rosy, y_o[r0:r0 + n, :])
        kb.wait_all("sp", [res("outdram")])
        kb.barrier()
        with nc.Block() as block:
            kb.replay(block)
    return nc


def _tables(half):
    log_g = np.log1p(-(2.0 ** (-5.0 - np.arange(NH, dtype=np.float64))))
    halfd = DK // 2
    inv = 10000.0 ** (-np.arange(halfd, dtype=np.float64) / halfd)
    pos = np.concatenate([half * NPR + np.arange(NPR), np.tile(16384 + np.arange(4), NSQ)]).astype(np.float64)
    ang = inv[:, None] * pos[None, :]
    cs = np.stack([np.cos(ang), np.sin(ang)], axis=1).astype(np.float32)
    n_in = np.concatenate([np.arange(NPR) % 128, np.tile(np.arange(4), NSQ)]).astype(np.float64)
    crossq = np.exp(log_g[:, None] * (n_in[None, :] + 1.0)).astype(np.float32)
    m = np.arange(128)
    dm = np.where(m[None, :, None] <= m[None, None, :], np.exp(log_g[:, None, None] * np.maximum(m[None, None, :] - m[None, :, None], 0)), 0.0) * DK ** -0.5
    dmask = np.ascontiguousarray(dm.transpose(1, 0, 2)).astype(np.float32)
    ms = np.arange(64)
    same = (ms[:, None] // 4) == (ms[None, :] // 4)
    dms = np.where(same[None] & (ms[None, :, None] <= ms[None, None, :]), np.exp(log_g[:, None, None] * np.maximum(ms[None, None, :] - ms[None, :, None], 0)), 0.0) * DK ** -0.5
    dmasks = np.ascontiguousarray(dms.transpose(1, 0, 2)).astype(np.float32)
    kdt = (np.exp(log_g[None, :] * (127.0 - m[:, None])) * DK ** -0.5).astype(np.float32)
    kds = np.zeros((64, NH, NSQ), np.float32)
    for t in range(64):
        kds[t, :, t // 4] = np.exp(log_g * (3.0 - (t % 4))) * DK ** -0.5
    invc = np.zeros((4, 15), np.float32)
    for gi, w in enumerate((2, 4, 8, 16)):
        p = np.arange(15)
        invc[gi] = 1.0 / np.minimum(p + 1, w) if half == 0 else 1.0 / w
    flagv = np.concatenate([[float(half)], float(half) * np.exp(log_g * 1024.0)]).astype(np.float32)[None, :]
    return dict(cs=cs, crossq=crossq, dmask=dmask, dmasks=dmasks, kdt=kdt, kds=kds, invc=invc.reshape(1, 60), flagv=flagv)


_NC_CACHE = {}


def kernel(x_prompt, x_sample, p_prompt, p_sample, state_pool, state_ret, state_conv,
           norm_mix, norm_ffn, norm_ple, norm_final, pool_w, pool_scale, ret_w_in, ret_w_out,
           ffn_w_up, ffn_conv_w, ffn_conv_b, ffn_w_down, ple_w_proj, ple_w_gate):
    f32 = lambda a: np.ascontiguousarray(np.asarray(a, dtype=np.float32))
    x_prompt, x_sample, p_prompt, p_sample = map(f32, (x_prompt, x_sample, p_prompt, p_sample))
    state_pool, state_ret, state_conv = map(f32, (state_pool, state_ret, state_conv))
    if "nc" not in _NC_CACHE:
        _NC_CACHE["nc"] = build_program()
    nc = _NC_CACHE["nc"]
    small = np.concatenate([f32(norm_mix).reshape(32, 128), f32(norm_ffn).reshape(32, 128), f32(norm_ple).reshape(32, 128),
                            f32(norm_final).reshape(16, 128), f32(pool_scale).reshape(16, 128)], axis=0)
    convp = np.zeros((768, 128), np.float32)
    convp[0:528] = f32(ffn_conv_w).reshape(528, 128)
    convp[528:704] = f32(ffn_conv_b).reshape(176, 128)
    shared = dict(small=small, convp=convp, identd=np.eye(128, dtype=np.float32), pool_w=f32(pool_w)[0],
                  ret_w_in=f32(ret_w_in)[0], ret_w_out=f32(ret_w_out)[0], ffn_w_up=f32(ffn_w_up), ffn_w_down=f32(ffn_w_down),
                  ple_w_proj=f32(ple_w_proj), ple_w_gate=f32(ple_w_gate))
    in_maps = []
    for core in range(8):
        s, half = core // 2, core % 2
        halo = x_prompt[s, NPR - HALO:NPR] if half else np.zeros((HALO, D), np.float32)
        sl = slice(NSQ * core, NSQ * (core + 1))
        xin = np.concatenate([halo, x_prompt[s, half * NPR:(half + 1) * NPR], x_sample[sl].reshape(NS, D)], axis=0)
        pin = np.concatenate([p_prompt[:, s, half * NPR:(half + 1) * NPR], p_sample[:, sl].reshape(2, NS, 256)], axis=1)
        m = dict(shared)
        m.update(xin=np.ascontiguousarray(xin), pin=np.ascontiguousarray(pin), spool=state_pool[0, sl].reshape(240, D),
                 sret=state_ret[0, sl], sconv=np.ascontiguousarray(state_conv[:, sl].reshape(2, 32, F2)))
        m.update(_tables(half))
        in_maps.append(m)
    res = run_bass_kernel_spmd(nc, in_maps, core_ids=list(range(8)), **({"trace": True} if PROFILE else {}))
    r = res.results
    if PROFILE:
        _NC_CACHE["prof"] = res
    if DEBUG:
        _NC_CACHE["dbg"] = r
    y_prompt = np.stack([np.concatenate([r[2 * s]["y"][0:NPR], r[2 * s + 1]["y"][0:NPR]], axis=0) for s in range(4)])
    y_sample = np.concatenate([r[c]["y"][NPR:].reshape(NSQ, 4, D) for c in range(8)], axis=0)
    pool_p = np.stack([r[2 * s + 1]["pool_p"] for s in range(4)])[None]
    pool_s = np.concatenate([r[c]["pool_s"].reshape(NSQ, 15, D) for c in range(8)], axis=0)[None]
    ret_p = np.stack([r[2 * s + 1]["ret_p"] for s in range(4)])[None]
    ret_s = np.concatenate([r[c]["ret_s"] for c in range(8)], axis=0)[None]
    conv_p = np.stack([r[2 * s + 1]["conv_p"] for s in range(4)], axis=1)
    conv_s = np.concatenate([r[c]["conv_s"].reshape(2, NSQ, 2, F2) for c in range(8)], axis=1)
    return (y_prompt, y_sample, pool_p, pool_s, ret_p, ret_s, conv_p, conv_s)
```

```python
import contextlib
import numpy as np
import concourse.bass as bass
import concourse.mybir as mybir
from concourse.bass_utils import run_bass_kernel_spmd

F32 = mybir.dt.float32
BF16 = mybir.dt.bfloat16
AF = mybir.ActivationFunctionType
ALU = mybir.AluOpType

D = 2048
DC = 16
FF = 5632
F2 = 11264
NH = 8
DK = 256
DV = 512
NPR = 1024
NSQ = 16
NS = 64
HALO = 17
P0 = HALO
S0 = P0 + NPR
NX = S0 + NS
NTOK = NPR + NS
EPS = 1e-6
DEBUG = False
PROFILE = False
ENGS = ("pe", "act", "dve", "pool", "sp")
PAIRS = [[0, 1], [2, 3], [4, 5], [6, 7]]


class Res:
    __slots__ = ("name", "w", "r")

    def __init__(self, name):
        self.name = name
        self.w = None
        self.r = []


class KB:
    def __init__(self, nc, stack, n_sp=24, n_pool=8):
        self.nc = nc
        self.ops = {e: [] for e in ENGS}
        self.sem = {}
        self.cnt = {}
        for e in ("pe", "act", "dve", "pool"):
            self.sem[e] = stack.enter_context(nc.semaphore("prog_" + e))
            self.cnt[e] = 0
        self.dsems = {}
        for q, n in (("sp", n_sp), ("pool", n_pool)):
            self.dsems[q] = [[stack.enter_context(nc.semaphore(f"d_{q}_{i}")), 0] for i in range(n)]
        self.dnext = {"sp": 0, "pool": 0}
        self.waited = {e: {} for e in ENGS}
        self.semobj = {}
        self.csems = []

    def _collect(self, eng, reads, writes):
        need = {}

        def add(tok, same_ok):
            if tok is None:
                return
            key, val = tok
            if same_ok and key == eng:
                return
            if need.get(key, 0) < val:
                need[key] = val
        for r in reads:
            add(r.w, False)
        for w in writes:
            add(w.w, True)
            for t in w.r:
                add(t, True)
        out = []
        for key, val in need.items():
            if self.waited[eng].get(key, 0) >= val:
                continue
            self.waited[eng][key] = val
            out.append((key, val))
        return out

    def _emit_waits(self, eng, waits):
        for key, val in waits:
            semh = self.sem[key] if isinstance(key, str) else self.semobj[key]
            self.ops[eng].append(("wait", semh, val))

    def _commit(self, tok, reads, writes):
        for r in reads:
            r.r.append(tok)
        for w in writes:
            w.w = tok
            w.r = []

    def op(self, eng, fn, reads=(), writes=()):
        self._emit_waits(eng, self._collect(eng, reads, writes))
        self.cnt[eng] += 1
        tok = (eng, self.cnt[eng])
        self.ops[eng].append(("ins", fn, self.sem[eng], 1))
        self._commit(tok, reads, writes)
        return tok

    def group(self, eng, fns, reads=(), writes=()):
        self._emit_waits(eng, self._collect(eng, reads, writes))
        for fn in fns[:-1]:
            self.ops[eng].append(("ins", fn, None, 0))
        self.cnt[eng] += 1
        tok = (eng, self.cnt[eng])
        self.ops[eng].append(("ins", fns[-1], self.sem[eng], 1))
        self._commit(tok, reads, writes)
        return tok

    def dma(self, q, out, in_, reads=(), writes=()):
        ring = self.dsems[q]
        i = self.dnext[q]
        self.dnext[q] = (i + 1) % len(ring)
        ent = ring[i]
        semh = ent[0]
        key = ("d", q, i)
        self.semobj[key] = semh
        if ent[1] > 0 and self.waited[q].get(key, 0) < ent[1]:
            self.waited[q][key] = ent[1]
            self.ops[q].append(("wait", semh, ent[1]))
        self._emit_waits(q, self._collect(q, reads, writes))
        ent[1] += 16
        tok = (key, ent[1])
        self.ops[q].append(("dma", out, in_, semh))
        self._commit(tok, reads, writes)
        return tok

    def custom(self, q, fn, cs, reads=(), writes=()):
        self._emit_waits(q, self._collect(q, reads, writes))
        key = ("c", id(cs))
        self.semobj[key] = cs[0]
        cs[1] += 1
        tok = (key, cs[1])
        self.ops[q].append(("ins", fn, cs[0], 1))
        self._commit(tok, reads, writes)
        return tok

    def barrier(self):
        toks = [(e, self.cnt[e]) for e in ("pe", "act", "dve", "pool") if self.cnt[e] > 0]
        for q, ring in self.dsems.items():
            for i, ent in enumerate(ring):
                if ent[1] > 0:
                    key = ("d", q, i)
                    self.semobj[key] = ent[0]
                    toks.append((key, ent[1]))
        for e in ENGS:
            for key, val in toks:
                if key == e:
                    continue
                if self.waited[e].get(key, 0) >= val:
                    continue
                self.waited[e][key] = val
                semh = self.sem[key] if isinstance(key, str) else self.semobj[key]
                self.ops[e].append(("wait", semh, val))

    def scope(self, name):
        for e in ENGS:
            self.ops[e].append(("scope", name))

    def wait_all(self, q, resources):
        self._emit_waits(q, self._collect(q, resources, resources))

    def replay(self, block):
        names = {"pe": "tensor", "act": "scalar", "dve": "vector", "pool": "gpsimd", "sp": "sync"}

        def make(e):
            def run(eng):
                cur = None
                for item in self.ops[e]:
                    k = item[0]
                    if k == "scope":
                        if not PROFILE:
                            continue
                        if cur is not None:
                            cur.__exit__(None, None, None)
                        cur = self.nc.named_scope(item[1])
                        cur.__enter__()
                        continue
                    if k == "wait":
                        eng.wait_ge(item[1], item[2])
                    elif k == "ins":
                        ins = item[1](eng)
                        if item[2] is not None:
                            ins.then_inc(item[2], item[3])
                    else:
                        eng.dma_start(out=item[1], in_=item[2]).then_inc(item[3], 16)
                if cur is not None:
                    cur.__exit__(None, None, None)
            return run
        for e in ENGS:
            getattr(block, names[e])(make(e))


def ctiles(c0, c1, step=512):
    out = []
    if step == 512:
        nt = -(-(c1 - c0) // step)
        base, rem = divmod(c1 - c0, nt)
        for i in range(nt):
            n = base + (1 if i < rem else 0)
            out.append((c0, n))
            c0 += n
        return out
    while c0 < c1:
        n = min(step, c1 - c0)
        out.append((c0, n))
        c0 += n
    return out


def build_program():
    nc = bass.Bass("TRN2", target_bir_lowering=False)

    def din(name, shape):
        return nc.dram_tensor(name, list(shape), F32, kind="ExternalInput").ap()

    def dout(name, shape):
        return nc.dram_tensor(name, list(shape), F32, kind="ExternalOutput").ap()

    xin = din("xin", (NX, D))
    pin = din("pin", (2, NTOK, 256))
    spool = din("spool", (240, D))
    sret = din("sret", (NSQ, NH, DK, DV))
    sconv = din("sconv", (2, 32, F2))
    small = din("small", (128, 128))
    convp = din("convp", (768, 128))
    identd = din("identd", (128, 128))
    pool_w = din("pool_w", (4, 512, 512))
    w_in = din("ret_w_in", (D, 12288))
    w_out = din("ret_w_out", (4096, D))
    w_up = din("ffn_w_up", (2, D, F2))
    w_down = din("ffn_w_down", (2, FF, D))
    w_proj = din("ple_w_proj", (2, 256, D))
    w_gate = din("ple_w_gate", (2, D, D))
    cs_d = din("cs", (128, 2, NTOK))
    crossq_d = din("crossq", (NH, NTOK))
    dmask_d = din("dmask", (128, NH, 128))
    dmasks_d = din("dmasks", (64, NH, 64))
    kdt_d = din("kdt", (128, NH))
    kds_d = din("kds", (64, NH, NSQ))
    invc_d = din("invc", (1, 60))
    flag_d = din("flagv", (1, 1 + NH))

    y_o = dout("y", (NTOK, D))
    poolp_o = dout("pool_p", (15, D))
    pools_o = dout("pool_s", (240, D))
    retp_o = dout("ret_p", (NH, DK, DV))
    rets_o = dout("ret_s", (NSQ, NH, DK, DV))
    convp_o = dout("conv_p", (2, 2, F2))
    convs_o = dout("conv_s", (2, 32, F2))

    xspill = nc.dram_tensor("xspill", [128, DC * NX], F32).ap()
    cc_in = [nc.dram_tensor(f"cc_in{h}", [DK, DV], F32) for h in range(NH)]
    cc_out = [nc.dram_tensor(f"cc_out{h}", [2 * DK, DV], F32) for h in range(NH)]
    cx_in = nc.dram_tensor("cx_in", [2, D], F32)
    cx_out = nc.dram_tensor("cx_out", [4, D], F32)

    with contextlib.ExitStack() as st:
        E = st.enter_context
        kb = KB(nc, st)
        ccsem = [E(nc.semaphore("ccsem")), 0]

        def sb(name, shape, dt):
            return E(nc.sbuf_tensor("sb_" + name, list(shape), dt))

        xT = sb("xT", (128, DC, NX), F32)
        hT = sb("hT", (128, DC, NX), BF16)
        NWS = 4
        wsl = [sb(f"wsl{i}", (128, 4096), BF16) for i in range(NWS)]
        gamT = sb("gamT", (128, 128), F32)
        convT = sb("convT", (128, 768), F32)
        identf = sb("identf", (128, 128), F32)
        identb = sb("identb", (128, 128), BF16)
        onesb = sb("onesb", (128, 128), BF16)
        epst = sb("epst", (128, 1), F32)
        flagt = sb("flagt", (128, 1 + NH), F32)
        cst = sb("cst", (128, 2, NTOK), F32)
        dmask = sb("dmask", (128, NH, 128), F32)
        dmasks = sb("dmasks", (64, NH, 64), F32)
        kdt = sb("kdt", (128, NH), F32)
        kds = sb("kds", (64, NH, NSQ), F32)
        invc = sb("invc", (128, 60), F32)
        rstd = sb("rstd", (128, 512), F32)
        SCRW = 12544
        scr = sb("scr", (128, SCRW), F32)
        pbank = [E(nc.psum_tensor(f"pb{i}", [128, 512], F32)) for i in range(8)]

        R = {}

        def res(name):
            if name not in R:
                R[name] = Res(name)
            return R[name]

        Rx, Rh = res("xT"), res("hT")
        Rw = [res(f"w{i}") for i in range(NWS)]
        Rp = [res(f"pb{i}") for i in range(8)]
        Rc = res("consts")
        Rrstd = res("rstd")
        Rscr = {}

        def carve(off_b, shape, dt, base=None):
            t = scr if base is None else base
            n = int(np.prod(shape[1:]))
            nb = n * (4 if dt == F32 else 2)
            assert off_b % 4 == 0 and nb % 4 == 0
            ap = t[:, off_b // 4: off_b // 4 + nb // 4]
            if dt != F32:
                ap = ap.bitcast(dt)
            if len(shape) == 3:
                ap = ap.rearrange("p (a b) -> p a b", a=shape[1])
            elif len(shape) == 4:
                ap = ap.rearrange("p (a b c) -> p a b c", a=shape[1], b=shape[2])
            return ap[0:shape[0]]

        state = {"bank": 0, "ws": 0}

        def nbank(lo=0, hi=6):
            b = lo + state["bank"] % (hi - lo)
            state["bank"] += 1
            return b

        def wslot():
            s = state["ws"] % NWS
            state["ws"] += 1
            return s

        kb.dma("sp", identf[:], identd, writes=[Rc])
        kb.dma("pool", identb[:], identd, writes=[Rc])
        kb.dma("sp", cst[:], cs_d, writes=[Rc])
        kb.dma("sp", dmask[:], dmask_d, writes=[Rc])
        kb.dma("sp", dmasks[:], dmasks_d, writes=[Rc])
        kb.dma("sp", kdt[:], kdt_d, writes=[Rc])
        kb.dma("sp", kds[:], kds_d, writes=[Rc])
        kb.dma("sp", invc[:], invc_d.partition_broadcast(128), writes=[Rc])
        kb.dma("sp", flagt[:], flag_d.partition_broadcast(128), writes=[Rc])
        kb.op("dve", lambda e: e.memset(onesb[:], 1.0), writes=[Rc])
        kb.op("dve", lambda e: e.memset(epst[:], EPS), writes=[Rc])

        stg = [carve(0, (128, D), F32), carve(8192, (128, D), F32)]
        Rstg = [res("stg0"), res("stg1")]
        stgi = {"i": 0}

        def tr_in(src_rows, nrows, dst_fn, rdst, width=D, evac="act"):
            i = stgi["i"] % 2
            stgi["i"] += 1
            kb.dma("sp", stg[i][0:nrows, 0:width], src_rows, writes=[Rstg[i]])
            for c4 in range(0, width // 128, 4):
                nch = min(4, width // 128 - c4)
                b = nbank()
                pv = pbank[b][:, :].rearrange("p (a b) -> p a b", a=4)
                kb.group("pe", [
                    (lambda e, k=k, c4=c4, i=i, pv=pv: e.transpose(pv[:, k, 0:nrows], stg[i][0:nrows, (c4 + k) * 128:(c4 + k + 1) * 128], identf[0:nrows, 0:nrows]))
                    for k in range(nch)], reads=[Rstg[i], Rc], writes=[Rp[b]])
                dst = dst_fn(c4, nch)
                if evac == "act":
                    kb.op("act", lambda e, dst=dst, pv=pv, nch=nch: e.activation(out=dst, in_=pv[:, 0:nch, 0:nrows], func=AF.Copy), reads=[Rp[b]], writes=[rdst])
                else:
                    kb.op("dve", lambda e, dst=dst, pv=pv, nch=nch: e.tensor_copy(out=dst, in_=pv[:, 0:nch, 0:nrows]), reads=[Rp[b]], writes=[rdst])

        def tr_out(src_fn, rsrc, ncols, nch_total, ostg, rostg, dst_rows):
            for c4 in range(0, nch_total, 4):
                nch = min(4, nch_total - c4)
                b = nbank()
                kb.group("pe", [
                    (lambda e, k=k, c4=c4, b=b: e.transpose(pbank[b][0:ncols, k * 128:(k + 1) * 128], src_fn(c4 + k), identf[:, :]))
                    for k in range(nch)], reads=[rsrc, Rc], writes=[Rp[b]])
                kb.op("act", lambda e, c4=c4, nch=nch, b=b: e.activation(out=ostg[0:ncols, c4 * 128:(c4 + nch) * 128], in_=pbank[b][0:ncols, 0:nch * 128], func=AF.Copy),
                      reads=[Rp[b]], writes=[rostg])
            kb.dma("sp", dst_rows, ostg[0:ncols, 0:nch_total * 128], reads=[rostg], writes=[res("outdram")])

        def wload(pieces):
            s = wslot()
            for dst_fn, src in pieces:
                kb.dma("pool", dst_fn(wsl[s]), src, writes=[Rw[s]])
            return s

        def wview_d(s, ncols):
            return wsl[s][:, 0:16 * ncols].rearrange("p (c f) -> p c f", c=16)

        def rmsnorm(gcol, c0, c1, sqv, rsq):
            for (t0, n) in ctiles(c0, c1):
                kb.op("act", lambda e, t0=t0, n=n: e.activation(out=sqv[:, :, 0:n], in_=xT[:, :, t0:t0 + n], func=AF.Square, scale=float(D) ** -0.5),
                      reads=[Rx], writes=[rsq])
                b = nbank()
                kb.group("pe", [(lambda e, c=c, n=n, b=b: e.matmul(pbank[b][:, 0:n], onesb[:, :], sqv[:, c, 0:n], start=(c == 0), stop=(c == DC - 1))) for c in range(DC)],
                         reads=[rsq, Rc], writes=[Rp[b]])
                kb.op("act", lambda e, n=n, b=b: e.activation(out=rstd[:, 0:n], in_=pbank[b][:, 0:n], func=AF.Sqrt, bias=epst[:, 0:1]),
                      reads=[Rp[b], Rc], writes=[Rrstd])
                kb.op("dve", lambda e, n=n: e.reciprocal(out=rstd[:, 0:n], in_=rstd[:, 0:n]), reads=[Rrstd], writes=[Rrstd])
                for c in range(DC):
                    kb.op("dve", lambda e, c=c, t0=t0, n=n: e.scalar_tensor_tensor(out=hT[:, c, t0:t0 + n], in0=xT[:, c, t0:t0 + n], scalar=gamT[:, gcol + c:gcol + c + 1],
                                                                                  in1=rstd[:, 0:n], op0=ALU.mult, op1=ALU.mult),
                          reads=[Rx, Rrstd, Rc], writes=[Rh])

        kb.scope("load")
        for i in range(1):
            kb.dma("sp", stg[0][:, 0:128], small, writes=[Rstg[0]])
            b = nbank()
            kb.op("pe", lambda e, b=b: e.transpose(pbank[b][:, 0:128], stg[0][:, 0:128], identf[:, :]), reads=[Rstg[0], Rc], writes=[Rp[b]])
            kb.op("act", lambda e, b=b: e.activation(out=gamT[:, :], in_=pbank[b][:, 0:128], func=AF.Copy), reads=[Rp[b]], writes=[Rc])
            for j in range(6):
                kb.dma("sp", stg[1][:, 0:128], convp[j * 128:(j + 1) * 128, :], writes=[Rstg[1]])
                b = nbank()
                kb.op("pe", lambda e, b=b: e.transpose(pbank[b][:, 0:128], stg[1][:, 0:128], identf[:, :]), reads=[Rstg[1], Rc], writes=[Rp[b]])
                kb.op("act", lambda e, b=b, j=j: e.activation(out=convT[:, j * 128:(j + 1) * 128], in_=pbank[b][:, 0:128], func=AF.Copy), reads=[Rp[b]], writes=[Rc])
        for (r0, n) in ctiles(0, NX, 128):
            tr_in(xin[r0:r0 + n, :], n, lambda c4, nch, r0=r0, n=n: xT[:, c4:c4 + nch, r0:r0 + n], Rx)
        kb.barrier()

        def tr_in2(stg_ap, rstg, src_rows, nrows, width, copies):
            kb.dma("sp", stg_ap[0:nrows, 0:width], src_rows, writes=[rstg])
            nch = width // 128
            b = nbank()
            pv = pbank[b][:, :].rearrange("p (a b) -> p a b", a=4)
            kb.group("pe", [(lambda e, k=k, pv=pv: e.transpose(pv[:, k, 0:nrows], stg_ap[0:nrows, k * 128:(k + 1) * 128], identf[0:nrows, 0:nrows])) for k in range(nch)],
                     reads=[rstg, Rc], writes=[Rp[b]])
            copies(pv, b)

        def sumsq_rstd(dst, rdst, c0, c1, sqv, rsq, step):
            for (t0, n) in ctiles(c0, c1, step):
                kb.op("act", lambda e, t0=t0, n=n: e.activation(out=sqv[:, :, 0:n], in_=xT[:, :, t0:t0 + n], func=AF.Square, scale=float(D) ** -0.5), reads=[Rx], writes=[rsq])
                b = nbank()
                kb.group("pe", [(lambda e, c=c, n=n, b=b: e.matmul(pbank[b][:, 0:n], onesb[:, :], sqv[:, c, 0:n], start=(c == 0), stop=(c == DC - 1))) for c in range(DC)],
                         reads=[rsq, Rc], writes=[Rp[b]])
                kb.op("act", lambda e, t0=t0, n=n, b=b: e.activation(out=dst[:, t0:t0 + n], in_=pbank[b][:, 0:n], func=AF.Sqrt, bias=epst[:, 0:1]), reads=[Rp[b], Rc], writes=[rdst])
            kb.op("dve", lambda e: e.reciprocal(out=dst[:, c0:c1], in_=dst[:, c0:c1]), reads=[rdst], writes=[rdst])

        kb.scope("pool")
        POOLW = (2, 4, 8, 16)
        L = S0
        ta = carve(0, (128, NX), F32)
        hf = carve(4480, (128, NX), F32)
        tb = carve(8960, (128, NX), F32)
        pp = carve(13440, (128, NX), F32)
        hs_g = carve(17920, (128, 4, NSQ, 19), F32)
        hst = [carve(22784, (128, NSQ, 19), F32), carve(24000, (128, NSQ, 19), F32)]
        fixt = carve(25216, (128, 16), F32)
        pend_g = carve(25280, (128, 4, 15), F32)
        hcmp_g = carve(25536, (128, 4, 120), F32)
        stg2 = [carve(27456, (128, 512), F32), carve(29504, (128, 512), F32)]
        ostg_g = carve(31552, (128, 512), F32)
        sqv = carve(33600, (128, DC, 64), BF16)
        Rta, Rhf, Rtb, Rpp, Rhsg, Rhst, Rfix, Rpend, Rhcmp, Rostg, Rsq = [res(n) for n in "ta hf tb pp hsg hst fix pend hcmp ostg sq".split()]
        Rstg2 = [res("stg2a"), res("stg2b")]
        sumsq_rstd(ta, Rta, 0, NX, sqv, Rsq, 64)
        for g in range(4):
            win = POOLW[g]
            steps = {2: 0, 4: 1, 8: 2, 16: 3}[win]
            for half in range(2):
                def cp(pv, b, half=half):
                    for k in range(4):
                        kb.op("act", lambda e, k=k, pv=pv: e.activation(out=hs_g[:, k, half * 8:(half + 1) * 8, 0:15], in_=pv[:, k, 0:120].rearrange("p (b c) -> p b c", b=8), func=AF.Copy),
                              reads=[Rp[b]], writes=[Rhsg])
                tr_in2(stg2[half], Rstg2[half], spool[half * 120:(half + 1) * 120, g * 512:(g + 1) * 512], 120, 512, cp)
            for k in range(4):
                c = 4 * g + k
                kb.op("dve", lambda e, c=c: e.scalar_tensor_tensor(out=hf[:, :], in0=xT[:, c, :], scalar=gamT[:, c:c + 1], in1=ta[:, :], op0=ALU.mult, op1=ALU.mult),
                      reads=[Rx, Rta, Rc], writes=[Rhf])
                kb.op("act", lambda e, k=k: e.activation(out=hs_g[:, k, :, 15:19], in_=hf[:, S0:NX].rearrange("p (b t) -> p b t", b=NSQ), func=AF.Copy), reads=[Rhf], writes=[Rhsg])
                kb.op("act", lambda e, k=k: e.activation(out=pend_g[:, k, :], in_=hf[:, S0 - 15:S0], func=AF.Copy), reads=[Rhf], writes=[Rpend])
                kb.op("dve", lambda e: e.tensor_tensor(out=tb[:, 1:L], in0=hf[:, 1:L], in1=hf[:, 0:L - 1], op=ALU.add), reads=[Rhf], writes=[Rtb])
                src, rsrc = tb, Rtb
                sh = 2
                for _ in range(steps):
                    dst, rdst = (pp, Rpp) if src is tb else (tb, Rtb)
                    kb.op("dve", lambda e, src=src, dst=dst, sh=sh: e.tensor_tensor(out=dst[:, 2 * sh - 1:L], in0=src[:, 2 * sh - 1:L], in1=src[:, sh - 1:L - sh], op=ALU.add),
                          reads=[rsrc], writes=[rdst])
                    src, rsrc = dst, rdst
                    sh *= 2
                kb.op("dve", lambda e, c=c, src=src, win=win: e.scalar_tensor_tensor(out=hT[:, c, 15:L], in0=src[:, 15:L], scalar=1.0 / win, in1=hf[:, 15:L], op0=ALU.mult, op1=ALU.subtract),
                      reads=[rsrc, Rhf], writes=[Rh])
                kb.op("dve", lambda e, src=src, g=g: e.tensor_tensor(out=fixt[:, 0:15], in0=src[:, P0:P0 + 15], in1=invc[:, g * 15:(g + 1) * 15], op=ALU.mult), reads=[rsrc, Rc], writes=[Rfix])
                kb.op("dve", lambda e, c=c: e.tensor_tensor(out=hT[:, c, P0:P0 + 15], in0=fixt[:, 0:15], in1=hf[:, P0:P0 + 15], op=ALU.subtract), reads=[Rfix, Rhf], writes=[Rh])
                hsrc = hs_g[:, k, :, :]
                kb.op("dve", lambda e, hsrc=hsrc: e.tensor_tensor(out=hst[0][:, :, 1:19], in0=hsrc[:, :, 1:19], in1=hsrc[:, :, 0:18], op=ALU.add), reads=[Rhsg], writes=[Rhst])
                a_i = 0
                sh = 2
                for _ in range(steps):
                    kb.op("dve", lambda e, a=a_i, sh=sh: e.tensor_tensor(out=hst[1 - a][:, :, 2 * sh - 1:19], in0=hst[a][:, :, 2 * sh - 1:19], in1=hst[a][:, :, sh - 1:19 - sh], op=ALU.add),
                          reads=[Rhst], writes=[Rhst])
                    a_i = 1 - a_i
                    sh *= 2
                kb.op("dve", lambda e, c=c, a=a_i, win=win, hsrc=hsrc: e.scalar_tensor_tensor(out=hT[:, c, S0:NX].rearrange("p (b t) -> p b t", b=NSQ), in0=hst[a][:, :, 15:19], scalar=1.0 / win,
                                                                                          in1=hsrc[:, :, 15:19], op0=ALU.mult, op1=ALU.subtract), reads=[Rhst, Rhsg], writes=[Rh])
            tr_out(lambda k: pend_g[:, k, :], Rpend, 15, 4, ostg_g, Rostg, poolp_o[:, g * 512:(g + 1) * 512])
            for half in range(2):
                for k in range(4):
                    kb.op("act", lambda e, k=k, half=half: e.activation(out=hcmp_g[:, k, :].rearrange("p (b t) -> p b t", b=8), in_=hs_g[:, k, half * 8:(half + 1) * 8, 4:19], func=AF.Copy),
                          reads=[Rhsg], writes=[Rhcmp])
                tr_out(lambda k: hcmp_g[:, k, :], Rhcmp, 120, 4, ostg_g, Rostg, pools_o[half * 120:(half + 1) * 120, g * 512:(g + 1) * 512])
            s_ = wload([(lambda sl: sl[:, 0:2048].rearrange("p (c f) -> p c f", c=4), pool_w[g].rearrange("(c p) f -> p c f", p=128))])
            wv = wsl[s_][:, 0:2048].rearrange("p (c f) -> p c f", c=4)
            for o in range(4):
                for (t0, n) in ctiles(15, NX):
                    b = nbank()
                    kb.group("pe", [(lambda e, ci=ci, o=o, t0=t0, n=n, b=b, wv=wv, g=g: e.matmul(pbank[b][:, 0:n], wv[:, ci, o * 128:(o + 1) * 128], hT[:, 4 * g + ci, t0:t0 + n], start=(ci == 0), stop=(ci == 3)))
                                    for ci in range(4)], reads=[Rw[s_], Rh], writes=[Rp[b]])
                    kb.op("dve", lambda e, o=o, t0=t0, n=n, b=b, g=g: e.scalar_tensor_tensor(out=xT[:, 4 * g + o, t0:t0 + n], in0=pbank[b][:, 0:n], scalar=gamT[:, 112 + 4 * g + o:113 + 4 * g + o],
                                                                                         in1=xT[:, 4 * g + o, t0:t0 + n], op0=ALU.mult, op1=ALU.add), reads=[Rp[b], Rx, Rc], writes=[Rx])
        kb.barrier()

        sq512 = carve(0, (128, DC, 512), BF16)
        Rsq5 = res("sq512")

        def conv_ffn(l):
            rmsnorm(32 + l * 16, 15, NX, sq512, Rsq5)
            kb.barrier()
            NU = NX - 15
            ub = [carve(0, (128, NU), F32), carve(4416, (128, NU), F32)]
            cbf = [carve(8832, (128, NTOK), F32), carve(8832 + 4352, (128, NTOK), F32)]
            sgb = carve(17536, (128, NTOK), F32)
            uext = carve(21888, (128, 2, NSQ, 6), F32)
            aT = [carve(22656, (128, 4, NTOK), BF16), carve(22656 + 8704, (128, 4, NTOK), BF16)]
            stg_s = [carve(40064, (128, 512), F32), carve(42112, (128, 512), F32)]
            uh_g = carve(44160, (128, 4, 32), F32)
            un_g = carve(44672, (128, 4, 34), F32)
            ostg_s = carve(45248, (128, 512), F32)
            Rub, Rcb = [res("ub0"), res("ub1")], [res("cb0"), res("cb1")]
            Rsg, Rue, RaT = res("sgb"), res("uext"), [res("aT0"), res("aT1")]
            Rss, Ruh, Run, Ros = [res("stgs0"), res("stgs1")], res("uhg"), res("ung"), res("ostgs")
            wupv = w_up[l].rearrange("(c p) f -> p c f", p=128)
            wdnv = w_down[l].rearrange("(k p) d -> p k d", p=128)
            for jj in range(22):
                sg_ = wload([(lambda sl: sl[:, :].rearrange("p (c f) -> p c f", c=16), wupv[:, :, jj * 256:(jj + 1) * 256])])
                su_ = wload([(lambda sl: sl[:, :].rearrange("p (c f) -> p c f", c=16), wupv[:, :, FF + jj * 256:FF + (jj + 1) * 256])])
                wg_v = wsl[sg_][:, :].rearrange("p (c f) -> p c f", c=16)
                wu_v = wsl[su_][:, :].rearrange("p (c f) -> p c f", c=16)
                for gu in range(2):
                    def cp(pv, b, gu=gu):
                        kb.op("act", lambda e, pv=pv: e.activation(out=uh_g[:, 2 * gu:2 * gu + 2, :], in_=pv[:, 0:2, 0:32], func=AF.Copy), reads=[Rp[b]], writes=[Ruh])
                    tr_in2(stg_s[gu], Rss[gu], sconv[l, :, gu * FF + jj * 256:gu * FF + (jj + 1) * 256], 32, 256, cp)
                a_i = (jj // 2) % 2
                kk0 = 2 * (jj % 2)
                for k in range(2):
                    j = 2 * jj + k
                    for gu, wv_ in ((0, wg_v), (1, wu_v)):
                        fch = gu * 44 + j
                        for (t0, n) in ctiles(15, NX):
                            b = nbank()
                            kb.group("pe", [(lambda e, c=c, k=k, t0=t0, n=n, b=b, wv_=wv_: e.matmul(pbank[b][:, 0:n], wv_[:, c, k * 128:(k + 1) * 128], hT[:, c, t0:t0 + n], start=(c == 0), stop=(c == DC - 1)))
                                            for c in range(DC)], reads=[Rw[sg_ if gu == 0 else su_], Rh], writes=[Rp[b]])
                            kb.op("act", lambda e, gu=gu, t0=t0, n=n, b=b: e.activation(out=ub[gu][:, t0 - 15:t0 - 15 + n], in_=pbank[b][:, 0:n], func=AF.Copy), reads=[Rp[b]], writes=[Rub[gu]])
                        cw = [convT[:, (l * 3 + q) * 88 + fch:(l * 3 + q) * 88 + fch + 1] for q in range(3)]
                        cbias = convT[:, 528 + l * 88 + fch:528 + l * 88 + fch + 1]
                        u = ub[gu]
                        kb.op("act", lambda e, gu=gu, u=u, cw=cw, cbias=cbias: e.activation(out=cbf[gu][:, 0:NPR], in_=u[:, 2:2 + NPR], func=AF.Identity, scale=cw[2], bias=cbias), reads=[Rub[gu], Rc], writes=[Rcb[gu]])
                        kb.op("dve", lambda e, gu=gu, u=u, cw=cw: e.scalar_tensor_tensor(out=cbf[gu][:, 0:NPR], in0=u[:, 1:1 + NPR], scalar=cw[1], in1=cbf[gu][:, 0:NPR], op0=ALU.mult, op1=ALU.add),
                              reads=[Rub[gu], Rcb[gu], Rc], writes=[Rcb[gu]])
                        kb.op("dve", lambda e, gu=gu, u=u, cw=cw: e.scalar_tensor_tensor(out=cbf[gu][:, 0:NPR], in0=u[:, 0:NPR], scalar=cw[0], in1=cbf[gu][:, 0:NPR], op0=ALU.mult, op1=ALU.add),
                              reads=[Rub[gu], Rcb[gu], Rc], writes=[Rcb[gu]])
                        ue = uext[:, gu, :, :]
                        kb.op("act", lambda e, ue=ue, gu=gu, k=k: e.activation(out=ue[:, :, 0:2], in_=uh_g[:, 2 * gu + k, :].rearrange("p (b t) -> p b t", b=NSQ), func=AF.Copy), reads=[Ruh], writes=[Rue])
                        kb.op("act", lambda e, ue=ue, u=u: e.activation(out=ue[:, :, 2:6], in_=u[:, 2 + NPR:2 + NPR + NS].rearrange("p (b t) -> p b t", b=NSQ), func=AF.Copy), reads=[Rub[gu]], writes=[Rue])
                        cs_ = cbf[gu][:, NPR:NTOK].rearrange("p (b t) -> p b t", b=NSQ)
                        kb.op("act", lambda e, ue=ue, cs_=cs_, cw=cw, cbias=cbias: e.activation(out=cs_, in_=ue[:, :, 2:6], func=AF.Identity, scale=cw[2], bias=cbias), reads=[Rue, Rc], writes=[Rcb[gu]])
                        kb.op("dve", lambda e, ue=ue, cs_=cs_, cw=cw: e.scalar_tensor_tensor(out=cs_, in0=ue[:, :, 1:5], scalar=cw[1], in1=cs_, op0=ALU.mult, op1=ALU.add), reads=[Rue, Rcb[gu], Rc], writes=[Rcb[gu]])
                        kb.op("dve", lambda e, ue=ue, cs_=cs_, cw=cw: e.scalar_tensor_tensor(out=cs_, in0=ue[:, :, 0:4], scalar=cw[0], in1=cs_, op0=ALU.mult, op1=ALU.add), reads=[Rue, Rcb[gu], Rc], writes=[Rcb[gu]])
                        kb.op("act", lambda e, u=u, gu=gu, k=k: e.activation(out=un_g[:, 2 * gu + k, 0:2], in_=u[:, NPR:NPR + 2], func=AF.Copy), reads=[Rub[gu]], writes=[Run])
                        kb.op("act", lambda e, ue=ue, gu=gu, k=k: e.activation(out=un_g[:, 2 * gu + k, 2:34].rearrange("p (b t) -> p b t", b=NSQ), in_=ue[:, :, 4:6], func=AF.Copy), reads=[Rue], writes=[Run])
                    kb.op("act", lambda e: e.activation(out=sgb[:, :], in_=cbf[0][:, :], func=AF.Silu), reads=[Rcb[0]], writes=[Rsg])
                    kb.op("dve", lambda e, a_i=a_i, k=k, kk0=kk0: e.tensor_tensor(out=aT[a_i][:, kk0 + k, :], in0=sgb[:, :], in1=cbf[1][:, :], op=ALU.mult), reads=[Rsg, Rcb[1]], writes=[RaT[a_i]])
                for gu in range(2):
                    b = nbank()
                    kb.group("pe", [(lambda e, k=k, gu=gu, b=b: e.transpose(pbank[b][0:34, k * 128:(k + 1) * 128], un_g[:, 2 * gu + k, :], identf[:, :])) for k in range(2)], reads=[Run, Rc], writes=[Rp[b]])
                    kb.op("act", lambda e, b=b: e.activation(out=ostg_s[0:34, 0:256], in_=pbank[b][0:34, 0:256], func=AF.Copy), reads=[Rp[b]], writes=[Ros])
                    kb.dma("sp", convp_o[l, :, gu * FF + jj * 256:gu * FF + (jj + 1) * 256], ostg_s[0:2, 0:256], reads=[Ros], writes=[res("outdram")])
                    kb.dma("sp", convs_o[l, :, gu * FF + jj * 256:gu * FF + (jj + 1) * 256], ostg_s[2:34, 0:256], reads=[Ros], writes=[res("outdram")])
                if jj % 2 == 0:
                    continue
                sds = []
                for hh in range(2):
                    sd_ = wload([(lambda sl: sl[:, :].rearrange("p (k d) -> p k d", k=2), wdnv[:, 2 * (jj - 1 + hh):2 * (jj - 1 + hh) + 2, :])])
                    sds.append((sd_, wsl[sd_][:, :].rearrange("p (k d) -> p k d", k=2)))
                for o in range(DC):
                    for (t0, n) in ctiles(0, NTOK):
                        b = nbank()
                        kb.group("pe", [(lambda e, k=k, o=o, t0=t0, n=n, b=b, a_i=a_i, sds=sds: e.matmul(pbank[b][:, 0:n], sds[k // 2][1][:, k % 2, o * 128:(o + 1) * 128], aT[a_i][:, k, t0:t0 + n], start=(k == 0), stop=(k == 3)))
                                        for k in range(4)], reads=[Rw[sds[0][0]], Rw[sds[1][0]], RaT[a_i]], writes=[Rp[b]])
                        kb.op("dve", lambda e, o=o, t0=t0, n=n, b=b: e.tensor_tensor(out=xT[:, o, P0 + t0:P0 + t0 + n], in0=pbank[b][:, 0:n], in1=xT[:, o, P0 + t0:P0 + t0 + n], op=ALU.add),
                              reads=[Rp[b], Rx], writes=[Rx])
            kb.barrier()

        def ple(l):
            rmsnorm(64 + l * 16, P0, NX, sq512, Rsq5)
            kb.barrier()
            pT = carve(0, (128, 2, NTOK), BF16)
            stg_p = [carve(4352, (128, 256), F32), carve(5376, (128, 256), F32)]
            tg = [carve(6400, (128, 512), F32), carve(8448, (128, 512), F32)]
            RpT, Rsp, Rtg = res("pT"), [res("stgp0"), res("stgp1")], [res("tg0"), res("tg1")]
            for i_, (r0, n) in enumerate(ctiles(0, NTOK, 128)):
                def cp(pv, b, r0=r0, n=n):
                    kb.op("act", lambda e, pv=pv: e.activation(out=pT[:, :, r0:r0 + n], in_=pv[:, 0:2, 0:n], func=AF.Copy), reads=[Rp[b]], writes=[RpT])
                tr_in2(stg_p[i_ % 2], Rsp[i_ % 2], pin[l, r0:r0 + n, :], n, 256, cp)
            wp_v = carve(10496, (128, 2, D), BF16)
            Rwp = res("wp_v")
            kb.dma("pool", wp_v, w_proj[l].rearrange("(k p) d -> p k d", p=128), writes=[Rwp])
            wgv = w_gate[l].rearrange("(c p) f -> p c f", p=128)
            it = 0
            for gq in range(8):
                sg_ = wload([(lambda sl: sl[:, :].rearrange("p (c f) -> p c f", c=16), wgv[:, :, gq * 256:(gq + 1) * 256])])
                wg_v = wsl[sg_][:, :].rearrange("p (c f) -> p c f", c=16)
                for o2 in range(2):
                    o = 2 * gq + o2
                    for (t0, n) in ctiles(0, NTOK):
                        b1 = nbank()
                        kb.group("pe", [(lambda e, c=c, o2=o2, t0=t0, n=n, b1=b1, wg_v=wg_v: e.matmul(pbank[b1][:, 0:n], wg_v[:, c, o2 * 128:(o2 + 1) * 128], hT[:, c, P0 + t0:P0 + t0 + n], start=(c == 0), stop=(c == DC - 1)))
                                        for c in range(DC)], reads=[Rw[sg_], Rh], writes=[Rp[b1]])
                        b2 = nbank()
                        kb.group("pe", [(lambda e, k=k, o=o, t0=t0, n=n, b2=b2: e.matmul(pbank[b2][:, 0:n], wp_v[:, k, o * 128:(o + 1) * 128], pT[:, k, t0:t0 + n], start=(k == 0), stop=(k == 1)))
                                        for k in range(2)], reads=[Rwp, RpT], writes=[Rp[b2]])
                        ti = it % 2
                        it += 1
                        kb.op("act", lambda e, n=n, b1=b1, ti=ti: e.activation(out=tg[ti][:, 0:n], in_=pbank[b1][:, 0:n], func=AF.Sigmoid), reads=[Rp[b1]], writes=[Rtg[ti]])
                        kb.op("dve", lambda e, n=n, b2=b2, ti=ti: e.tensor_tensor(out=tg[ti][:, 0:n], in0=pbank[b2][:, 0:n], in1=tg[ti][:, 0:n], op=ALU.mult), reads=[Rp[b2], Rtg[ti]], writes=[Rtg[ti]])
                        kb.op("dve", lambda e, o=o, t0=t0, n=n, ti=ti: e.tensor_tensor(out=xT[:, o, P0 + t0:P0 + t0 + n], in0=tg[ti][:, 0:n], in1=xT[:, o, P0 + t0:P0 + t0 + n], op=ALU.add),
                              reads=[Rtg[ti], Rx], writes=[Rx])
            kb.barrier()

        kb.scope("ffn0")
        conv_ffn(0)
        kb.scope("ple0")
        ple(0)
        if DEBUG:
            dbg0 = dout("dbg0", (NTOK, D))
            d_st = carve(0, (128, D), F32)
            for (r0, n) in ctiles(0, NTOK, 128):
                tr_out(lambda c, r0=r0, n=n: xT[:, c, P0 + r0:P0 + r0 + n], Rx, n, DC, d_st, res("dst_dbg"), dbg0[r0:r0 + n, :])
            kb.barrier()

        kb.scope("retnorm")
        LOGG = [float(np.log1p(-(2.0 ** (-5.0 - h)))) for h in range(NH)]
        xflat = xT[:, :, :].rearrange("p a b -> p (a b)")
        ypart = [nc.dram_tensor(f"ypart{h}", [128, DC * NTOK], F32).ap() for h in range(NH)]
        rmsnorm(16, P0, NX, sq512, Rsq5)
        Rxsp = res("xspill")
        kb.dma("sp", xspill, xflat, reads=[Rx], writes=[Rxsp])
        kb.barrier()

        def xr(off, shape, dt):
            return carve(off, shape, dt, base=xflat)
        qT, qcT, kT = xr(0, (128, 2, NTOK), BF16), xr(4352, (128, 2, NTOK), BF16), xr(8704, (128, 2, NTOK), BF16)
        kdtok = xr(13056, (128, 9, 256), BF16)
        vtok = xr(17664, (128, 9, 512), BF16)
        ogT = xr(26880, (128, 4, NTOK), BF16)
        crq = xr(35584, (128, NTOK), F32)
        Sst = xr(39936, (128, 2, 512), F32)
        Sbf = xr(44032, (128, 2, 512), BF16)
        Sin = xr(46080, (128, 2, 512), F32)
        ktok_s = xr(50176, (128, 256), BF16)
        kdsb = [xr(50688, (128, 256), BF16), xr(51200, (128, 256), BF16)]
        Ssm = [xr(51712 + 4096 * i, (128, 2, 512), F32) for i in range(3)]
        Ssb = [xr(64000 + 2048 * i, (128, 2, 512), BF16) for i in range(2)]
        scm = xr(68096, (128, 128), BF16)
        scms = xr(68352, (128, 64), BF16)
        T = [carve(2048 * i, (128, 512), F32) for i in range(3)]
        ob = carve(6144, (128, 4, 128), BF16)
        osq = carve(7168, (128, 4, 128), BF16)
        mu, msq, var = carve(8192, (128, 128), F32), carve(8704, (128, 128), F32), carve(9216, (128, 128), F32)
        sgt = [carve(10240, (128, 512), F32), carve(12288, (128, 512), F32)]
        yst = [carve(14336 + 2048 * i, (128, 512), F32) for i in range(3)]
        sgT = carve(26624, (128, 4, NTOK), BF16)
        RsgT = res("sgT")
        RqT, RqcT, RkT, Rkd, Rvt, Rog, Rcrq, RSst, RSbf, RSin, Rkts = [res(n) for n in "qT qcT kT kdtok vtok ogT crq Sst Sbf Sin ktoks".split()]
        Rkdsb, RSsm, RSsb = [res("kdsb0"), res("kdsb1")], [res(f"Ssm{i}") for i in range(3)], [res(f"Ssb{i}") for i in range(2)]
        Rscm, Rscms, RT = res("scm"), res("scms"), [res(f"T{i}") for i in range(3)]
        Rob, Rosq, Rmu, Rmsq, Rvar = res("ob"), res("osq"), res("mu"), res("msq"), res("var")
        Rsgt, Ryst = [res("sgt0"), res("sgt1")], [res(f"yst{i}") for i in range(3)]
        w_in_v = w_in.rearrange("(c p) f -> p c f", p=128)
        w_out_v = w_out.rearrange("(k p) d -> p k d", p=128)
        cnt = {"sg": 0, "ys": 0, "ss": 0, "sb": 0, "kd": 0}

        def wd16(col0):
            s_ = wload([(lambda sl: sl[:, :].rearrange("p (c f) -> p c f", c=16), w_in_v[:, :, col0:col0 + 256])])
            return s_, wsl[s_][:, :].rearrange("p (c f) -> p c f", c=16)

        def groupnorm(psb, bo, n, tok0):
            kb.op("act", lambda e: e.activation(out=ob[:, :, 0:n], in_=psb[:, :, 0:n], func=AF.Copy), reads=[Rp[bo]], writes=[Rob])
            kb.op("act", lambda e: e.activation(out=osq[:, :, 0:n], in_=psb[:, :, 0:n], func=AF.Square), reads=[Rp[bo]], writes=[Rosq])
            b = nbank()
            kb.group("pe", [(lambda e, v=v, b=b: e.matmul(pbank[b][:, 0:n], onesb[:, :], ob[:, v, 0:n], start=(v == 0), stop=(v == 3))) for v in range(4)] +
                     [(lambda e, v=v, b=b: e.matmul(pbank[b][:, 128:128 + n], onesb[:, :], osq[:, v, 0:n], start=(v == 0), stop=(v == 3))) for v in range(4)],
                     reads=[Rob, Rosq, Rc], writes=[Rp[b]])
            kb.op("dve", lambda e, b=b: e.tensor_scalar(out=mu[:, 0:n], in0=pbank[b][:, 0:n], scalar1=1.0 / DV, scalar2=None, op0=ALU.mult), reads=[Rp[b]], writes=[Rmu])
            kb.op("dve", lambda e: e.tensor_tensor(out=msq[:, 0:n], in0=mu[:, 0:n], in1=mu[:, 0:n], op=ALU.mult), reads=[Rmu], writes=[Rmsq])
            kb.op("dve", lambda e, b=b: e.scalar_tensor_tensor(out=var[:, 0:n], in0=pbank[b][:, 128:128 + n], scalar=1.0 / DV, in1=msq[:, 0:n], op0=ALU.mult, op1=ALU.subtract),
                  reads=[Rp[b], Rmsq], writes=[Rvar])
            kb.op("act", lambda e: e.activation(out=var[:, 0:n], in_=var[:, 0:n], func=AF.Sqrt, bias=epst[:, 0:1]), reads=[Rvar, Rc], writes=[Rvar])
            kb.op("dve", lambda e: e.reciprocal(out=var[:, 0:n], in_=var[:, 0:n]), reads=[Rvar], writes=[Rvar])
            kb.op("dve", lambda e: e.tensor_tensor(out=psb[:, :, 0:n], in0=psb[:, :, 0:n], in1=mu[:, 0:n].unsqueeze(1).to_broadcast([128, 4, n]), op=ALU.subtract),
                  reads=[Rp[bo], Rmu], writes=[Rp[bo]])
            kb.op("dve", lambda e: e.tensor_tensor(out=ogT[:, :, tok0:tok0 + n], in0=psb[:, :, 0:n], in1=var[:, 0:n].unsqueeze(1).to_broadcast([128, 4, n]), op=ALU.mult),
                  reads=[Rp[bo], Rvar], writes=[Rog])

        for h in range(NH):
            kb.scope(f"h{h}a_proj")
            g128 = float(np.exp(LOGG[h] * 128.0))
            g4 = float(np.exp(LOGG[h] * 4.0))
            kb.dma("sp", crq[:, :], crossq_d[h:h + 1, :].partition_broadcast(128), writes=[Rcrq])
            for which in range(2):
                s_, wv_ = wd16(which * 2048 + h * 256)
                for (t0, n) in ctiles(0, NTOK):
                    b1, b2 = nbank(), nbank()
                    for bb, half_ in ((b1, 0), (b2, 1)):
                        kb.group("pe", [(lambda e, c=c, bb=bb, half_=half_, t0=t0, n=n, wv_=wv_: e.matmul(pbank[bb][:, 0:n], wv_[:, c, half_ * 128:(half_ + 1) * 128], hT[:, c, P0 + t0:P0 + t0 + n], start=(c == 0), stop=(c == DC - 1)))
                                        for c in range(DC)], reads=[Rw[s_], Rh], writes=[Rp[bb]])
                    cosv, sinv = cst[:, 0, t0:t0 + n], cst[:, 1, t0:t0 + n]
                    dst = qT if which == 0 else kT
                    rdst = RqT if which == 0 else RkT
                    kb.op("dve", lambda e, b1=b1, n=n, cosv=cosv: e.tensor_tensor(out=T[0][:, 0:n], in0=pbank[b1][:, 0:n], in1=cosv, op=ALU.mult), reads=[Rp[b1], Rc], writes=[RT[0]])
                    kb.op("dve", lambda e, b2=b2, n=n, sinv=sinv: e.tensor_tensor(out=T[1][:, 0:n], in0=pbank[b2][:, 0:n], in1=sinv, op=ALU.mult), reads=[Rp[b2], Rc], writes=[RT[1]])
                    kb.op("dve", lambda e, n=n: e.tensor_tensor(out=T[0][:, 0:n], in0=T[0][:, 0:n], in1=T[1][:, 0:n], op=ALU.subtract), reads=[RT[0], RT[1]], writes=[RT[0]])
                    kb.op("act", lambda e, n=n, t0=t0, dst=dst: e.activation(out=dst[:, 0, t0:t0 + n], in_=T[0][:, 0:n], func=AF.Copy), reads=[RT[0]], writes=[rdst])
                    if which == 0:
                        kb.op("dve", lambda e, n=n, t0=t0: e.tensor_tensor(out=qcT[:, 0, t0:t0 + n], in0=T[0][:, 0:n], in1=crq[:, t0:t0 + n], op=ALU.mult), reads=[RT[0], Rcrq], writes=[RqcT])
                    kb.op("dve", lambda e, b1=b1, n=n, sinv=sinv: e.tensor_tensor(out=T[1][:, 0:n], in0=pbank[b1][:, 0:n], in1=sinv, op=ALU.mult), reads=[Rp[b1], Rc], writes=[RT[1]])
                    kb.op("dve", lambda e, b2=b2, n=n, cosv=cosv: e.tensor_tensor(out=T[2][:, 0:n], in0=pbank[b2][:, 0:n], in1=cosv, op=ALU.mult), reads=[Rp[b2], Rc], writes=[RT[2]])
                    kb.op("dve", lambda e, n=n: e.tensor_tensor(out=T[1][:, 0:n], in0=T[1][:, 0:n], in1=T[2][:, 0:n], op=ALU.add), reads=[RT[1], RT[2]], writes=[RT[1]])
                    kb.op("act", lambda e, n=n, t0=t0, dst=dst: e.activation(out=dst[:, 1, t0:t0 + n], in_=T[1][:, 0:n], func=AF.Copy), reads=[RT[1]], writes=[rdst])
                    if which == 0:
                        kb.op("dve", lambda e, n=n, t0=t0: e.tensor_tensor(out=qcT[:, 1, t0:t0 + n], in0=T[1][:, 0:n], in1=crq[:, t0:t0 + n], op=ALU.mult), reads=[RT[1], Rcrq], writes=[RqcT])
            sv = [wd16(4096 + h * 512 + hv * 256) for hv in range(2)]
            for i_, (r0, n) in enumerate(ctiles(0, NTOK, 128)):
                b = nbank()
                for hv in range(2):
                    s_, wv_ = sv[hv]
                    kb.group("pe", [(lambda e, c=c, b=b, hv=hv, r0=r0, n=n, wv_=wv_: e.matmul(pbank[b][0:n, hv * 256:(hv + 1) * 256], hT[:, c, P0 + r0:P0 + r0 + n], wv_[:, c, :], start=(c == 0), stop=(c == DC - 1)))
                                    for c in range(DC)], reads=[Rw[s_], Rh], writes=[Rp[b]])
                kb.op("act", lambda e, b=b, i_=i_, n=n: e.activation(out=vtok[0:n, i_, :], in_=pbank[b][0:n, :], func=AF.Copy), reads=[Rp[b]], writes=[Rvt])
            for i_, (r0, n) in enumerate(ctiles(0, NTOK, 128)):
                b = nbank()
                kb.group("pe", [(lambda e, ch=ch, b=b, r0=r0, n=n: e.matmul(pbank[b][0:n, ch * 128:(ch + 1) * 128], kT[:, ch, r0:r0 + n], identb[:, :], start=True, stop=True)) for ch in range(2)],
                         reads=[RkT, Rc], writes=[Rp[b]])
                if i_ < 8:
                    kb.op("act", lambda e, b=b, i_=i_, h=h: e.activation(out=kdtok[:, i_, :], in_=pbank[b][:, 0:256], func=AF.Copy, scale=kdt[:, h:h + 1]), reads=[Rp[b], Rc], writes=[Rkd])
                else:
                    kb.op("act", lambda e, b=b: e.activation(out=ktok_s[0:NS, :], in_=pbank[b][0:NS, 0:256], func=AF.Copy), reads=[Rp[b]], writes=[Rkts])

            def chain_step(j, Sdst, Rdst):
                bs = [nbank(), nbank()]
                for ch in range(2):
                    kb.op("pe", lambda e, ch=ch, j=j, bs=bs: e.matmul(pbank[bs[ch]][:, :], kdtok[:, j, ch * 128:(ch + 1) * 128], vtok[:, j, :], start=True, stop=True), reads=[Rkd, Rvt], writes=[Rp[bs[ch]]])
                    kb.op("dve", lambda e, ch=ch, bs=bs, g128=g128: e.scalar_tensor_tensor(out=Sdst[:, ch, :], in0=Sdst[:, ch, :], scalar=g128, in1=pbank[bs[ch]][:, :], op0=ALU.mult, op1=ALU.add),
                          reads=[Rdst, Rp[bs[ch]]], writes=[Rdst])

            kb.op("dve", lambda e: e.memset(Sst[:, :, :], 0.0), writes=[RSst])
            for j in range(8):
                chain_step(j, Sst, RSst)
            Rcci, Rcco = res(f"ccin{h}"), res(f"ccout{h}")
            kb.dma("sp", cc_in[h].ap().rearrange("(c p) v -> p c v", p=128), Sst[:, :, :], reads=[RSst], writes=[Rcci])
            kb.custom("pool", lambda e, h=h: e.collective_compute("AllGather", ALU.bypass, replica_groups=PAIRS, ins=[cc_in[h].ap().opt()], outs=[cc_out[h].ap().opt()]),
                      ccsem, reads=[Rcci], writes=[Rcco])
            kb.scope(f"h{h}b_sample_gate")
            gate_items = []
            gslots = [wd16(8192 + h * 512 + hv * 256) for hv in range(2)]
            for hv in range(2):
                for o2 in range(2):
                    for (t0, n) in ctiles(0, NTOK):
                        def item(hv=hv, o2=o2, t0=t0, n=n):
                            s_, wv_ = gslots[hv]
                            v = 2 * hv + o2
                            b = nbank()
                            kb.group("pe", [(lambda e, c=c, b=b, o2=o2, t0=t0, n=n, wv_=wv_: e.matmul(pbank[b][:, 0:n], wv_[:, c, o2 * 128:(o2 + 1) * 128], hT[:, c, P0 + t0:P0 + t0 + n], start=(c == 0), stop=(c == DC - 1)))
                                            for c in range(DC)], reads=[Rw[s_], Rh], writes=[Rp[b]])
                            kb.op("act", lambda e, b=b, n=n, v=v, t0=t0: e.activation(out=sgT[:, v, t0:t0 + n], in_=pbank[b][:, 0:n], func=AF.Silu), reads=[Rp[b]], writes=[RsgT])
                        gate_items.append(item)
            b = nbank()
            kb.group("pe", [(lambda e, ch=ch, b=b: e.matmul(pbank[b][0:NS, 0:NS], kT[:, ch, NPR:NTOK], qT[:, ch, NPR:NTOK], start=(ch == 0), stop=(ch == 1))) for ch in range(2)],
                     reads=[RkT, RqT], writes=[Rp[b]])
            kb.op("dve", lambda e, b=b, h=h: e.tensor_tensor(out=scms[0:NS, :], in0=pbank[b][0:NS, 0:NS], in1=dmasks[:, h, :], op=ALU.mult), reads=[Rp[b], Rc], writes=[Rscms])
            bo = 6
            psb = pbank[bo][:, :].rearrange("p (a b) -> p a b", a=4)
            kb.group("pe", [(lambda e, v=v, psb=psb: e.matmul(psb[:, v, 0:NS], vtok[0:NS, 8, v * 128:(v + 1) * 128], scms[0:NS, :], start=(v == 0), stop=False)) for v in range(4)],
                     reads=[Rvt, Rscms], writes=[Rp[bo]])
            ss_base = cnt["ss"]
            cnt["ss"] += NSQ

            def sload(bq_):
                si_ = (ss_base + bq_) % 3
                kb.dma("sp", Ssm[si_][:, :, :], sret[bq_, h].rearrange("(c p) v -> p c v", p=128), writes=[RSsm[si_]])
            sload(0)
            sload(1)
            for bq in range(NSQ):
                si = (ss_base + bq) % 3
                sbi = cnt["sb"] % 2
                cnt["sb"] += 1
                if bq + 2 < NSQ:
                    sload(bq + 2)
                kb.op("act", lambda e, si=si, sbi=sbi: e.activation(out=Ssb[sbi][:, :, :], in_=Ssm[si][:, :, :], func=AF.Copy), reads=[RSsm[si]], writes=[RSsb[sbi]])
                fns = []
                for v in range(4):
                    for ch in range(2):
                        fns.append(lambda e, v=v, ch=ch, psb=psb, sbi=sbi, bq=bq: e.matmul(psb[:, v, 4 * bq:4 * bq + 4], Ssb[sbi][:, ch, v * 128:(v + 1) * 128], qcT[:, ch, NPR + 4 * bq:NPR + 4 * bq + 4],
                                                                                         start=False, stop=(ch == 1 and bq == NSQ - 1)))
                kb.group("pe", fns, reads=[RSsb[sbi], RqcT], writes=[Rp[bo]])
                ki = cnt["kd"] % 2
                cnt["kd"] += 1
                kb.op("act", lambda e, ki=ki, h=h, bq=bq: e.activation(out=kdsb[ki][0:NS, :], in_=ktok_s[0:NS, :], func=AF.Copy, scale=kds[:, h, bq:bq + 1]), reads=[Rkts, Rc], writes=[Rkdsb[ki]])
                bs = [nbank(), nbank()]
                for ch in range(2):
                    kb.op("pe", lambda e, ch=ch, bs=bs, ki=ki: e.matmul(pbank[bs[ch]][:, :], kdsb[ki][0:NS, ch * 128:(ch + 1) * 128], vtok[0:NS, 8, :], start=True, stop=True),
                          reads=[Rkdsb[ki], Rvt], writes=[Rp[bs[ch]]])
                    kb.op("dve", lambda e, ch=ch, bs=bs, si=si, g4=g4: e.scalar_tensor_tensor(out=Ssm[si][:, ch, :], in0=Ssm[si][:, ch, :], scalar=g4, in1=pbank[bs[ch]][:, :], op0=ALU.mult, op1=ALU.add),
                          reads=[RSsm[si], Rp[bs[ch]]], writes=[RSsm[si]])
                kb.dma("sp", rets_o[bq, h].rearrange("(c p) v -> p c v", p=128), Ssm[si][:, :, :], reads=[RSsm[si]], writes=[res("outdram")])
                if gate_items:
                    gate_items.pop(0)()
            while gate_items:
                gate_items.pop(0)()
            groupnorm(psb, bo, NS, NPR)
            kb.dma("sp", Sin[:, :, :], cc_out[h].ap()[0:DK, :].rearrange("(c p) v -> p c v", p=128), reads=[Rcco], writes=[RSin])
            kb.scope(f"h{h}c_rec")
            kb.op("dve", lambda e: e.tensor_scalar(out=Sst[:, :, :], in0=Sin[:, :, :], scalar1=flagt[:, 0:1], scalar2=None, op0=ALU.mult), reads=[RSin, Rc], writes=[RSst])
            gn_pending = None
            for j in range(8):
                tj = 128 * j
                kb.op("act", lambda e: e.activation(out=Sbf[:, :, :], in_=Sst[:, :, :], func=AF.Copy), reads=[RSst], writes=[RSbf])
                b = nbank()
                kb.group("pe", [(lambda e, ch=ch, b=b, tj=tj: e.matmul(pbank[b][:, 0:128], kT[:, ch, tj:tj + 128], qT[:, ch, tj:tj + 128], start=(ch == 0), stop=(ch == 1))) for ch in range(2)],
                         reads=[RkT, RqT], writes=[Rp[b]])
                kb.op("dve", lambda e, b=b, h=h: e.tensor_tensor(out=scm[:, :], in0=pbank[b][:, 0:128], in1=dmask[:, h, :], op=ALU.mult), reads=[Rp[b], Rc], writes=[Rscm])
                bo = 6 + (j % 2)
                psb = pbank[bo][:, :].rearrange("p (a b) -> p a b", a=4)
                fns = []
                for v in range(4):
                    fns.append(lambda e, v=v, psb=psb, j=j: e.matmul(psb[:, v, :], vtok[:, j, v * 128:(v + 1) * 128], scm[:, :], start=True, stop=False))
                    for ch in range(2):
                        fns.append(lambda e, v=v, ch=ch, psb=psb, tj=tj: e.matmul(psb[:, v, :], Sbf[:, ch, v * 128:(v + 1) * 128], qcT[:, ch, tj:tj + 128], start=False, stop=(ch == 1)))
                kb.group("pe", fns, reads=[Rvt, Rscm, RSbf, RqcT], writes=[Rp[bo]])
                chain_step(j, Sst, RSst)
                if gn_pending is not None:
                    groupnorm(*gn_pending)
                gn_pending = (psb, bo, 128, tj)
            groupnorm(*gn_pending)
            kb.dma("sp", retp_o[h].rearrange("(c p) v -> p c v", p=128), Sst[:, :, :], reads=[RSst], writes=[res("outdram")])
            kb.op("dve", lambda e: e.tensor_tensor(out=ogT[:, :, :], in0=ogT[:, :, :], in1=sgT[:, :, :], op=ALU.mult), reads=[Rog, RsgT], writes=[Rog])
            kb.scope(f"h{h}d_wout")
            so = []
            for hv in range(2):
                s_ = wload([(lambda sl: sl[:, :].rearrange("p (k d) -> p k d", k=2), w_out_v[:, 4 * h + 2 * hv:4 * h + 2 * hv + 2, :])])
                so.append((s_, wsl[s_][:, :].rearrange("p (k d) -> p k d", k=2)))
            Ryp = res(f"ypart{h}")
            for o in range(DC):
                for (t0, n) in ctiles(0, NTOK):
                    b = nbank()
                    kb.group("pe", [(lambda e, v=v, b=b, o=o, t0=t0, n=n, so=so: e.matmul(pbank[b][:, 0:n], so[v // 2][1][:, v % 2, o * 128:(o + 1) * 128], ogT[:, v, t0:t0 + n], start=(v == 0), stop=(v == 3)))
                                    for v in range(4)], reads=[Rw[so[0][0]], Rw[so[1][0]], Rog], writes=[Rp[b]])
                    yi = cnt["ys"] % 3
                    cnt["ys"] += 1
                    kb.op("act", lambda e, b=b, n=n, yi=yi: e.activation(out=yst[yi][:, 0:n], in_=pbank[b][:, 0:n], func=AF.Copy), reads=[Rp[b]], writes=[Ryst[yi]])
                    kb.dma("sp", ypart[h][:, o * NTOK + t0:o * NTOK + t0 + n], yst[yi][:, 0:n], reads=[Ryst[yi]], writes=[Ryp])
        kb.barrier()
        kb.scope("restore")
        kb.dma("sp", xflat, xspill, reads=[Rxsp], writes=[Rx])
        yrow = [carve(26624 + 8704 * i, (128, 2, NTOK), F32) for i in range(2)]
        Ryrow = [res("yrow0"), res("yrow1")]
        it_ = 0
        for h in range(NH):
            for o in range(0, DC, 2):
                yi = it_ % 2
                it_ += 1
                kb.dma("sp", yrow[yi][:, :, :], ypart[h][:, o * NTOK:(o + 2) * NTOK].rearrange("p (a b) -> p a b", a=2), reads=[res(f"ypart{h}")], writes=[Ryrow[yi]])
                kb.op("dve", lambda e, o=o, yi=yi: e.tensor_tensor(out=xT[:, o:o + 2, P0:NX], in0=xT[:, o:o + 2, P0:NX], in1=yrow[yi][:, :, :], op=ALU.add),
                      reads=[Rx, Ryrow[yi]], writes=[Rx])
        hx = carve(20480, (128, DC, 2), F32)
        Rhx, Rcxi, Rcxo = res("hx"), res("cxin"), res("cxout")
        cxi = nc.dram_tensor("cxi", [128, 2 * DC], F32)
        cxo = nc.dram_tensor("cxo", [256, 2 * DC], F32)
        kb.op("act", lambda e: e.activation(out=hx[:, :, :], in_=xT[:, :, S0 - 2:S0], func=AF.Copy), reads=[Rx], writes=[Rhx])
        kb.dma("sp", cxi.ap(), hx[:, :, :].rearrange("p a b -> p (a b)"), reads=[Rhx], writes=[Rcxi])
        kb.custom("pool", lambda e: e.collective_compute("AllGather", ALU.bypass, replica_groups=PAIRS, ins=[cxi.ap().opt()], outs=[cxo.ap().opt()]), ccsem, reads=[Rcxi], writes=[Rcxo])
        kb.dma("sp", hx[:, :, :].rearrange("p a b -> p (a b)"), cxo.ap()[0:128, :], reads=[Rcxo], writes=[Rhx])
        kb.op("dve", lambda e: e.tensor_scalar(out=xT[:, :, 15:17], in0=hx[:, :, :], scalar1=flagt[:, 0:1], scalar2=None, op0=ALU.mult), reads=[Rhx, Rc, Rx], writes=[Rx])
        kb.barrier()
        kb.scope("ffn1")
        conv_ffn(1)
        kb.scope("ple1")
        ple(1)

        kb.scope("final")
        ta2 = carve(16384, (128, NX), F32)
        yT = carve(20864, (128, DC, 128), F32)
        ostg_y = carve(29056, (128, D), F32)
        Rta2, RyT, Rosy = res("ta2"), res("yT"), res("ostgy")
        sqv2 = carve(0, (128, DC, 512), BF16)
        sumsq_rstd(ta2, Rta2, P0, NX, sqv2, Rsq5, 512)
        for (r0, n) in ctiles(0, NTOK, 128):
            for c in range(DC):
                kb.op("dve", lambda e, c=c, r0=r0, n=n: e.scalar_tensor_tensor(out=yT[:, c, 0:n], in0=xT[:, c, P0 + r0:P0 + r0 + n], scalar=gamT[:, 96 + c:97 + c], in1=ta2[:, P0 + r0:P0 + r0 + n],
                                                                              op0=ALU.mult, op1=ALU.mult), reads=[Rx, Rta2, Rc], writes=[RyT])
            tr_out(lambda c, n=n: yT[:, c, 0:n], RyT, n, DC, ostg_y, Rosy, y_o[r0:r0 + n, :])
        kb.wait_all("sp", [res("outdram")])
        kb.barrier()
        with nc.Block() as block:
            kb.replay(block)
    return nc


def _tables(half):
    log_g = np.log1p(-(2.0 ** (-5.0 - np.arange(NH, dtype=np.float64))))
    halfd = DK // 2
    inv = 10000.0 ** (-np.arange(halfd, dtype=np.float64) / halfd)
    pos = np.concatenate([half * NPR + np.arange(NPR), np.tile(16384 + np.arange(4), NSQ)]).astype(np.float64)
    ang = inv[:, None] * pos[None, :]
    cs = np.stack([np.cos(ang), np.sin(ang)], axis=1).astype(np.float32)
    n_in = np.concatenate([np.arange(NPR) % 128, np.tile(np.arange(4), NSQ)]).astype(np.float64)
    crossq = np.exp(log_g[:, None] * (n_in[None, :] + 1.0)).astype(np.float32)
    m = np.arange(128)
    dm = np.where(m[None, :, None] <= m[None, None, :], np.exp(log_g[:, None, None] * np.maximum(m[None, None, :] - m[None, :, None], 0)), 0.0) * DK ** -0.5
    dmask = np.ascontiguousarray(dm.transpose(1, 0, 2)).astype(np.float32)
    ms = np.arange(64)
    same = (ms[:, None] // 4) == (ms[None, :] // 4)
    dms = np.where(same[None] & (ms[None, :, None] <= ms[None, None, :]), np.exp(log_g[:, None, None] * np.maximum(ms[None, None, :] - ms[None, :, None], 0)), 0.0) * DK ** -0.5
    dmasks = np.ascontiguousarray(dms.transpose(1, 0, 2)).astype(np.float32)
    kdt = (np.exp(log_g[None, :] * (127.0 - m[:, None])) * DK ** -0.5).astype(np.float32)
    kds = np.zeros((64, NH, NSQ), np.float32)
    for t in range(64):
        kds[t, :, t // 4] = np.exp(log_g * (3.0 - (t % 4))) * DK ** -0.5
    invc = np.zeros((4, 15), np.float32)
    for gi, w in enumerate((2, 4, 8, 16)):
        p = np.arange(15)
        invc[gi] = 1.0 / np.minimum(p + 1, w) if half == 0 else 1.0 / w
    flagv = np.concatenate([[float(half)], float(half) * np.exp(log_g * 1024.0)]).astype(np.float32)[None, :]
    return dict(cs=cs, crossq=crossq, dmask=dmask, dmasks=dmasks, kdt=kdt, kds=kds, invc=invc.reshape(1, 60), flagv=flagv)


_NC_CACHE = {}


def kernel(x_prompt, x_sample, p_prompt, p_sample, state_pool, state_ret, state_conv,
           norm_mix, norm_ffn, norm_ple, norm_final, pool_w, pool_scale, ret_w_in, ret_w_out,
           ffn_w_up, ffn_conv_w, ffn_conv_b, ffn_w_down, ple_w_proj, ple_w_gate):
    f32 = lambda a: np.ascontiguousarray(np.asarray(a, dtype=np.float32))
    x_prompt, x_sample, p_prompt, p_sample = map(f32, (x_prompt, x_sample, p_prompt, p_sample))
    state_pool, state_ret, state_conv = map(f32, (state_pool, state_ret, state_conv))
    if "nc" not in _NC_CACHE:
        _NC_CACHE["nc"] = build_program()
    nc = _NC_CACHE["nc"]
    small = np.concatenate([f32(norm_mix).reshape(32, 128), f32(norm_ffn).reshape(32, 128), f32(norm_ple).reshape(32, 128),
                            f32(norm_final).reshape(16, 128), f32(pool_scale).reshape(16, 128)], axis=0)
    convp = np.zeros((768, 128), np.float32)
    convp[0:528] = f32(ffn_conv_w).reshape(528, 128)
    convp[528:704] = f32(ffn_conv_b).reshape(176, 128)
    shared = dict(small=small, convp=convp, identd=np.eye(128, dtype=np.float32), pool_w=f32(pool_w)[0],
                  ret_w_in=f32(ret_w_in)[0], ret_w_out=f32(ret_w_out)[0], ffn_w_up=f32(ffn_w_up), ffn_w_down=f32(ffn_w_down),
                  ple_w_proj=f32(ple_w_proj), ple_w_gate=f32(ple_w_gate))
    in_maps = []
    for core in range(8):
        s, half = core // 2, core % 2
        halo = x_prompt[s, NPR - HALO:NPR] if half else np.zeros((HALO, D), np.float32)
        sl = slice(NSQ * core, NSQ * (core + 1))
        xin = np.concatenate([halo, x_prompt[s, half * NPR:(half + 1) * NPR], x_sample[sl].reshape(NS, D)], axis=0)
        pin = np.concatenate([p_prompt[:, s, half * NPR:(half + 1) * NPR], p_sample[:, sl].reshape(2, NS, 256)], axis=1)
        m = dict(shared)
        m.update(xin=np.ascontiguousarray(xin), pin=np.ascontiguousarray(pin), spool=state_pool[0, sl].reshape(240, D),
                 sret=state_ret[0, sl], sconv=np.ascontiguousarray(state_conv[:, sl].reshape(2, 32, F2)))
        m.update(_tables(half))
        in_maps.append(m)
    res = run_bass_kernel_spmd(nc, in_maps, core_ids=list(range(8)), **({"trace": True} if PROFILE else {}))
    r = res.results
    if PROFILE:
        _NC_CACHE["prof"] = res
    if DEBUG:
        _NC_CACHE["dbg"] = r
    y_prompt = np.stack([np.concatenate([r[2 * s]["y"][0:NPR], r[2 * s + 1]["y"][0:NPR]], axis=0) for s in range(4)])
    y_sample = np.concatenate([r[c]["y"][NPR:].reshape(NSQ, 4, D) for c in range(8)], axis=0)
    pool_p = np.stack([r[2 * s + 1]["pool_p"] for s in range(4)])[None]
    pool_s = np.concatenate([r[c]["pool_s"].reshape(NSQ, 15, D) for c in range(8)], axis=0)[None]
    ret_p = np.stack([r[2 * s + 1]["ret_p"] for s in range(4)])[None]
    ret_s = np.concatenate([r[c]["ret_s"] for c in range(8)], axis=0)[None]
    conv_p = np.stack([r[2 * s + 1]["conv_p"] for s in range(4)], axis=1)
    conv_s = np.concatenate([r[c]["conv_s"].reshape(2, NSQ, 2, F2) for c in range(8)], axis=1)
    return (y_prompt, y_sample, pool_p, pool_s, ret_p, ret_s, conv_p, conv_s)
```

```python
import contextlib
import numpy as np
import concourse.bass as bass
import concourse.mybir as mybir
from concourse.bass_utils import run_bass_kernel_spmd

F32 = mybir.dt.float32
BF16 = mybir.dt.bfloat16
AF = mybir.ActivationFunctionType
ALU = mybir.AluOpType

D = 2048
DC = 16
FF = 5632
F2 = 11264
NH = 8
DK = 256
DV = 512
NPR = 1024
NSQ = 16
NS = 64
HALO = 17
P0 = HALO
S0 = P0 + NPR
NX = S0 + NS
NTOK = NPR + NS
EPS = 1e-6
DEBUG = False
PROFILE = False
ENGS = ("pe", "act", "dve", "pool", "sp")
PAIRS = [[0, 1], [2, 3], [4, 5], [6, 7]]


class Res:
    __slots__ = ("name", "w", "r")

    def __init__(self, name):
        self.name = name
        self.w = None
        self.r = []


class KB:
    def __init__(self, nc, stack, n_sp=24, n_pool=8):
        self.nc = nc
        self.ops = {e: [] for e in ENGS}
        self.sem = {}
        self.cnt = {}
        for e in ("pe", "act", "dve", "pool"):
            self.sem[e] = stack.enter_context(nc.semaphore("prog_" + e))
            self.cnt[e] = 0
        self.dsems = {}
        for q, n in (("sp", n_sp), ("pool", n_pool)):
            self.dsems[q] = [[stack.enter_context(nc.semaphore(f"d_{q}_{i}")), 0] for i in range(n)]
        self.dnext = {"sp": 0, "pool": 0}
        self.waited = {e: {} for e in ENGS}
        self.semobj = {}
        self.csems = []

    def _collect(self, eng, reads, writes):
        need = {}

        def add(tok, same_ok):
            if tok is None:
                return
            key, val = tok
            if same_ok and key == eng:
                return
            if need.get(key, 0) < val:
                need[key] = val
        for r in reads:
            add(r.w, False)
        for w in writes:
            add(w.w, True)
            for t in w.r:
                add(t, True)
        out = []
        for key, val in need.items():
            if self.waited[eng].get(key, 0) >= val:
                continue
            self.waited[eng][key] = val
            out.append((key, val))
        return out

    def _emit_waits(self, eng, waits):
        for key, val in waits:
            semh = self.sem[key] if isinstance(key, str) else self.semobj[key]
            self.ops[eng].append(("wait", semh, val))

    def _commit(self, tok, reads, writes):
        for r in reads:
            r.r.append(tok)
        for w in writes:
            w.w = tok
            w.r = []

    def op(self, eng, fn, reads=(), writes=()):
        self._emit_waits(eng, self._collect(eng, reads, writes))
        self.cnt[eng] += 1
        tok = (eng, self.cnt[eng])
        self.ops[eng].append(("ins", fn, self.sem[eng], 1))
        self._commit(tok, reads, writes)
        return tok

    def group(self, eng, fns, reads=(), writes=()):
        self._emit_waits(eng, self._collect(eng, reads, writes))
        for fn in fns[:-1]:
            self.ops[eng].append(("ins", fn, None, 0))
        self.cnt[eng] += 1
        tok = (eng, self.cnt[eng])
        self.ops[eng].append(("ins", fns[-1], self.sem[eng], 1))
        self._commit(tok, reads, writes)
        return tok

    def dma(self, q, out, in_, reads=(), writes=()):
        ring = self.dsems[q]
        i = self.dnext[q]
        self.dnext[q] = (i + 1) % len(ring)
        ent = ring[i]
        semh = ent[0]
        key = ("d", q, i)
        self.semobj[key] = semh
        if ent[1] > 0 and self.waited[q].get(key, 0) < ent[1]:
            self.waited[q][key] = ent[1]
            self.ops[q].append(("wait", semh, ent[1]))
        self._emit_waits(q, self._collect(q, reads, writes))
        ent[1] += 16
        tok = (key, ent[1])
        self.ops[q].append(("dma", out, in_, semh))
        self._commit(tok, reads, writes)
        return tok

    def custom(self, q, fn, cs, reads=(), writes=()):
        self._emit_waits(q, self._collect(q, reads, writes))
        key = ("c", id(cs))
        self.semobj[key] = cs[0]
        cs[1] += 1
        tok = (key, cs[1])
        self.ops[q].append(("ins", fn, cs[0], 1))
        self._commit(tok, reads, writes)
        return tok

    def barrier(self):
        toks = [(e, self.cnt[e]) for e in ("pe", "act", "dve", "pool") if self.cnt[e] > 0]
        for q, ring in self.dsems.items():
            for i, ent in enumerate(ring):
                if ent[1] > 0:
                    key = ("d", q, i)
                    self.semobj[key] = ent[0]
                    toks.append((key, ent[1]))
        for e in ENGS:
            for key, val in toks:
                if key == e:
                    continue
                if self.waited[e].get(key, 0) >= val:
                    continue
                self.waited[e][key] = val
                semh = self.sem[key] if isinstance(key, str) else self.semobj[key]
                self.ops[e].append(("wait", semh, val))

    def scope(self, name):
        for e in ENGS:
            self.ops[e].append(("scope", name))

    def wait_all(self, q, resources):
        self._emit_waits(q, self._collect(q, resources, resources))

    def replay(self, block):
        names = {"pe": "tensor", "act": "scalar", "dve": "vector", "pool": "gpsimd", "sp": "sync"}

        def make(e):
            def run(eng):
                cur = None
                for item in self.ops[e]:
                    k = item[0]
                    if k == "scope":
                        if not PROFILE:
                            continue
                        if cur is not None:
                            cur.__exit__(None, None, None)
                        cur = self.nc.named_scope(item[1])
                        cur.__enter__()
                        continue
                    if k == "wait":
                        eng.wait_ge(item[1], item[2])
                    elif k == "ins":
                        ins = item[1](eng)
                        if item[2] is not None:
                            ins.then_inc(item[2], item[3])
                    else:
                        eng.dma_start(out=item[1], in_=item[2]).then_inc(item[3], 16)
                if cur is not None:
                    cur.__exit__(None, None, None)
            return run
        for e in ENGS:
            getattr(block, names[e])(make(e))


def ctiles(c0, c1, step=512):
    out = []
    if step == 512:
        nt = -(-(c1 - c0) // step)
        base, rem = divmod(c1 - c0, nt)
        for i in range(nt):
            n = base + (1 if i < rem else 0)
            out.append((c0, n))
            c0 += n
        return out
    while c0 < c1:
        n = min(step, c1 - c0)
        out.append((c0, n))
        c0 += n
    return out


def build_program():
    nc = bass.Bass("TRN2", target_bir_lowering=False)

    def din(name, shape):
        return nc.dram_tensor(name, list(shape), F32, kind="ExternalInput").ap()

    def dout(name, shape):
        return nc.dram_tensor(name, list(shape), F32, kind="ExternalOutput").ap()

    xin = din("xin", (NX, D))
    pin = din("pin", (2, NTOK, 256))
    spool = din("spool", (240, D))
    sret = din("sret", (NSQ, NH, DK, DV))
    sconv = din("sconv", (2, 32, F2))
    small = din("small", (128, 128))
    convp = din("convp", (768, 128))
    identd = din("identd", (128, 128))
    pool_w = din("pool_w", (4, 512, 512))
    w_in = din("ret_w_in", (D, 12288))
    w_out = din("ret_w_out", (4096, D))
    w_up = din("ffn_w_up", (2, D, F2))
    w_down = din("ffn_w_down", (2, FF, D))
    w_proj = din("ple_w_proj", (2, 256, D))
    w_gate = din("ple_w_gate", (2, D, D))
    cs_d = din("cs", (128, 2, NTOK))
    crossq_d = din("crossq", (NH, NTOK))
    dmask_d = din("dmask", (128, NH, 128))
    dmasks_d = din("dmasks", (64, NH, 64))
    kdt_d = din("kdt", (128, NH))
    kds_d = din("kds", (64, NH, NSQ))
    invc_d = din("invc", (1, 60))
    flag_d = din("flagv", (1, 1 + NH))

    y_o = dout("y", (NTOK, D))
    poolp_o = dout("pool_p", (15, D))
    pools_o = dout("pool_s", (240, D))
    retp_o = dout("ret_p", (NH, DK, DV))
    rets_o = dout("ret_s", (NSQ, NH, DK, DV))
    convp_o = dout("conv_p", (2, 2, F2))
    convs_o = dout("conv_s", (2, 32, F2))

    xspill = nc.dram_tensor("xspill", [128, DC * NX], F32).ap()
    cc_in = [nc.dram_tensor(f"cc_in{h}", [DK, DV], F32) for h in range(NH)]
    cc_out = [nc.dram_tensor(f"cc_out{h}", [2 * DK, DV], F32) for h in range(NH)]
    cx_in = nc.dram_tensor("cx_in", [2, D], F32)
    cx_out = nc.dram_tensor("cx_out", [4, D], F32)

    with contextlib.ExitStack() as st:
        E = st.enter_context
        kb = KB(nc, st)
        ccsem = [E(nc.semaphore("ccsem")), 0]

        def sb(name, shape, dt):
            return E(nc.sbuf_tensor("sb_" + name, list(shape), dt))

        xT = sb("xT", (128, DC, NX), F32)
        hT = sb("hT", (128, DC, NX), BF16)
        NWS = 4
        wsl = [sb(f"wsl{i}", (128, 4096), BF16) for i in range(NWS)]
        gamT = sb("gamT", (128, 128), F32)
        convT = sb("convT", (128, 768), F32)
        identf = sb("identf", (128, 128), F32)
        identb = sb("identb", (128, 128), BF16)
        onesb = sb("onesb", (128, 128), BF16)
        epst = sb("epst", (128, 1), F32)
        flagt = sb("flagt", (128, 1 + NH), F32)
        cst = sb("cst", (128, 2, NTOK), F32)
        dmask = sb("dmask", (128, NH, 128), F32)
        dmasks = sb("dmasks", (64, NH, 64), F32)
        kdt = sb("kdt", (128, NH), F32)
        kds = sb("kds", (64, NH, NSQ), F32)
        invc = sb("invc", (128, 60), F32)
        rstd = sb("rstd", (128, 512), F32)
        SCRW = 12544
        scr = sb("scr", (128, SCRW), F32)
        pbank = [E(nc.psum_tensor(f"pb{i}", [128, 512], F32)) for i in range(8)]

        R = {}

        def res(name):
            if name not in R:
                R[name] = Res(name)
            return R[name]

        Rx, Rh = res("xT"), res("hT")
        Rw = [res(f"w{i}") for i in range(NWS)]
        Rp = [res(f"pb{i}") for i in range(8)]
        Rc = res("consts")
        Rrstd = res("rstd")
        Rscr = {}

        def carve(off_b, shape, dt, base=None):
            t = scr if base is None else base
            n = int(np.prod(shape[1:]))
            nb = n * (4 if dt == F32 else 2)
            assert off_b % 4 == 0 and nb % 4 == 0
            ap = t[:, off_b // 4: off_b // 4 + nb // 4]
            if dt != F32:
                ap = ap.bitcast(dt)
            if len(shape) == 3:
                ap = ap.rearrange("p (a b) -> p a b", a=shape[1])
            elif len(shape) == 4:
                ap = ap.rearrange("p (a b c) -> p a b c", a=shape[1], b=shape[2])
            return ap[0:shape[0]]

        state = {"bank": 0, "ws": 0}

        def nbank(lo=0, hi=6):
            b = lo + state["bank"] % (hi - lo)
            state["bank"] += 1
            return b

        def wslot():
            s = state["ws"] % NWS
            state["ws"] += 1
            return s

        kb.dma("sp", identf[:], identd, writes=[Rc])
        kb.dma("pool", identb[:], identd, writes=[Rc])
        kb.dma("sp", cst[:], cs_d, writes=[Rc])
        kb.dma("sp", dmask[:], dmask_d, writes=[Rc])
        kb.dma("sp", dmasks[:], dmasks_d, writes=[Rc])
        kb.dma("sp", kdt[:], kdt_d, writes=[Rc])
        kb.dma("sp", kds[:], kds_d, writes=[Rc])
        kb.dma("sp", invc[:], invc_d.partition_broadcast(128), writes=[Rc])
        kb.dma("sp", flagt[:], flag_d.partition_broadcast(128), writes=[Rc])
        kb.op("dve", lambda e: e.memset(onesb[:], 1.0), writes=[Rc])
        kb.op("dve", lambda e: e.memset(epst[:], EPS), writes=[Rc])

        stg = [carve(0, (128, D), F32), carve(8192, (128, D), F32)]
        Rstg = [res("stg0"), res("stg1")]
        stgi = {"i": 0}

        def tr_in(src_rows, nrows, dst_fn, rdst, width=D, evac="act"):
            i = stgi["i"] % 2
            stgi["i"] += 1
            kb.dma("sp", stg[i][0:nrows, 0:width], src_rows, writes=[Rstg[i]])
            for c4 in range(0, width // 128, 4):
                nch = min(4, width // 128 - c4)
                b = nbank()
                pv = pbank[b][:, :].rearrange("p (a b) -> p a b", a=4)
                kb.group("pe", [
                    (lambda e, k=k, c4=c4, i=i, pv=pv: e.transpose(pv[:, k, 0:nrows], stg[i][0:nrows, (c4 + k) * 128:(c4 + k + 1) * 128], identf[0:nrows, 0:nrows]))
                    for k in range(nch)], reads=[Rstg[i], Rc], writes=[Rp[b]])
                dst = dst_fn(c4, nch)
                if evac == "act":
                    kb.op("act", lambda e, dst=dst, pv=pv, nch=nch: e.activation(out=dst, in_=pv[:, 0:nch, 0:nrows], func=AF.Copy), reads=[Rp[b]], writes=[rdst])
                else:
                    kb.op("dve", lambda e, dst=dst, pv=pv, nch=nch: e.tensor_copy(out=dst, in_=pv[:, 0:nch, 0:nrows]), reads=[Rp[b]], writes=[rdst])

        def tr_out(src_fn, rsrc, ncols, nch_total, ostg, rostg, dst_rows):
            for c4 in range(0, nch_total, 4):
                nch = min(4, nch_total - c4)
                b = nbank()
                kb.group("pe", [
                    (lambda e, k=k, c4=c4, b=b: e.transpose(pbank[b][0:ncols, k * 128:(k + 1) * 128], src_fn(c4 + k), identf[:, :]))
                    for k in range(nch)], reads=[rsrc, Rc], writes=[Rp[b]])
                kb.op("act", lambda e, c4=c4, nch=nch, b=b: e.activation(out=ostg[0:ncols, c4 * 128:(c4 + nch) * 128], in_=pbank[b][0:ncols, 0:nch * 128], func=AF.Copy),
                      reads=[Rp[b]], writes=[rostg])
            kb.dma("sp", dst_rows, ostg[0:ncols, 0:nch_total * 128], reads=[rostg], writes=[res("outdram")])

        def wload(pieces):
            s = wslot()
            for dst_fn, src in pieces:
                kb.dma("pool", dst_fn(wsl[s]), src, writes=[Rw[s]])
            return s

        def wview_d(s, ncols):
            return wsl[s][:, 0:16 * ncols].rearrange("p (c f) -> p c f", c=16)

        def rmsnorm(gcol, c0, c1, sqv, rsq):
            for (t0, n) in ctiles(c0, c1):
                kb.op("act", lambda e, t0=t0, n=n: e.activation(out=sqv[:, :, 0:n], in_=xT[:, :, t0:t0 + n], func=AF.Square, scale=float(D) ** -0.5),
                      reads=[Rx], writes=[rsq])
                b = nbank()
                kb.group("pe", [(lambda e, c=c, n=n, b=b: e.matmul(pbank[b][:, 0:n], onesb[:, :], sqv[:, c, 0:n], start=(c == 0), stop=(c == DC - 1))) for c in range(DC)],
                         reads=[rsq, Rc], writes=[Rp[b]])
                kb.op("act", lambda e, n=n, b=b: e.activation(out=rstd[:, 0:n], in_=pbank[b][:, 0:n], func=AF.Sqrt, bias=epst[:, 0:1]),
                      reads=[Rp[b], Rc], writes=[Rrstd])
                kb.op("dve", lambda e, n=n: e.reciprocal(out=rstd[:, 0:n], in_=rstd[:, 0:n]), reads=[Rrstd], writes=[Rrstd])
                for c in range(DC):
                    kb.op("dve", lambda e, c=c, t0=t0, n=n: e.scalar_tensor_tensor(out=hT[:, c, t0:t0 + n], in0=xT[:, c, t0:t0 + n], scalar=gamT[:, gcol + c:gcol + c + 1],
                                                                                  in1=rstd[:, 0:n], op0=ALU.mult, op1=ALU.mult),
                          reads=[Rx, Rrstd, Rc], writes=[Rh])

        kb.scope("load")
        for i in range(1):
            kb.dma("sp", stg[0][:, 0:128], small, writes=[Rstg[0]])
            b = nbank()
            kb.op("pe", lambda e, b=b: e.transpose(pbank[b][:, 0:128], stg[0][:, 0:128], identf[:, :]), reads=[Rstg[0], Rc], writes=[Rp[b]])
            kb.op("act", lambda e, b=b: e.activation(out=gamT[:, :], in_=pbank[b][:, 0:128], func=AF.Copy), reads=[Rp[b]], writes=[Rc])
            for j in range(6):
                kb.dma("sp", stg[1][:, 0:128], convp[j * 128:(j + 1) * 128, :], writes=[Rstg[1]])
                b = nbank()
                kb.op("pe", lambda e, b=b: e.transpose(pbank[b][:, 0:128], stg[1][:, 0:128], identf[:, :]), reads=[Rstg[1], Rc], writes=[Rp[b]])
                kb.op("act", lambda e, b=b, j=j: e.activation(out=convT[:, j * 128:(j + 1) * 128], in_=pbank[b][:, 0:128], func=AF.Copy), reads=[Rp[b]], writes=[Rc])
        for (r0, n) in ctiles(0, NX, 128):
            tr_in(xin[r0:r0 + n, :], n, lambda c4, nch, r0=r0, n=n: xT[:, c4:c4 + nch, r0:r0 + n], Rx)
        kb.barrier()

        def tr_in2(stg_ap, rstg, src_rows, nrows, width, copies):
            kb.dma("sp", stg_ap[0:nrows, 0:width], src_rows, writes=[rstg])
            nch = width // 128
            b = nbank()
            pv = pbank[b][:, :].rearrange("p (a b) -> p a b", a=4)
            kb.group("pe", [(lambda e, k=k, pv=pv: e.transpose(pv[:, k, 0:nrows], stg_ap[0:nrows, k * 128:(k + 1) * 128], identf[0:nrows, 0:nrows])) for k in range(nch)],
                     reads=[rstg, Rc], writes=[Rp[b]])
            copies(pv, b)

        def sumsq_rstd(dst, rdst, c0, c1, sqv, rsq, step):
            for (t0, n) in ctiles(c0, c1, step):
                kb.op("act", lambda e, t0=t0, n=n: e.activation(out=sqv[:, :, 0:n], in_=xT[:, :, t0:t0 + n], func=AF.Square, scale=float(D) ** -0.5), reads=[Rx], writes=[rsq])
                b = nbank()
                kb.group("pe", [(lambda e, c=c, n=n, b=b: e.matmul(pbank[b][:, 0:n], onesb[:, :], sqv[:, c, 0:n], start=(c == 0), stop=(c == DC - 1))) for c in range(DC)],
                         reads=[rsq, Rc], writes=[Rp[b]])
                kb.op("act", lambda e, t0=t0, n=n, b=b: e.activation(out=dst[:, t0:t0 + n], in_=pbank[b][:, 0:n], func=AF.Sqrt, bias=epst[:, 0:1]), reads=[Rp[b], Rc], writes=[rdst])
            kb.op("dve", lambda e: e.reciprocal(out=dst[:, c0:c1], in_=dst[:, c0:c1]), reads=[rdst], writes=[rdst])

        kb.scope("pool")
        POOLW = (2, 4, 8, 16)
        L = S0
        ta = carve(0, (128, NX), F32)
        hf = carve(4480, (128, NX), F32)
        tb = carve(8960, (128, NX), F32)
        pp = carve(13440, (128, NX), F32)
        hs_g = carve(17920, (128, 4, NSQ, 19), F32)
        hst = [carve(22784, (128, NSQ, 19), F32), carve(24000, (128, NSQ, 19), F32)]
        fixt = carve(25216, (128, 16), F32)
        pend_g = carve(25280, (128, 4, 15), F32)
        hcmp_g = carve(25536, (128, 4, 120), F32)
        stg2 = [carve(27456, (128, 512), F32), carve(29504, (128, 512), F32)]
        ostg_g = carve(31552, (128, 512), F32)
        sqv = carve(33600, (128, DC, 64), BF16)
        Rta, Rhf, Rtb, Rpp, Rhsg, Rhst, Rfix, Rpend, Rhcmp, Rostg, Rsq = [res(n) for n in "ta hf tb pp hsg hst fix pend hcmp ostg sq".split()]
        Rstg2 = [res("stg2a"), res("stg2b")]
        sumsq_rstd(ta, Rta, 0, NX, sqv, Rsq, 64)
        for g in range(4):
            win = POOLW[g]
            steps = {2: 0, 4: 1, 8: 2, 16: 3}[win]
            for half in range(2):
                def cp(pv, b, half=half):
                    for k in range(4):
                        kb.op("act", lambda e, k=k, pv=pv: e.activation(out=hs_g[:, k, half * 8:(half + 1) * 8, 0:15], in_=pv[:, k, 0:120].rearrange("p (b c) -> p b c", b=8), func=AF.Copy),
                              reads=[Rp[b]], writes=[Rhsg])
                tr_in2(stg2[half], Rstg2[half], spool[half * 120:(half + 1) * 120, g * 512:(g + 1) * 512], 120, 512, cp)
            for k in range(4):
                c = 4 * g + k
                kb.op("dve", lambda e, c=c: e.scalar_tensor_tensor(out=hf[:, :], in0=xT[:, c, :], scalar=gamT[:, c:c + 1], in1=ta[:, :], op0=ALU.mult, op1=ALU.mult),
                      reads=[Rx, Rta, Rc], writes=[Rhf])
                kb.op("act", lambda e, k=k: e.activation(out=hs_g[:, k, :, 15:19], in_=hf[:, S0:NX].rearrange("p (b t) -> p b t", b=NSQ), func=AF.Copy), reads=[Rhf], writes=[Rhsg])
                kb.op("act", lambda e, k=k: e.activation(out=pend_g[:, k, :], in_=hf[:, S0 - 15:S0], func=AF.Copy), reads=[Rhf], writes=[Rpend])
                kb.op("dve", lambda e: e.tensor_tensor(out=tb[:, 1:L], in0=hf[:, 1:L], in1=hf[:, 0:L - 1], op=ALU.add), reads=[Rhf], writes=[Rtb])
                src, rsrc = tb, Rtb
                sh = 2
                for _ in range(steps):
                    dst, rdst = (pp, Rpp) if src is tb else (tb, Rtb)
                    kb.op("dve", lambda e, src=src, dst=dst, sh=sh: e.tensor_tensor(out=dst[:, 2 * sh - 1:L], in0=src[:, 2 * sh - 1:L], in1=src[:, sh - 1:L - sh], op=ALU.add),
                          reads=[rsrc], writes=[rdst])
                    src, rsrc = dst, rdst
                    sh *= 2
                kb.op("dve", lambda e, c=c, src=src, win=win: e.scalar_tensor_tensor(out=hT[:, c, 15:L], in0=src[:, 15:L], scalar=1.0 / win, in1=hf[:, 15:L], op0=ALU.mult, op1=ALU.subtract),
                      reads=[rsrc, Rhf], writes=[Rh])
                kb.op("dve", lambda e, src=src, g=g: e.tensor_tensor(out=fixt[:, 0:15], in0=src[:, P0:P0 + 15], in1=invc[:, g * 15:(g + 1) * 15], op=ALU.mult), reads=[rsrc, Rc], writes=[Rfix])
                kb.op("dve", lambda e, c=c: e.tensor_tensor(out=hT[:, c, P0:P0 + 15], in0=fixt[:, 0:15], in1=hf[:, P0:P0 + 15], op=ALU.subtract), reads=[Rfix, Rhf], writes=[Rh])
                hsrc = hs_g[:, k, :, :]
                kb.op("dve", lambda e, hsrc=hsrc: e.tensor_tensor(out=hst[0][:, :, 1:19], in0=hsrc[:, :, 1:19], in1=hsrc[:, :, 0:18], op=ALU.add), reads=[Rhsg], writes=[Rhst])
                a_i = 0
                sh = 2
                for _ in range(steps):
                    kb.op("dve", lambda e, a=a_i, sh=sh: e.tensor_tensor(out=hst[1 - a][:, :, 2 * sh - 1:19], in0=hst[a][:, :, 2 * sh - 1:19], in1=hst[a][:, :, sh - 1:19 - sh], op=ALU.add),
                          reads=[Rhst], writes=[Rhst])
                    a_i = 1 - a_i
                    sh *= 2
                kb.op("dve", lambda e, c=c, a=a_i, win=win, hsrc=hsrc: e.scalar_tensor_tensor(out=hT[:, c, S0:NX].rearrange("p (b t) -> p b t", b=NSQ), in0=hst[a][:, :, 15:19], scalar=1.0 / win,
                                                                                          in1=hsrc[:, :, 15:19], op0=ALU.mult, op1=ALU.subtract), reads=[Rhst, Rhsg], writes=[Rh])
            tr_out(lambda k: pend_g[:, k, :], Rpend, 15, 4, ostg_g, Rostg, poolp_o[:, g * 512:(g + 1) * 512])
            for half in range(2):
                for k in range(4):
                    kb.op("act", lambda e, k=k, half=half: e.activation(out=hcmp_g[:, k, :].rearrange("p (b t) -> p b t", b=8), in_=hs_g[:, k, half * 8:(half + 1) * 8, 4:19], func=AF.Copy),
                          reads=[Rhsg], writes=[Rhcmp])
                tr_out(lambda k: hcmp_g[:, k, :], Rhcmp, 120, 4, ostg_g, Rostg, pools_o[half * 120:(half + 1) * 120, g * 512:(g + 1) * 512])
            s_ = wload([(lambda sl: sl[:, 0:2048].rearrange("p (c f) -> p c f", c=4), pool_w[g].rearrange("(c p) f -> p c f", p=128))])
            wv = wsl[s_][:, 0:2048].rearrange("p (c f) -> p c f", c=4)
            for o in range(4):
                for (t0, n) in ctiles(15, NX):
                    b = nbank()
                    kb.group("pe", [(lambda e, ci=ci, o=o, t0=t0, n=n, b=b, wv=wv, g=g: e.matmul(pbank[b][:, 0:n], wv[:, ci, o * 128:(o + 1) * 128], hT[:, 4 * g + ci, t0:t0 + n], start=(ci == 0), stop=(ci == 3)))
                                    for ci in range(4)], reads=[Rw[s_], Rh], writes=[Rp[b]])
                    kb.op("dve", lambda e, o=o, t0=t0, n=n, b=b, g=g: e.scalar_tensor_tensor(out=xT[:, 4 * g + o, t0:t0 + n], in0=pbank[b][:, 0:n], scalar=gamT[:, 112 + 4 * g + o:113 + 4 * g + o],
                                                                                         in1=xT[:, 4 * g + o, t0:t0 + n], op0=ALU.mult, op1=ALU.add), reads=[Rp[b], Rx, Rc], writes=[Rx])
        kb.barrier()

        sq512 = carve(0, (128, DC, 512), BF16)
        Rsq5 = res("sq512")

        def conv_ffn(l):
            rmsnorm(32 + l * 16, 15, NX, sq512, Rsq5)
            kb.barrier()
            NU = NX - 15
            ub = [carve(0, (128, NU), F32), carve(4416, (128, NU), F32)]
            cbf = [carve(8832, (128, NTOK), F32), carve(8832 + 4352, (128, NTOK), F32)]
            sgb = carve(17536, (128, NTOK), F32)
            uext = carve(21888, (128, 2, NSQ, 6), F32)
            aT = [carve(22656, (128, 4, NTOK), BF16), carve(22656 + 8704, (128, 4, NTOK), BF16)]
            stg_s = [carve(40064, (128, 512), F32), carve(42112, (128, 512), F32)]
            uh_g = carve(44160, (128, 4, 32), F32)
            un_g = carve(44672, (128, 4, 34), F32)
            ostg_s = carve(45248, (128, 512), F32)
            Rub, Rcb = [res("ub0"), res("ub1")], [res("cb0"), res("cb1")]
            Rsg, Rue, RaT = res("sgb"), res("uext"), [res("aT0"), res("aT1")]
            Rss, Ruh, Run, Ros = [res("stgs0"), res("stgs1")], res("uhg"), res("ung"), res("ostgs")
            wupv = w_up[l].rearrange("(c p) f -> p c f", p=128)
            wdnv = w_down[l].rearrange("(k p) d -> p k d", p=128)
            for jj in range(22):
                sg_ = wload([(lambda sl: sl[:, :].rearrange("p (c f) -> p c f", c=16), wupv[:, :, jj * 256:(jj + 1) * 256])])
                su_ = wload([(lambda sl: sl[:, :].rearrange("p (c f) -> p c f", c=16), wupv[:, :, FF + jj * 256:FF + (jj + 1) * 256])])
                wg_v = wsl[sg_][:, :].rearrange("p (c f) -> p c f", c=16)
                wu_v = wsl[su_][:, :].rearrange("p (c f) -> p c f", c=16)
                for gu in range(2):
                    def cp(pv, b, gu=gu):
                        kb.op("act", lambda e, pv=pv: e.activation(out=uh_g[:, 2 * gu:2 * gu + 2, :], in_=pv[:, 0:2, 0:32], func=AF.Copy), reads=[Rp[b]], writes=[Ruh])
                    tr_in2(stg_s[gu], Rss[gu], sconv[l, :, gu * FF + jj * 256:gu * FF + (jj + 1) * 256], 32, 256, cp)
                a_i = (jj // 2) % 2
                kk0 = 2 * (jj % 2)
                for k in range(2):
                    j = 2 * jj + k
                    for gu, wv_ in ((0, wg_v), (1, wu_v)):
                        fch = gu * 44 + j
                        for (t0, n) in ctiles(15, NX):
                            b = nbank()
                            kb.group("pe", [(lambda e, c=c, k=k, t0=t0, n=n, b=b, wv_=wv_: e.matmul(pbank[b][:, 0:n], wv_[:, c, k * 128:(k + 1) * 128], hT[:, c, t0:t0 + n], start=(c == 0), stop=(c == DC - 1)))
                                            for c in range(DC)], reads=[Rw[sg_ if gu == 0 else su_], Rh], writes=[Rp[b]])
                            kb.op("act", lambda e, gu=gu, t0=t0, n=n, b=b: e.activation(out=ub[gu][:, t0 - 15:t0 - 15 + n], in_=pbank[b][:, 0:n], func=AF.Copy), reads=[Rp[b]], writes=[Rub[gu]])
                        cw = [convT[:, (l * 3 + q) * 88 + fch:(l * 3 + q) * 88 + fch + 1] for q in range(3)]
                        cbias = convT[:, 528 + l * 88 + fch:528 + l * 88 + fch + 1]
                        u = ub[gu]
                        kb.op("act", lambda e, gu=gu, u=u, cw=cw, cbias=cbias: e.activation(out=cbf[gu][:, 0:NPR], in_=u[:, 2:2 + NPR], func=AF.Identity, scale=cw[2], bias=cbias), reads=[Rub[gu], Rc], writes=[Rcb[gu]])
                        kb.op("dve", lambda e, gu=gu, u=u, cw=cw: e.scalar_tensor_tensor(out=cbf[gu][:, 0:NPR], in0=u[:, 1:1 + NPR], scalar=cw[1], in1=cbf[gu][:, 0:NPR], op0=ALU.mult, op1=ALU.add),
                              reads=[Rub[gu], Rcb[gu], Rc], writes=[Rcb[gu]])
                        kb.op("dve", lambda e, gu=gu, u=u, cw=cw: e.scalar_tensor_tensor(out=cbf[gu][:, 0:NPR], in0=u[:, 0:NPR], scalar=cw[0], in1=cbf[gu][:, 0:NPR], op0=ALU.mult, op1=ALU.add),
                              reads=[Rub[gu], Rcb[gu], Rc], writes=[Rcb[gu]])
                        ue = uext[:, gu, :, :]
                        kb.op("act", lambda e, ue=ue, gu=gu, k=k: e.activation(out=ue[:, :, 0:2], in_=uh_g[:, 2 * gu + k, :].rearrange("p (b t) -> p b t", b=NSQ), func=AF.Copy), reads=[Ruh], writes=[Rue])
                        kb.op("act", lambda e, ue=ue, u=u: e.activation(out=ue[:, :, 2:6], in_=u[:, 2 + NPR:2 + NPR + NS].rearrange("p (b t) -> p b t", b=NSQ), func=AF.Copy), reads=[Rub[gu]], writes=[Rue])
                        cs_ = cbf[gu][:, NPR:NTOK].rearrange("p (b t) -> p b t", b=NSQ)
                        kb.op("act", lambda e, ue=ue, cs_=cs_, cw=cw, cbias=cbias: e.activation(out=cs_, in_=ue[:, :, 2:6], func=AF.Identity, scale=cw[2], bias=cbias), reads=[Rue, Rc], writes=[Rcb[gu]])
                        kb.op("dve", lambda e, ue=ue, cs_=cs_, cw=cw: e.scalar_tensor_tensor(out=cs_, in0=ue[:, :, 1:5], scalar=cw[1], in1=cs_, op0=ALU.mult, op1=ALU.add), reads=[Rue, Rcb[gu], Rc], writes=[Rcb[gu]])
                        kb.op("dve", lambda e, ue=ue, cs_=cs_, cw=cw: e.scalar_tensor_tensor(out=cs_, in0=ue[:, :, 0:4], scalar=cw[0], in1=cs_, op0=ALU.mult, op1=ALU.add), reads=[Rue, Rcb[gu], Rc], writes=[Rcb[gu]])
                        kb.op("act", lambda e, u=u, gu=gu, k=k: e.activation(out=un_g[:, 2 * gu + k, 0:2], in_=u[:, NPR:NPR + 2], func=AF.Copy), reads=[Rub[gu]], writes=[Run])
                        kb.op("act", lambda e, ue=ue, gu=gu, k=k: e.activation(out=un_g[:, 2 * gu + k, 2:34].rearrange("p (b t) -> p b t", b=NSQ), in_=ue[:, :, 4:6], func=AF.Copy), reads=[Rue], writes=[Run])
                    kb.op("act", lambda e: e.activation(out=sgb[:, :], in_=cbf[0][:, :], func=AF.Silu), reads=[Rcb[0]], writes=[Rsg])
                    kb.op("dve", lambda e, a_i=a_i, k=k, kk0=kk0: e.tensor_tensor(out=aT[a_i][:, kk0 + k, :], in0=sgb[:, :], in1=cbf[1][:, :], op=ALU.mult), reads=[Rsg, Rcb[1]], writes=[RaT[a_i]])
                for gu in range(2):
                    b = nbank()
                    kb.group("pe", [(lambda e, k=k, gu=gu, b=b: e.transpose(pbank[b][0:34, k * 128:(k + 1) * 128], un_g[:, 2 * gu + k, :], identf[:, :])) for k in range(2)], reads=[Run, Rc], writes=[Rp[b]])
                    kb.op("act", lambda e, b=b: e.activation(out=ostg_s[0:34, 0:256], in_=pbank[b][0:34, 0:256], func=AF.Copy), reads=[Rp[b]], writes=[Ros])
                    kb.dma("sp", convp_o[l, :, gu * FF + jj * 256:gu * FF + (jj + 1) * 256], ostg_s[0:2, 0:256], reads=[Ros], writes=[res("outdram")])
                    kb.dma("sp", convs_o[l, :, gu * FF + jj * 256:gu * FF + (jj + 1) * 256], ostg_s[2:34, 0:256], reads=[Ros], writes=[res("outdram")])
                if jj % 2 == 0:
                    continue
                sds = []
                for hh in range(2):
                    sd_ = wload([(lambda sl: sl[:, :].rearrange("p (k d) -> p k d", k=2), wdnv[:, 2 * (jj - 1 + hh):2 * (jj - 1 + hh) + 2, :])])
                    sds.append((sd_, wsl[sd_][:, :].rearrange("p (k d) -> p k d", k=2)))
                for o in range(DC):
                    for (t0, n) in ctiles(0, NTOK):
                        b = nbank()
                        kb.group("pe", [(lambda e, k=k, o=o, t0=t0, n=n, b=b, a_i=a_i, sds=sds: e.matmul(pbank[b][:, 0:n], sds[k // 2][1][:, k % 2, o * 128:(o + 1) * 128], aT[a_i][:, k, t0:t0 + n], start=(k == 0), stop=(k == 3)))
                                        for k in range(4)], reads=[Rw[sds[0][0]], Rw[sds[1][0]], RaT[a_i]], writes=[Rp[b]])
                        kb.op("dve", lambda e, o=o, t0=t0, n=n, b=b: e.tensor_tensor(out=xT[:, o, P0 + t0:P0 + t0 + n], in0=pbank[b][:, 0:n], in1=xT[:, o, P0 + t0:P0 + t0 + n], op=ALU.add),
                              reads=[Rp[b], Rx], writes=[Rx])
            kb.barrier()

        def ple(l):
            rmsnorm(64 + l * 16, P0, NX, sq512, Rsq5)
            kb.barrier()
            pT = carve(0, (128, 2, NTOK), BF16)
            stg_p = [carve(4352, (128, 256), F32), carve(5376, (128, 256), F32)]
            tg = [carve(6400, (128, 512), F32), carve(8448, (128, 512), F32)]
            RpT, Rsp, Rtg = res("pT"), [res("stgp0"), res("stgp1")], [res("tg0"), res("tg1")]
            for i_, (r0, n) in enumerate(ctiles(0, NTOK, 128)):
                def cp(pv, b, r0=r0, n=n):
                    kb.op("act", lambda e, pv=pv: e.activation(out=pT[:, :, r0:r0 + n], in_=pv[:, 0:2, 0:n], func=AF.Copy), reads=[Rp[b]], writes=[RpT])
                tr_in2(stg_p[i_ % 2], Rsp[i_ % 2], pin[l, r0:r0 + n, :], n, 256, cp)
            wp_v = carve(10496, (128, 2, D), BF16)
            Rwp = res("wp_v")
            kb.dma("pool", wp_v, w_proj[l].rearrange("(k p) d -> p k d", p=128), writes=[Rwp])
            wgv = w_gate[l].rearrange("(c p) f -> p c f", p=128)
            it = 0
            for gq in range(8):
                sg_ = wload([(lambda sl: sl[:, :].rearrange("p (c f) -> p c f", c=16), wgv[:, :, gq * 256:(gq + 1) * 256])])
                wg_v = wsl[sg_][:, :].rearrange("p (c f) -> p c f", c=16)
                for o2 in range(2):
                    o = 2 * gq + o2
                    for (t0, n) in ctiles(0, NTOK):
                        b1 = nbank()
                        kb.group("pe", [(lambda e, c=c, o2=o2, t0=t0, n=n, b1=b1, wg_v=wg_v: e.matmul(pbank[b1][:, 0:n], wg_v[:, c, o2 * 128:(o2 + 1) * 128], hT[:, c, P0 + t0:P0 + t0 + n], start=(c == 0), stop=(c == DC - 1)))
                                        for c in range(DC)], reads=[Rw[sg_], Rh], writes=[Rp[b1]])
                        b2 = nbank()
                        kb.group("pe", [(lambda e, k=k, o=o, t0=t0, n=n, b2=b2: e.matmul(pbank[b2][:, 0:n], wp_v[:, k, o * 128:(o + 1) * 128], pT[:, k, t0:t0 + n], start=(k == 0), stop=(k == 1)))
                                        for k in range(2)], reads=[Rwp, RpT], writes=[Rp[b2]])
                        ti = it % 2
                        it += 1
                        kb.op("act", lambda e, n=n, b1=b1, ti=ti: e.activation(out=tg[ti][:, 0:n], in_=pbank[b1][:, 0:n], func=AF.Sigmoid), reads=[Rp[b1]], writes=[Rtg[ti]])
                        kb.op("dve", lambda e, n=n, b2=b2, ti=ti: e.tensor_tensor(out=tg[ti][:, 0:n], in0=pbank[b2][:, 0:n], in1=tg[ti][:, 0:n], op=ALU.mult), reads=[Rp[b2], Rtg[ti]], writes=[Rtg[ti]])
                        kb.op("dve", lambda e, o=o, t0=t0, n=n, ti=ti: e.tensor_tensor(out=xT[:, o, P0 + t0:P0 + t0 + n], in0=tg[ti][:, 0:n], in1=xT[:, o, P0 + t0:P0 + t0 + n], op=ALU.add),
                              reads=[Rtg[ti], Rx], writes=[Rx])
            kb.barrier()

        kb.scope("ffn0")
        conv_ffn(0)
        kb.scope("ple0")
        ple(0)
        if DEBUG:
            dbg0 = dout("dbg0", (NTOK, D))
            d_st = carve(0, (128, D), F32)
            for (r0, n) in ctiles(0, NTOK, 128):
                tr_out(lambda c, r0=r0, n=n: xT[:, c, P0 + r0:P0 + r0 + n], Rx, n, DC, d_st, res("dst_dbg"), dbg0[r0:r0 + n, :])
            kb.barrier()

        kb.scope("retnorm")
        LOGG = [float(np.log1p(-(2.0 ** (-5.0 - h)))) for h in range(NH)]
        xflat = xT[:, :, :].rearrange("p a b -> p (a b)")
        ypart = [nc.dram_tensor(f"ypart{h}", [128, DC * NTOK], F32).ap() for h in range(NH)]
        rmsnorm(16, P0, NX, sq512, Rsq5)
        Rxsp = res("xspill")
        kb.dma("sp", xspill, xflat, reads=[Rx], writes=[Rxsp])
        kb.barrier()

        def xr(off, shape, dt):
            return carve(off, shape, dt, base=xflat)
        qT, qcT, kT = xr(0, (128, 2, NTOK), BF16), xr(4352, (128, 2, NTOK), BF16), xr(8704, (128, 2, NTOK), BF16)
        kdtok = xr(13056, (128, 9, 256), BF16)
        vtok = xr(17664, (128, 9, 512), BF16)
        ogT = xr(26880, (128, 4, NTOK), BF16)
        crq = xr(35584, (128, NTOK), F32)
        Sst = xr(39936, (128, 2, 512), F32)
        Sbf = xr(44032, (128, 2, 512), BF16)
        Sin = xr(46080, (128, 2, 512), F32)
        ktok_s = xr(50176, (128, 256), BF16)
        kdsb = [xr(50688, (128, 256), BF16), xr(51200, (128, 256), BF16)]
        Ssm = [xr(51712 + 4096 * i, (128, 2, 512), F32) for i in range(3)]
        Ssb = [xr(64000 + 2048 * i, (128, 2, 512), BF16) for i in range(2)]
        scm = xr(68096, (128, 128), BF16)
        scms = xr(68352, (128, 64), BF16)
        T = [carve(2048 * i, (128, 512), F32) for i in range(3)]
        ob = carve(6144, (128, 4, 128), BF16)
        osq = carve(7168, (128, 4, 128), BF16)
        mu, msq, var = carve(8192, (128, 128), F32), carve(8704, (128, 128), F32), carve(9216, (128, 128), F32)
        sgt = [carve(10240, (128, 512), F32), carve(12288, (128, 512), F32)]
        yst = [carve(14336 + 2048 * i, (128, 512), F32) for i in range(3)]
        sgT = carve(26624, (128, 4, NTOK), BF16)
        RsgT = res("sgT")
        ogB = carve(35328, (128, 4, NTOK), BF16)
        RqT, RqcT, RkT, Rkd, Rvt, Rog, Rcrq, RSst, RSbf, RSin, Rkts = [res(n) for n in "qT qcT kT kdtok vtok ogT crq Sst Sbf Sin ktoks".split()]
        RogB = res("ogB")
        OG = {"t": ogT, "r": Rog}
        Rkdsb, RSsm, RSsb = [res("kdsb0"), res("kdsb1")], [res(f"Ssm{i}") for i in range(3)], [res(f"Ssb{i}") for i in range(2)]
        Rscm, Rscms, RT = res("scm"), res("scms"), [res(f"T{i}") for i in range(3)]
        Rob, Rosq, Rmu, Rmsq, Rvar = res("ob"), res("osq"), res("mu"), res("msq"), res("var")
        Rsgt, Ryst = [res("sgt0"), res("sgt1")], [res(f"yst{i}") for i in range(3)]
        w_in_v = w_in.rearrange("(c p) f -> p c f", p=128)
        w_out_v = w_out.rearrange("(k p) d -> p k d", p=128)
        cnt = {"sg": 0, "ys": 0, "ss": 0, "sb": 0, "kd": 0}

        def wd16(col0):
            s_ = wload([(lambda sl: sl[:, :].rearrange("p (c f) -> p c f", c=16), w_in_v[:, :, col0:col0 + 256])])
            return s_, wsl[s_][:, :].rearrange("p (c f) -> p c f", c=16)

        def groupnorm(psb, bo, n, tok0):
            kb.op("act", lambda e: e.activation(out=ob[:, :, 0:n], in_=psb[:, :, 0:n], func=AF.Copy), reads=[Rp[bo]], writes=[Rob])
            kb.op("act", lambda e: e.activation(out=osq[:, :, 0:n], in_=psb[:, :, 0:n], func=AF.Square), reads=[Rp[bo]], writes=[Rosq])
            b = nbank()
            kb.group("pe", [(lambda e, v=v, b=b: e.matmul(pbank[b][:, 0:n], onesb[:, :], ob[:, v, 0:n], start=(v == 0), stop=(v == 3))) for v in range(4)] +
                     [(lambda e, v=v, b=b: e.matmul(pbank[b][:, 128:128 + n], onesb[:, :], osq[:, v, 0:n], start=(v == 0), stop=(v == 3))) for v in range(4)],
                     reads=[Rob, Rosq, Rc], writes=[Rp[b]])
            kb.op("dve", lambda e, b=b: e.tensor_scalar(out=mu[:, 0:n], in0=pbank[b][:, 0:n], scalar1=1.0 / DV, scalar2=None, op0=ALU.mult), reads=[Rp[b]], writes=[Rmu])
            kb.op("dve", lambda e: e.tensor_tensor(out=msq[:, 0:n], in0=mu[:, 0:n], in1=mu[:, 0:n], op=ALU.mult), reads=[Rmu], writes=[Rmsq])
            kb.op("dve", lambda e, b=b: e.scalar_tensor_tensor(out=var[:, 0:n], in0=pbank[b][:, 128:128 + n], scalar=1.0 / DV, in1=msq[:, 0:n], op0=ALU.mult, op1=ALU.subtract),
                  reads=[Rp[b], Rmsq], writes=[Rvar])
            kb.op("act", lambda e: e.activation(out=var[:, 0:n], in_=var[:, 0:n], func=AF.Sqrt, bias=epst[:, 0:1]), reads=[Rvar, Rc], writes=[Rvar])
            kb.op("dve", lambda e: e.reciprocal(out=var[:, 0:n], in_=var[:, 0:n]), reads=[Rvar], writes=[Rvar])
            kb.op("dve", lambda e: e.tensor_tensor(out=psb[:, :, 0:n], in0=psb[:, :, 0:n], in1=mu[:, 0:n].unsqueeze(1).to_broadcast([128, 4, n]), op=ALU.subtract),
                  reads=[Rp[bo], Rmu], writes=[Rp[bo]])
            ogd, rogd = OG["t"], OG["r"]
            kb.op("dve", lambda e, ogd=ogd: e.tensor_tensor(out=ogd[:, :, tok0:tok0 + n], in0=psb[:, :, 0:n], in1=var[:, 0:n].unsqueeze(1).to_broadcast([128, 4, n]), op=ALU.mult),
                  reads=[Rp[bo], Rvar], writes=[rogd])

        for h in range(NH):
            kb.scope(f"h{h}a_proj")
            g128 = float(np.exp(LOGG[h] * 128.0))
            OG["t"], OG["r"] = (ogT, Rog) if h % 2 == 0 else (ogB, RogB)
            g4 = float(np.exp(LOGG[h] * 4.0))
            kb.dma("sp", crq[:, :], crossq_d[h:h + 1, :].partition_broadcast(128), writes=[Rcrq])
            for which in range(2):
                s_, wv_ = wd16(which * 2048 + h * 256)
                for (t0, n) in ctiles(0, NTOK):
                    b1, b2 = nbank(), nbank()
                    for bb, half_ in ((b1, 0), (b2, 1)):
                        kb.group("pe", [(lambda e, c=c, bb=bb, half_=half_, t0=t0, n=n, wv_=wv_: e.matmul(pbank[bb][:, 0:n], wv_[:, c, half_ * 128:(half_ + 1) * 128], hT[:, c, P0 + t0:P0 + t0 + n], start=(c == 0), stop=(c == DC - 1)))
                                        for c in range(DC)], reads=[Rw[s_], Rh], writes=[Rp[bb]])
                    cosv, sinv = cst[:, 0, t0:t0 + n], cst[:, 1, t0:t0 + n]
                    dst = qT if which == 0 else kT
                    rdst = RqT if which == 0 else RkT
                    kb.op("dve", lambda e, b1=b1, n=n, cosv=cosv: e.tensor_tensor(out=T[0][:, 0:n], in0=pbank[b1][:, 0:n], in1=cosv, op=ALU.mult), reads=[Rp[b1], Rc], writes=[RT[0]])
                    kb.op("dve", lambda e, b2=b2, n=n, sinv=sinv: e.tensor_tensor(out=T[1][:, 0:n], in0=pbank[b2][:, 0:n], in1=sinv, op=ALU.mult), reads=[Rp[b2], Rc], writes=[RT[1]])
                    kb.op("dve", lambda e, n=n: e.tensor_tensor(out=T[0][:, 0:n], in0=T[0][:, 0:n], in1=T[1][:, 0:n], op=ALU.subtract), reads=[RT[0], RT[1]], writes=[RT[0]])
                    kb.op("act", lambda e, n=n, t0=t0, dst=dst: e.activation(out=dst[:, 0, t0:t0 + n], in_=T[0][:, 0:n], func=AF.Copy), reads=[RT[0]], writes=[rdst])
                    if which == 0:
                        kb.op("dve", lambda e, n=n, t0=t0: e.tensor_tensor(out=qcT[:, 0, t0:t0 + n], in0=T[0][:, 0:n], in1=crq[:, t0:t0 + n], op=ALU.mult), reads=[RT[0], Rcrq], writes=[RqcT])
                    kb.op("dve", lambda e, b1=b1, n=n, sinv=sinv: e.tensor_tensor(out=T[1][:, 0:n], in0=pbank[b1][:, 0:n], in1=sinv, op=ALU.mult), reads=[Rp[b1], Rc], writes=[RT[1]])
                    kb.op("dve", lambda e, b2=b2, n=n, cosv=cosv: e.tensor_tensor(out=T[2][:, 0:n], in0=pbank[b2][:, 0:n], in1=cosv, op=ALU.mult), reads=[Rp[b2], Rc], writes=[RT[2]])
                    kb.op("dve", lambda e, n=n: e.tensor_tensor(out=T[1][:, 0:n], in0=T[1][:, 0:n], in1=T[2][:, 0:n], op=ALU.add), reads=[RT[1], RT[2]], writes=[RT[1]])
                    kb.op("act", lambda e, n=n, t0=t0, dst=dst: e.activation(out=dst[:, 1, t0:t0 + n], in_=T[1][:, 0:n], func=AF.Copy), reads=[RT[1]], writes=[rdst])
                    if which == 0:
                        kb.op("dve", lambda e, n=n, t0=t0: e.tensor_tensor(out=qcT[:, 1, t0:t0 + n], in0=T[1][:, 0:n], in1=crq[:, t0:t0 + n], op=ALU.mult), reads=[RT[1], Rcrq], writes=[RqcT])
            sv = [wd16(4096 + h * 512 + hv * 256) for hv in range(2)]
            for i_, (r0, n) in enumerate(ctiles(0, NTOK, 128)):
                b = nbank()
                for hv in range(2):
                    s_, wv_ = sv[hv]
                    kb.group("pe", [(lambda e, c=c, b=b, hv=hv, r0=r0, n=n, wv_=wv_: e.matmul(pbank[b][0:n, hv * 256:(hv + 1) * 256], hT[:, c, P0 + r0:P0 + r0 + n], wv_[:, c, :], start=(c == 0), stop=(c == DC - 1)))
                                    for c in range(DC)], reads=[Rw[s_], Rh], writes=[Rp[b]])
                kb.op("act", lambda e, b=b, i_=i_, n=n: e.activation(out=vtok[0:n, i_, :], in_=pbank[b][0:n, :], func=AF.Copy), reads=[Rp[b]], writes=[Rvt])
            for i_, (r0, n) in enumerate(ctiles(0, NTOK, 128)):
                b = nbank()
                kb.group("pe", [(lambda e, ch=ch, b=b, r0=r0, n=n: e.matmul(pbank[b][0:n, ch * 128:(ch + 1) * 128], kT[:, ch, r0:r0 + n], identb[:, :], start=True, stop=True)) for ch in range(2)],
                         reads=[RkT, Rc], writes=[Rp[b]])
                if i_ < 8:
                    kb.op("act", lambda e, b=b, i_=i_, h=h: e.activation(out=kdtok[:, i_, :], in_=pbank[b][:, 0:256], func=AF.Copy, scale=kdt[:, h:h + 1]), reads=[Rp[b], Rc], writes=[Rkd])
                else:
                    kb.op("act", lambda e, b=b: e.activation(out=ktok_s[0:NS, :], in_=pbank[b][0:NS, 0:256], func=AF.Copy), reads=[Rp[b]], writes=[Rkts])

            def chain_step(j, Sdst, Rdst):
                bs = [nbank(), nbank()]
                for ch in range(2):
                    kb.op("pe", lambda e, ch=ch, j=j, bs=bs: e.matmul(pbank[bs[ch]][:, :], kdtok[:, j, ch * 128:(ch + 1) * 128], vtok[:, j, :], start=True, stop=True), reads=[Rkd, Rvt], writes=[Rp[bs[ch]]])
                    kb.op("dve", lambda e, ch=ch, bs=bs, g128=g128: e.scalar_tensor_tensor(out=Sdst[:, ch, :], in0=Sdst[:, ch, :], scalar=g128, in1=pbank[bs[ch]][:, :], op0=ALU.mult, op1=ALU.add),
                          reads=[Rdst, Rp[bs[ch]]], writes=[Rdst])

            kb.op("dve", lambda e: e.memset(Sst[:, :, :], 0.0), writes=[RSst])
            for j in range(8):
                chain_step(j, Sst, RSst)
            Rcci, Rcco = res(f"ccin{h}"), res(f"ccout{h}")
            kb.dma("sp", cc_in[h].ap().rearrange("(c p) v -> p c v", p=128), Sst[:, :, :], reads=[RSst], writes=[Rcci])
            kb.custom("pool", lambda e, h=h: e.collective_compute("AllGather", ALU.bypass, replica_groups=PAIRS, ins=[cc_in[h].ap().opt()], outs=[cc_out[h].ap().opt()]),
                      ccsem, reads=[Rcci], writes=[Rcco])
            kb.scope(f"h{h}b_sample_gate")
            gate_items = []
            gslots = [wd16(8192 + h * 512 + hv * 256) for hv in range(2)]
            for hv in range(2):
                for o2 in range(2):
                    for (t0, n) in ctiles(0, NTOK):
                        def item(hv=hv, o2=o2, t0=t0, n=n):
                            s_, wv_ = gslots[hv]
                            v = 2 * hv + o2
                            b = nbank()
                            kb.group("pe", [(lambda e, c=c, b=b, o2=o2, t0=t0, n=n, wv_=wv_: e.matmul(pbank[b][:, 0:n], wv_[:, c, o2 * 128:(o2 + 1) * 128], hT[:, c, P0 + t0:P0 + t0 + n], start=(c == 0), stop=(c == DC - 1)))
                                            for c in range(DC)], reads=[Rw[s_], Rh], writes=[Rp[b]])
                            kb.op("act", lambda e, b=b, n=n, v=v, t0=t0: e.activation(out=sgT[:, v, t0:t0 + n], in_=pbank[b][:, 0:n], func=AF.Silu), reads=[Rp[b]], writes=[RsgT])
                        gate_items.append(item)
            b = nbank()
            kb.group("pe", [(lambda e, ch=ch, b=b: e.matmul(pbank[b][0:NS, 0:NS], kT[:, ch, NPR:NTOK], qT[:, ch, NPR:NTOK], start=(ch == 0), stop=(ch == 1))) for ch in range(2)],
                     reads=[RkT, RqT], writes=[Rp[b]])
            kb.op("dve", lambda e, b=b, h=h: e.tensor_tensor(out=scms[0:NS, :], in0=pbank[b][0:NS, 0:NS], in1=dmasks[:, h, :], op=ALU.mult), reads=[Rp[b], Rc], writes=[Rscms])
            bo = 6
            psb = pbank[bo][:, :].rearrange("p (a b) -> p a b", a=4)
            kb.group("pe", [(lambda e, v=v, psb=psb: e.matmul(psb[:, v, 0:NS], vtok[0:NS, 8, v * 128:(v + 1) * 128], scms[0:NS, :], start=(v == 0), stop=False)) for v in range(4)],
                     reads=[Rvt, Rscms], writes=[Rp[bo]])
            ss_base = cnt["ss"]
            cnt["ss"] += NSQ

            def sload(bq_):
                si_ = (ss_base + bq_) % 3
                kb.dma("sp", Ssm[si_][:, :, :], sret[bq_, h].rearrange("(c p) v -> p c v", p=128), writes=[RSsm[si_]])
            sload(0)
            sload(1)
            for bq in range(NSQ):
                si = (ss_base + bq) % 3
                sbi = cnt["sb"] % 2
                cnt["sb"] += 1
                if bq + 2 < NSQ:
                    sload(bq + 2)
                kb.op("act", lambda e, si=si, sbi=sbi: e.activation(out=Ssb[sbi][:, :, :], in_=Ssm[si][:, :, :], func=AF.Copy), reads=[RSsm[si]], writes=[RSsb[sbi]])
                fns = []
                for v in range(4):
                    for ch in range(2):
                        fns.append(lambda e, v=v, ch=ch, psb=psb, sbi=sbi, bq=bq: e.matmul(psb[:, v, 4 * bq:4 * bq + 4], Ssb[sbi][:, ch, v * 128:(v + 1) * 128], qcT[:, ch, NPR + 4 * bq:NPR + 4 * bq + 4],
                                                                                         start=False, stop=(ch == 1 and bq == NSQ - 1)))
                kb.group("pe", fns, reads=[RSsb[sbi], RqcT], writes=[Rp[bo]])
                ki = cnt["kd"] % 2
                cnt["kd"] += 1
                kb.op("act", lambda e, ki=ki, h=h, bq=bq: e.activation(out=kdsb[ki][0:NS, :], in_=ktok_s[0:NS, :], func=AF.Copy, scale=kds[:, h, bq:bq + 1]), reads=[Rkts, Rc], writes=[Rkdsb[ki]])
                bs = [nbank(), nbank()]
                for ch in range(2):
                    kb.op("pe", lambda e, ch=ch, bs=bs, ki=ki: e.matmul(pbank[bs[ch]][:, :], kdsb[ki][0:NS, ch * 128:(ch + 1) * 128], vtok[0:NS, 8, :], start=True, stop=True),
                          reads=[Rkdsb[ki], Rvt], writes=[Rp[bs[ch]]])
                    kb.op("dve", lambda e, ch=ch, bs=bs, si=si, g4=g4: e.scalar_tensor_tensor(out=Ssm[si][:, ch, :], in0=Ssm[si][:, ch, :], scalar=g4, in1=pbank[bs[ch]][:, :], op0=ALU.mult, op1=ALU.add),
                          reads=[RSsm[si], Rp[bs[ch]]], writes=[RSsm[si]])
                kb.dma("sp", rets_o[bq, h].rearrange("(c p) v -> p c v", p=128), Ssm[si][:, :, :], reads=[RSsm[si]], writes=[res("outdram")])
                if gate_items:
                    gate_items.pop(0)()
            while gate_items:
                gate_items.pop(0)()
            groupnorm(psb, bo, NS, NPR)
            kb.dma("sp", Sin[:, :, :], cc_out[h].ap()[0:DK, :].rearrange("(c p) v -> p c v", p=128), reads=[Rcco], writes=[RSin])
            kb.scope(f"h{h}c_rec")
            kb.op("dve", lambda e: e.tensor_scalar(out=Sst[:, :, :], in0=Sin[:, :, :], scalar1=flagt[:, 0:1], scalar2=None, op0=ALU.mult), reads=[RSin, Rc], writes=[RSst])
            for j in range(8):
                tj = 128 * j
                kb.op("act", lambda e: e.activation(out=Sbf[:, :, :], in_=Sst[:, :, :], func=AF.Copy), reads=[RSst], writes=[RSbf])
                b = nbank()
                kb.group("pe", [(lambda e, ch=ch, b=b, tj=tj: e.matmul(pbank[b][:, 0:128], kT[:, ch, tj:tj + 128], qT[:, ch, tj:tj + 128], start=(ch == 0), stop=(ch == 1))) for ch in range(2)],
                         reads=[RkT, RqT], writes=[Rp[b]])
                kb.op("dve", lambda e, b=b, h=h: e.tensor_tensor(out=scm[:, :], in0=pbank[b][:, 0:128], in1=dmask[:, h, :], op=ALU.mult), reads=[Rp[b], Rc], writes=[Rscm])
                bo = nbank()
                psb = pbank[bo][:, :].rearrange("p (a b) -> p a b", a=4)
                fns = []
                for v in range(4):
                    fns.append(lambda e, v=v, psb=psb, j=j: e.matmul(psb[:, v, :], vtok[:, j, v * 128:(v + 1) * 128], scm[:, :], start=True, stop=False))
                    for ch in range(2):
                        fns.append(lambda e, v=v, ch=ch, psb=psb, tj=tj: e.matmul(psb[:, v, :], Sbf[:, ch, v * 128:(v + 1) * 128], qcT[:, ch, tj:tj + 128], start=False, stop=(ch == 1)))
                kb.group("pe", fns, reads=[Rvt, Rscm, RSbf, RqcT], writes=[Rp[bo]])
                chain_step(j, Sst, RSst)
                groupnorm(psb, bo, 128, tj)
            kb.dma("sp", retp_o[h].rearrange("(c p) v -> p c v", p=128), Sst[:, :, :], reads=[RSst], writes=[res("outdram")])
            kb.op("dve", lambda e, ogd=OG["t"]: e.tensor_tensor(out=ogd[:, :, :], in0=ogd[:, :, :], in1=sgT[:, :, :], op=ALU.mult), reads=[OG["r"], RsgT], writes=[OG["r"]])
            kb.scope(f"h{h}d_wout")
            if h % 2 == 0:
                continue
            so = []
            for hh in (h - 1, h):
                for hv in range(2):
                    s_ = wload([(lambda sl: sl[:, :].rearrange("p (k d) -> p k d", k=2), w_out_v[:, 4 * hh + 2 * hv:4 * hh + 2 * hv + 2, :])])
                    so.append((s_, wsl[s_][:, :].rearrange("p (k d) -> p k d", k=2)))
            Ryp = res(f"ypart{h // 2}")
            for o in range(DC):
                for (t0, n) in ctiles(0, NTOK):
                    b = nbank()
                    kb.group("pe", [(lambda e, v=v, b=b, o=o, t0=t0, n=n, so=so: e.matmul(pbank[b][:, 0:n], so[v // 2][1][:, v % 2, o * 128:(o + 1) * 128], (ogT if v < 4 else ogB)[:, v % 4, t0:t0 + n], start=(v == 0), stop=(v == 7)))
                                    for v in range(8)], reads=[Rw[so[i][0]] for i in range(4)] + [Rog, RogB], writes=[Rp[b]])
                    yi = cnt["ys"] % 3
                    cnt["ys"] += 1
                    kb.op("act", lambda e, b=b, n=n, yi=yi: e.activation(out=yst[yi][:, 0:n], in_=pbank[b][:, 0:n], func=AF.Copy), reads=[Rp[b]], writes=[Ryst[yi]])
                    kb.dma("sp", ypart[h // 2][:, o * NTOK + t0:o * NTOK + t0 + n], yst[yi][:, 0:n], reads=[Ryst[yi]], writes=[Ryp])
        kb.barrier()
        kb.scope("restore")
        kb.dma("sp", xflat, xspill, reads=[Rxsp], writes=[Rx])
        yrow = [carve(26624 + 8704 * i, (128, 2, NTOK), F32) for i in range(2)]
        Ryrow = [res("yrow0"), res("yrow1")]
        it_ = 0
        for h in range(NH // 2):
            for o in range(0, DC, 2):
                yi = it_ % 2
                it_ += 1
                kb.dma("sp", yrow[yi][:, :, :], ypart[h][:, o * NTOK:(o + 2) * NTOK].rearrange("p (a b) -> p a b", a=2), reads=[res(f"ypart{h}")], writes=[Ryrow[yi]])
                kb.op("dve", lambda e, o=o, yi=yi: e.tensor_tensor(out=xT[:, o:o + 2, P0:NX], in0=xT[:, o:o + 2, P0:NX], in1=yrow[yi][:, :, :], op=ALU.add),
                      reads=[Rx, Ryrow[yi]], writes=[Rx])
        hx = carve(20480, (128, DC, 2), F32)
        Rhx, Rcxi, Rcxo = res("hx"), res("cxin"), res("cxout")
        cxi = nc.dram_tensor("cxi", [128, 2 * DC], F32)
        cxo = nc.dram_tensor("cxo", [256, 2 * DC], F32)
        kb.op("act", lambda e: e.activation(out=hx[:, :, :], in_=xT[:, :, S0 - 2:S0], func=AF.Copy), reads=[Rx], writes=[Rhx])
        kb.dma("sp", cxi.ap(), hx[:, :, :].rearrange("p a b -> p (a b)"), reads=[Rhx], writes=[Rcxi])
        kb.custom("pool", lambda e: e.collective_compute("AllGather", ALU.bypass, replica_groups=PAIRS, ins=[cxi.ap().opt()], outs=[cxo.ap().opt()]), ccsem, reads=[Rcxi], writes=[Rcxo])
        kb.dma("sp", hx[:, :, :].rearrange("p a b -> p (a b)"), cxo.ap()[0:128, :], reads=[Rcxo], writes=[Rhx])
        kb.op("dve", lambda e: e.tensor_scalar(out=xT[:, :, 15:17], in0=hx[:, :, :], scalar1=flagt[:, 0:1], scalar2=None, op0=ALU.mult), reads=[Rhx, Rc, Rx], writes=[Rx])
        kb.barrier()
        kb.scope("ffn1")
        conv_ffn(1)
        kb.scope("ple1")
        ple(1)

        kb.scope("final")
        ta2 = carve(16384, (128, NX), F32)
        yT = carve(20864, (128, DC, 128), F32)
        ostg_y = carve(29056, (128, D), F32)
        Rta2, RyT, Rosy = res("ta2"), res("yT"), res("ostgy")
        sqv2 = carve(0, (128, DC, 512), BF16)
        sumsq_rstd(ta2, Rta2, P0, NX, sqv2, Rsq5, 512)
        for (r0, n) in ctiles(0, NTOK, 128):
            for c in range(DC):
                kb.op("dve", lambda e, c=c, r0=r0, n=n: e.scalar_tensor_tensor(out=yT[:, c, 0:n], in0=xT[:, c, P0 + r0:P0 + r0 + n], scalar=gamT[:, 96 + c:97 + c], in1=ta2[:, P0 + r0:P0 + r0 + n],
                                                                              op0=ALU.mult, op1=ALU.mult), reads=[Rx, Rta2, Rc], writes=[RyT])
            tr_out(lambda c, n=n: yT[:, c, 0:n], RyT, n, DC, ostg_y, Rosy, y_o[r0:r0 + n, :])
        kb.wait_all("sp", [res("outdram")])
        kb.barrier()
        with nc.Block() as block:
            kb.replay(block)
    return nc


def _tables(half):
    log_g = np.log1p(-(2.0 ** (-5.0 - np.arange(NH, dtype=np.float64))))
    halfd = DK // 2
    inv = 10000.0 ** (-np.arange(halfd, dtype=np.float64) / halfd)
    pos = np.concatenate([half * NPR + np.arange(NPR), np.tile(16384 + np.arange(4), NSQ)]).astype(np.float64)
    ang = inv[:, None] * pos[None, :]
    cs = np.stack([np.cos(ang), np.sin(ang)], axis=1).astype(np.float32)
    n_in = np.concatenate([np.arange(NPR) % 128, np.tile(np.arange(4), NSQ)]).astype(np.float64)
    crossq = np.exp(log_g[:, None] * (n_in[None, :] + 1.0)).astype(np.float32)
    m = np.arange(128)
    dm = np.where(m[None, :, None] <= m[None, None, :], np.exp(log_g[:, None, None] * np.maximum(m[None, None, :] - m[None, :, None], 0)), 0.0) * DK ** -0.5
    dmask = np.ascontiguousarray(dm.transpose(1, 0, 2)).astype(np.float32)
    ms = np.arange(64)
    same = (ms[:, None] // 4) == (ms[None, :] // 4)
    dms = np.where(same[None] & (ms[None, :, None] <= ms[None, None, :]), np.exp(log_g[:, None, None] * np.maximum(ms[None, None, :] - ms[None, :, None], 0)), 0.0) * DK ** -0.5
    dmasks = np.ascontiguousarray(dms.transpose(1, 0, 2)).astype(np.float32)
    kdt = (np.exp(log_g[None, :] * (127.0 - m[:, None])) * DK ** -0.5).astype(np.float32)
    kds = np.zeros((64, NH, NSQ), np.float32)
    for t in range(64):
        kds[t, :, t // 4] = np.exp(log_g * (3.0 - (t % 4))) * DK ** -0.5
    invc = np.zeros((4, 15), np.float32)
    for gi, w in enumerate((2, 4, 8, 16)):
        p = np.arange(15)
        invc[gi] = 1.0 / np.minimum(p + 1, w) if half == 0 else 1.0 / w
    flagv = np.concatenate([[float(half)], float(half) * np.exp(log_g * 1024.0)]).astype(np.float32)[None, :]
    return dict(cs=cs, crossq=crossq, dmask=dmask, dmasks=dmasks, kdt=kdt, kds=kds, invc=invc.reshape(1, 60), flagv=flagv)


_NC_CACHE = {}


def kernel(x_prompt, x_sample, p_prompt, p_sample, state_pool, state_ret, state_conv,
           norm_mix, norm_ffn, norm_ple, norm_final, pool_w, pool_scale, ret_w_in, ret_w_out,
           ffn_w_up, ffn_conv_w, ffn_conv_b, ffn_w_down, ple_w_proj, ple_w_gate):
    f32 = lambda a: np.ascontiguousarray(np.asarray(a, dtype=np.float32))
    x_prompt, x_sample, p_prompt, p_sample = map(f32, (x_prompt, x_sample, p_prompt, p_sample))
    state_pool, state_ret, state_conv = map(f32, (state_pool, state_ret, state_conv))
    if "nc" not in _NC_CACHE:
        _NC_CACHE["nc"] = build_program()
    nc = _NC_CACHE["nc"]
    small = np.concatenate([f32(norm_mix).reshape(32, 128), f32(norm_ffn).reshape(32, 128), f32(norm_ple).reshape(32, 128),
                            f32(norm_final).reshape(16, 128), f32(pool_scale).reshape(16, 128)], axis=0)
    convp = np.zeros((768, 128), np.float32)
    convp[0:528] = f32(ffn_conv_w).reshape(528, 128)
    convp[528:704] = f32(ffn_conv_b).reshape(176, 128)
    shared = dict(small=small, convp=convp, identd=np.eye(128, dtype=np.float32), pool_w=f32(pool_w)[0],
                  ret_w_in=f32(ret_w_in)[0], ret_w_out=f32(ret_w_out)[0], ffn_w_up=f32(ffn_w_up), ffn_w_down=f32(ffn_w_down),
                  ple_w_proj=f32(ple_w_proj), ple_w_gate=f32(ple_w_gate))
    in_maps = []
    for core in range(8):
        s, half = core // 2, core % 2
        halo = x_prompt[s, NPR - HALO:NPR] if half else np.zeros((HALO, D), np.float32)
        sl = slice(NSQ * core, NSQ * (core + 1))
        xin = np.concatenate([halo, x_prompt[s, half * NPR:(half + 1) * NPR], x_sample[sl].reshape(NS, D)], axis=0)
        pin = np.concatenate([p_prompt[:, s, half * NPR:(half + 1) * NPR], p_sample[:, sl].reshape(2, NS, 256)], axis=1)
        m = dict(shared)
        m.update(xin=np.ascontiguousarray(xin), pin=np.ascontiguousarray(pin), spool=state_pool[0, sl].reshape(240, D),
                 sret=state_ret[0, sl], sconv=np.ascontiguousarray(state_conv[:, sl].reshape(2, 32, F2)))
        m.update(_tables(half))
        in_maps.append(m)
    res = run_bass_kernel_spmd(nc, in_maps, core_ids=list(range(8)), **({"trace": True} if PROFILE else {}))
    r = res.results
    if PROFILE:
        _NC_CACHE["prof"] = res
    if DEBUG:
        _NC_CACHE["dbg"] = r
    y_prompt = np.stack([np.concatenate([r[2 * s]["y"][0:NPR], r[2 * s + 1]["y"][0:NPR]], axis=0) for s in range(4)])
    y_sample = np.concatenate([r[c]["y"][NPR:].reshape(NSQ, 4, D) for c in range(8)], axis=0)
    pool_p = np.stack([r[2 * s + 1]["pool_p"] for s in range(4)])[None]
    pool_s = np.concatenate([r[c]["pool_s"].reshape(NSQ, 15, D) for c in range(8)], axis=0)[None]
    ret_p = np.stack([r[2 * s + 1]["ret_p"] for s in range(4)])[None]
    ret_s = np.concatenate([r[c]["ret_s"] for c in range(8)], axis=0)[None]
    conv_p = np.stack([r[2 * s + 1]["conv_p"] for s in range(4)], axis=1)
    conv_s = np.concatenate([r[c]["conv_s"].reshape(2, NSQ, 2, F2) for c in range(8)], axis=1)
    return (y_prompt, y_sample, pool_p, pool_s, ret_p, ret_s, conv_p, conv_s)
```
